# Optimizing a Trainium2 kernel written in Bass

```python
import math
import numpy as np
import jax
import jax.numpy as jnp
from jax import lax

D_MODEL = 4096
BATCH = 2
SEQ = 4096
DEPTH = 2

CTX_LEN = 256
GRID_W = 64
ROPE_THETA = 10000.0
EPS = 1e-6
Q_BLOCK = 128
F32 = jnp.float32

N_EVEN = (DEPTH + 1) // 2
N_ODD = DEPTH // 2
MIX_WIDTH = D_MODEL

A_HEAD_DIM = 128
A_WIDTH = MIX_WIDTH // 2
A_HEADS = A_WIDTH // A_HEAD_DIM
GLA_CHUNK = 64
B_VDIM = 128
B_HEADS = (MIX_WIDTH - A_WIDTH) // B_VDIM
B_NOPE = 128
B_ROPE = 64
B_Q_RANK = 768
B_KV_RANK = 512
EVEN_SPLITS = (A_WIDTH, A_WIDTH, A_WIDTH, A_WIDTH, A_WIDTH, B_Q_RANK, B_KV_RANK, B_ROPE)
EVEN_IN = 5 * A_WIDTH + B_Q_RANK + B_KV_RANK + B_ROPE
C_HEAD_DIM = 128
C_VDIM = 2 * C_HEAD_DIM
C_WIDTH = MIX_WIDTH // 2
C_HEADS = C_WIDTH // C_VDIM
D_WIDTH = MIX_WIDTH - C_WIDTH
D_BLOCKS = 16
D_BLOCK_DIM = D_WIDTH // D_BLOCKS
D_CONV = 4
RG_C = 8.0
ODD_SPLITS = (C_WIDTH, C_WIDTH, C_WIDTH, D_WIDTH, D_WIDTH)
ODD_IN = 3 * C_WIDTH + 2 * D_WIDTH
P_HEADS = 8
P_NKEYS = 128
P_EXPERTS = P_NKEYS * P_NKEYS
P_QDIM = 256
P_HALF = P_QDIM // 2
P_TOPK = 16
P_TOKEN_BLOCK = 64

kernel_name = 'hybrid_hgrn2_mla_diffattn_rglru_peer_dit'


def rms_norm(x, g):
    xf = x.astype(F32)
    y = xf * lax.rsqrt(jnp.mean(xf * xf, axis=-1, keepdims=True) + EPS)
    return (y * g.astype(F32)).astype(x.dtype)


def modulate(h, shift, scale):
    return h * (1.0 + scale) + shift


def split_cols(p, sizes):
    return jnp.split(p, np.cumsum(sizes)[:-1].tolist(), axis=-1)


def rope_tables(row, col, dim):
    n_f = dim // 4
    inv = ROPE_THETA ** (-jnp.arange(n_f, dtype=F32) / n_f)
    ang = jnp.concatenate([row[:, None].astype(F32) * inv, col[:, None].astype(F32) * inv], axis=-1)
    return jnp.cos(ang), jnp.sin(ang)


def apply_rope(x, cos, sin):
    half = x.shape[-1] // 2
    xf = x.astype(F32)
    x1, x2 = xf[..., :half], xf[..., half:]
    return jnp.concatenate([x1 * cos - x2 * sin, x2 * cos + x1 * sin], axis=-1).astype(x.dtype)


def attend(q, k, v, map_w, scale):
    B, H, M, Tq, dk = q.shape
    nb = Tq // Q_BLOCK
    kf, vf, wf = k.astype(F32), v.astype(F32), map_w.astype(F32)
    qb = jnp.moveaxis(q.astype(F32).reshape(B, H, M, nb, Q_BLOCK, dk), 3, 0)

    def block(qblk):
        s = jnp.einsum('bhmqd,bhmkd->bhmqk', qblk, kf) * scale
        p = jnp.einsum('bhmqk,m->bhqk', jax.nn.softmax(s, axis=-1), wf)
        return jnp.einsum('bhqk,bhkv->bhqv', p, vf)

    o = lax.map(block, qb)
    return jnp.moveaxis(o, 0, 2).reshape(B, H, Tq, v.shape[-1]).astype(v.dtype)


def gla_chunkwise(q, k, v, g, s0):
    B, H, T, K = q.shape
    V = v.shape[-1]
    n = T // GLA_CHUNK
    r = lambda t: t.reshape(B, H, n, GLA_CHUNK, t.shape[-1])
    q, k, v, g = r(q), r(k), r(v), r(g)
    b = jnp.cumsum(g, axis=3)
    b_mid = b[:, :, :, GLA_CHUNK // 2 - 1:GLA_CHUNK // 2]
    b_last = b[:, :, :, -1:]
    att = jnp.einsum('bhnck,bhnsk->bhncs', q * jnp.exp(b - b_mid), k * jnp.exp(b_mid - b))
    upto_t = jnp.tril(jnp.ones((GLA_CHUNK, GLA_CHUNK), dtype=bool))
    att = jnp.where(upto_t, att, 0.0)
    o = jnp.einsum('bhncs,bhnsv->bhncv', att, v)
    kv = jnp.einsum('bhnck,bhncv->bhnkv', k * jnp.exp(b_last - b), v)
    decay = jnp.exp(b_last[:, :, :, 0])

    def step(S, inp):
        dec, kv_c = inp
        return dec[..., None] * S + kv_c, S

    s_fin, s_start = lax.scan(step, s0, (jnp.moveaxis(decay, 2, 0), jnp.moveaxis(kv, 2, 0)))
    o = o + jnp.einsum('bhnck,nbhkv->bhncv', q * jnp.exp(b), s_start)
    return o.reshape(B, H, T, V), s_fin


def hgrn2_bidir(parts_c, parts_l, lb, onorm_g, need_ctx):
    def heads(t):
        B, T, _ = t.shape
        return t.reshape(B, T, A_HEADS, A_HEAD_DIM).transpose(0, 2, 1, 3)

    def prep(parts):
        q, f_f, f_b, i, _ = parts
        qh = heads(jax.nn.silu(q.astype(F32)) * A_HEAD_DIM ** -0.5)
        vh = heads(i.astype(F32))
        kg = []
        for d, fz in enumerate((f_f, f_b)):
            f = lb[d] + (1.0 - lb[d]) * jax.nn.sigmoid(fz.astype(F32))
            kg.append((heads(1.0 - f), heads(jnp.log(f))))
        return qh, vh, kg

    qc, vc, kgc = prep(parts_c)
    ql, vl, kgl = prep(parts_l)
    s0 = jnp.zeros((ql.shape[0], A_HEADS, A_HEAD_DIM, A_HEAD_DIM), F32)
    o_c, o_l = 0.0, 0.0
    for d in range(2):
        flip = (lambda t: jnp.flip(t, axis=2)) if d == 1 else (lambda t: t)
        oc_d, s_ctx = gla_chunkwise(flip(qc), flip(kgc[d][0]), flip(vc), flip(kgc[d][1]), s0)
        ol_d, _ = gla_chunkwise(flip(ql), flip(kgl[d][0]), flip(vl), flip(kgl[d][1]), s_ctx)
        o_l = o_l + flip(ol_d)
        if need_ctx:
            o_c = o_c + flip(oc_d)

    def readout(o, gate):
        B, H, T, V = o.shape
        o = rms_norm(o.transpose(0, 2, 1, 3), onorm_g).reshape(B, T, A_WIDTH)
        return o * jax.nn.silu(gate.astype(F32))

    out_l = readout(o_l, parts_l[4])
    out_c = readout(o_c, parts_c[4]) if need_ctx else None
    return out_c, out_l


def mla(parts_c, parts_l, rope_cs, cq_g, ckv_g, w_uq, w_ukv, qn_g, kn_g, qr_g, kr_g, need_ctx):
    cos, sin = rope_cs

    def project(parts, rotate):
        cq, ckv, kr = parts
        B, T, _ = cq.shape
        q = (rms_norm(cq, cq_g) @ w_uq).reshape(B, T, B_HEADS, B_NOPE + B_ROPE)
        kv = (rms_norm(ckv, ckv_g) @ w_ukv).reshape(B, T, B_HEADS, B_NOPE + B_VDIM)
        q_nope = rms_norm(q[..., :B_NOPE], qn_g)
        q_rope = rms_norm(q[..., B_NOPE:], qr_g)
        k_nope = rms_norm(kv[..., :B_NOPE], kn_g)
        v = kv[..., B_NOPE:]
        k_rope = rms_norm(kr, kr_g)
        if rotate:
            q_rope = apply_rope(q_rope, cos[:, None, :], sin[:, None, :])
            k_rope = apply_rope(k_rope, cos, sin)
        k_rope = jnp.broadcast_to(k_rope[:, :, None, :], (B, T, B_HEADS, B_ROPE))
        q = jnp.concatenate([q_nope, q_rope], -1).transpose(0, 2, 1, 3)[:, :, None]
        k = jnp.concatenate([k_nope, k_rope], -1).transpose(0, 2, 1, 3)[:, :, None]
        return q, k, v.transpose(0, 2, 1, 3)

    qc, kc, vc = project(parts_c, False)
    ql, kl, vl = project(parts_l, True)
    w1 = jnp.ones((1,), F32)
    scale = (B_NOPE + B_ROPE) ** -0.5

    def merge(o):
        B, H, T, V = o.shape
        return o.transpose(0, 2, 1, 3).reshape(B, T, H * V)

    out_l = merge(attend(ql, jnp.concatenate([kc, kl], axis=3), jnp.concatenate([vc, vl], axis=2), w1, scale))
    out_c = merge(attend(qc, kc, vc, w1, scale)) if need_ctx else None
    return out_c, out_l


def diff_attn(parts_c, parts_l, rope_cs, qn_g, kn_g, lam_p, onorm_g, lam_init, need_ctx):
    cos, sin = rope_cs

    def project(parts, rotate):
        q, k, v = parts
        B, T, _ = q.shape
        q = rms_norm(q.reshape(B, T, C_HEADS, 2, C_HEAD_DIM), qn_g)
        k = rms_norm(k.reshape(B, T, C_HEADS, 2, C_HEAD_DIM), kn_g)
        if rotate:
            cs, sn = cos[:, None, None, :], sin[:, None, None, :]
            q, k = apply_rope(q, cs, sn), apply_rope(k, cs, sn)
        v = v.reshape(B, T, C_HEADS, C_VDIM).transpose(0, 2, 1, 3)
        return q.transpose(0, 2, 3, 1, 4), k.transpose(0, 2, 3, 1, 4), v

    lp = lam_p.astype(F32)
    lam = jnp.exp(jnp.sum(lp[0] * lp[1])) - jnp.exp(jnp.sum(lp[2] * lp[3])) + lam_init
    map_w = jnp.stack([jnp.ones((), F32), -lam])
    scale = C_HEAD_DIM ** -0.5
    qc, kc, vc = project(parts_c, False)
    ql, kl, vl = project(parts_l, True)

    def readout(o):
        B, H, T, V = o.shape
        return (rms_norm(o.transpose(0, 2, 1, 3), onorm_g) * (1.0 - lam_init)).reshape(B, T, C_WIDTH)

    out_l = readout(attend(ql, jnp.concatenate([kc, kl], axis=3), jnp.concatenate([vc, vl], axis=2), map_w, scale))
    out_c = readout(attend(qc, kc, vc, map_w, scale)) if need_ctx else None
    return out_c, out_l


def dwconv(x, w, b):
    y = lax.conv_general_dilated(x, w[:, None, :].astype(x.dtype), window_strides=(1,),
                                 padding=[(D_CONV // 2, D_CONV - 1 - D_CONV // 2)],
                                 dimension_numbers=('NWC', 'WIO', 'NWC'), feature_group_count=x.shape[-1])
    return y + b.astype(x.dtype)


def rg_lru_coeffs(x, w, b, lam):
    B, T, _ = x.shape
    xf = x.astype(F32)
    z = jnp.einsum('btnc,gncd->gbtnd', xf.reshape(B, T, D_BLOCKS, D_BLOCK_DIM), w.astype(F32))
    gates = jax.nn.sigmoid(z.reshape(2, B, T, D_WIDTH) + b[:, None, None, :].astype(F32))
    log_a = -RG_C * gates[0] * jax.nn.softplus(-lam.astype(F32))
    return jnp.exp(log_a), jnp.sqrt(-jnp.expm1(2.0 * log_a)) * (gates[1] * xf)


def linear_scan(a, u, h0):
    u = u.at[:, 0].add(a[:, 0] * h0)
    comb = lambda l, r: (l[0] * r[0], r[0] * l[1] + r[1])
    return lax.associative_scan(comb, (a, u), axis=1)[1]


def rglru_bidir(gate_c, x_c, gate_l, x_l, conv_w, conv_b, w_gate, b_gate, lam, need_ctx):
    xc = dwconv(x_c, conv_w, conv_b)
    xl = dwconv(x_l, conv_w, conv_b)
    h0 = jnp.zeros((xl.shape[0], D_WIDTH), F32)
    h_c, h_l = 0.0, 0.0
    for d in range(2):
        flip = (lambda t: jnp.flip(t, axis=1)) if d == 1 else (lambda t: t)
        a_c, u_c = rg_lru_coeffs(flip(xc), w_gate[d], b_gate[d], lam[d])
        hc = linear_scan(a_c, u_c, h0)
        a_l, u_l = rg_lru_coeffs(flip(xl), w_gate[d], b_gate[d], lam[d])
        hl = linear_scan(a_l, u_l, hc[:, -1])
        h_l = h_l + flip(hl)
        if need_ctx:
            h_c = h_c + flip(hc)
    out_l = h_l * jax.nn.gelu(gate_l.astype(F32))
    out_c = h_c * jax.nn.gelu(gate_c.astype(F32)) if need_ctx else None
    return out_c, out_l


def peer(h, w_q, subkeys, u_tab, v_tab):
    N, D = h.shape
    q = (h @ w_q).reshape(N, P_HEADS, 2, P_HALF)
    s = jnp.einsum('nhpd,hpkd->nhpk', q, subkeys).astype(F32)
    s1, i1 = lax.top_k(s[:, :, 0], P_TOPK)
    s2, i2 = lax.top_k(s[:, :, 1], P_TOPK)
    cand_s = (s1[..., :, None] + s2[..., None, :]).reshape(N, P_HEADS, P_TOPK * P_TOPK)
    cand_i = (i1[..., :, None] * P_NKEYS + i2[..., None, :]).reshape(N, P_HEADS, P_TOPK * P_TOPK)
    top_s, pos = lax.top_k(cand_s, P_TOPK)
    idx = jnp.take_along_axis(cand_i, pos, axis=-1)
    gate = jax.nn.softmax(top_s, axis=-1)
    nb = N // P_TOKEN_BLOCK

    def block(args):
        hb, ib, gb = args
        u = jnp.take(u_tab, ib, axis=0)
        act = jax.nn.gelu(jnp.einsum('td,thkd->thk', hb, u).astype(F32), approximate=False) * gb
        return jnp.einsum('thk,thkd->td', act, jnp.take(v_tab, ib, axis=0).astype(F32))

    out = lax.map(block, (h.reshape(nb, P_TOKEN_BLOCK, D),
                          idx.reshape(nb, P_TOKEN_BLOCK, P_HEADS, P_TOPK),
                          gate.reshape(nb, P_TOKEN_BLOCK, P_HEADS, P_TOPK)))
    return out.reshape(N, D)


def setup_inputs(seed: int = 0) -> dict:
    key = jax.random.key(seed)
    ks = iter(jax.random.split(key, 48))
    nrm = lambda shape, s: jax.random.normal(next(ks), shape, F32) * s
    gain = lambda shape: 1.0 + 0.02 * jax.random.normal(next(ks), shape, F32)
    D = D_MODEL
    lam_u = jax.random.uniform(next(ks), (N_ODD, 2, D_WIDTH), F32, 0.9, 0.999)
    lam_s = lam_u ** (1.0 / RG_C)
    return {
        'x': nrm((BATCH, SEQ, D), 1.0),
        'c': nrm((BATCH, D), 1.0),
        'ctx': nrm((BATCH, CTX_LEN, D), 1.0),
        'c_ctx': nrm((D,), 1.0),
        'norm1_g': gain((DEPTH, D)),
        'norm2_g': gain((DEPTH, D)),
        'w_mod': nrm((DEPTH, D, 6 * D), 0.5 * D ** -0.5),
        'b_mod': nrm((DEPTH, 6 * D), 0.02),
        'e_w_in': nrm((N_EVEN, D, EVEN_IN), D ** -0.5),
        'e_w_out': nrm((N_EVEN, MIX_WIDTH, D), MIX_WIDTH ** -0.5),
        'a_lb_logits': nrm((2, N_EVEN + 1, A_WIDTH), 0.1),
        'a_onorm_g': gain((N_EVEN, A_HEAD_DIM)),
        'b_cq_g': gain((N_EVEN, B_Q_RANK)),
        'b_ckv_g': gain((N_EVEN, B_KV_RANK)),
        'b_w_uq': nrm((N_EVEN, B_Q_RANK, B_HEADS * (B_NOPE + B_ROPE)), B_Q_RANK ** -0.5),
        'b_w_ukv': nrm((N_EVEN, B_KV_RANK, B_HEADS * (B_NOPE + B_VDIM)), B_KV_RANK ** -0.5),
        'b_qn_g': gain((N_EVEN, B_NOPE)),
        'b_kn_g': gain((N_EVEN, B_NOPE)),
        'b_qr_g': gain((N_EVEN, B_ROPE)),
        'b_kr_g': gain((N_EVEN, B_ROPE)),
        'o_w_in': nrm((N_ODD, D, ODD_IN), D ** -0.5),
        'o_w_out': nrm((N_ODD, MIX_WIDTH, D), MIX_WIDTH ** -0.5),
        'c_qn_g': gain((N_ODD, C_HEAD_DIM)),
        'c_kn_g': gain((N_ODD, C_HEAD_DIM)),
        'c_lam': nrm((N_ODD, 4, C_HEAD_DIM), 0.1),
        'c_onorm_g': gain((N_ODD, C_VDIM)),
        'd_conv_w': nrm((N_ODD, D_CONV, D_WIDTH), D_CONV ** -0.5),
        'd_conv_b': nrm((N_ODD, D_WIDTH), 0.01),
        'd_w_gate': nrm((N_ODD, 2, 2, D_BLOCKS, D_BLOCK_DIM, D_BLOCK_DIM), D_BLOCK_DIM ** -0.5),
        'd_b_gate': nrm((N_ODD, 2, 2, D_WIDTH), 0.01),
        'd_lambda': jnp.log(lam_s) - jnp.log1p(-lam_s),
        'p_w_q': nrm((DEPTH, D, P_HEADS * P_QDIM), D ** -0.5),
        'p_subkeys': nrm((DEPTH, P_HEADS, 2, P_NKEYS, P_HALF), P_HALF ** -0.5),
        'p_u': nrm((DEPTH, P_EXPERTS, D), D ** -0.5),
        'p_v': nrm((DEPTH, P_EXPERTS, D), 0.5),
    }


def reference(x, c, ctx, c_ctx, norm1_g, norm2_g, w_mod, b_mod, e_w_in, e_w_out, a_lb_logits, a_onorm_g,
              b_cq_g, b_ckv_g, b_w_uq, b_w_ukv, b_qn_g, b_kn_g, b_qr_g, b_kr_g, o_w_in, o_w_out,
              c_qn_g, c_kn_g, c_lam, c_onorm_g, d_conv_w, d_conv_b, d_w_gate, d_b_gate, d_lambda,
              p_w_q, p_subkeys, p_u, p_v):
    B, T, D = x.shape
    Tc = ctx.shape[1]
    rows = T // GRID_W
    row = jnp.repeat(jnp.arange(rows), GRID_W)
    col = jnp.tile(jnp.arange(GRID_W), rows)
    rope_b = rope_tables(row, col, B_ROPE)
    rope_c = rope_tables(row, col, C_HEAD_DIM)
    lbs = jnp.cumsum(jax.nn.softmax(a_lb_logits.astype(F32), axis=1), axis=1)
    cond_l = jax.nn.silu(c)
    cond_c = jax.nn.silu(c_ctx)
    for l in range(DEPTH):
        j = l // 2
        need_ctx = l < DEPTH - 1
        mod_l = jnp.split((cond_l @ w_mod[l] + b_mod[l])[:, None, :], 6, axis=-1)
        mod_c = jnp.split((cond_c @ w_mod[l] + b_mod[l])[None, None, :], 6, axis=-1)
        hl = modulate(rms_norm(x, norm1_g[l]), mod_l[0], mod_l[1])
        hc = modulate(rms_norm(ctx, norm1_g[l]), mod_c[0], mod_c[1])
        if l % 2 == 0:
            pl = split_cols(hl @ e_w_in[j], EVEN_SPLITS)
            pc = split_cols(hc @ e_w_in[j], EVEN_SPLITS)
            g1_c, g1_l = hgrn2_bidir(pc[:5], pl[:5], lbs[:, j], a_onorm_g[j], need_ctx)
            g2_c, g2_l = mla(pc[5:], pl[5:], rope_b, b_cq_g[j], b_ckv_g[j], b_w_uq[j], b_w_ukv[j],
                             b_qn_g[j], b_kn_g[j], b_qr_g[j], b_kr_g[j], need_ctx)
            w_out = e_w_out[j]
        else:
            pl = split_cols(hl @ o_w_in[j], ODD_SPLITS)
            pc = split_cols(hc @ o_w_in[j], ODD_SPLITS)
            lam_init = 0.8 - 0.6 * math.exp(-0.3 * l)
            g1_c, g1_l = diff_attn(pc[:3], pl[:3], rope_c, c_qn_g[j], c_kn_g[j], c_lam[j], c_onorm_g[j],
                                   lam_init, need_ctx)
            g2_c, g2_l = rglru_bidir(pc[3], pc[4], pl[3], pl[4], d_conv_w[j], d_conv_b[j], d_w_gate[j],
                                     d_b_gate[j], d_lambda[j], need_ctx)
            w_out = o_w_out[j]
        x = x + (mod_l[2] * (jnp.concatenate([g1_l, g2_l], axis=-1) @ w_out)).astype(x.dtype)
        if need_ctx:
            ctx = ctx + (mod_c[2] * (jnp.concatenate([g1_c, g2_c], axis=-1) @ w_out)).astype(ctx.dtype)
        hl = modulate(rms_norm(x, norm2_g[l]), mod_l[3], mod_l[4]).reshape(B * T, D)
        if need_ctx:
            hc = modulate(rms_norm(ctx, norm2_g[l]), mod_c[3], mod_c[4]).reshape(B * Tc, D)
            y = peer(jnp.concatenate([hc, hl], axis=0), p_w_q[l], p_subkeys[l], p_u[l], p_v[l])
            ctx = ctx + (mod_c[5] * y[:B * Tc].reshape(B, Tc, D)).astype(ctx.dtype)
            y = y[B * Tc:]
        else:
            y = peer(hl, p_w_q[l], p_subkeys[l], p_u[l], p_v[l])
        x = x + (mod_l[5] * y.reshape(B, T, D)).astype(x.dtype)
    return x
```

```python
import math
from contextlib import ExitStack
import numpy as np
import concourse.bass as bass
import concourse.mybir as mybir
from concourse.bass_utils import run_bass_kernel_spmd

F32 = mybir.dt.float32
BF16 = mybir.dt.bfloat16
U32 = mybir.dt.uint32
I32 = mybir.dt.int32
AF = mybir.ActivationFunctionType
ALU = mybir.AluOpType
AX = mybir.AxisListType

ENGS = ('pe', 'dve', 'act', 'pool', 'sp')
D = 4096
TC = 256
TL = 4096
TT = TC + TL
EPS = 1e-6
NCH = TT // 64


class Prog:
    def __init__(self, nc, stack, n_dma_sems=(('sp', 16), ('pool', 8), ('act', 4))):
        self.nc = nc
        self.streams = {e: [] for e in ENGS}
        self.sems = {}
        self.cnt = {}
        for e in ENGS:
            self.sems[('eng', e)] = stack.enter_context(nc.semaphore('s_' + e))
            self.cnt[('eng', e)] = 0
        self.dma_pool = {}
        self.dma_next = {}
        for e, n in n_dma_sems:
            ids = []
            for i in range(n):
                k = ('dma', e, i)
                self.sems[k] = stack.enter_context(nc.semaphore('d_%s%d' % (e, i)))
                self.cnt[k] = 0
                ids.append(k)
            self.dma_pool[e] = ids
            self.dma_next[e] = 0
        self.known = {e: {} for e in ENGS}
        self.last_w = {}
        self.readers = {}
        self.nops = 0

    def _need(self, eng, ev, waits):
        if ev is None:
            return
        k, v = ev
        if k == ('eng', eng) and eng == 'pe':
            return
        if self.known[eng].get(k, 0) >= v:
            return
        if waits.get(k, 0) < v:
            waits[k] = v

    def _deps(self, eng, reads, writes):
        waits = {}
        for r in reads:
            self._need(eng, self.last_w.get(r), waits)
        for w in writes:
            self._need(eng, self.last_w.get(w), waits)
            for ev in self.readers.get(w, ()):
                self._need(eng, ev, waits)
        for k, v in waits.items():
            self.known[eng][k] = v
        return [(self.sems[k], v) for k, v in waits.items()]

    def _record(self, ev, reads, writes):
        for r in reads:
            self.readers.setdefault(r, []).append(ev)
        for w in writes:
            self.last_w[w] = ev
            self.readers[w] = []

    def op(self, eng, fn, reads=(), writes=(), sig=True):
        reads = tuple(reads)
        writes = tuple(writes)
        assert sig or eng == 'pe'
        waits = self._deps(eng, reads, writes)
        k = ('eng', eng)
        if sig:
            self.cnt[k] += 1
            ev = (k, self.cnt[k])
        else:
            ev = (k, self.cnt[k] + 1)
        self._record(ev, reads, writes)
        self.streams[eng].append((waits, fn, self.sems[k] if sig else None, 1))
        self.nops += 1

    def dma(self, eng, out, in_, reads=(), writes=(), **kw):
        reads = tuple(reads)
        writes = tuple(writes)
        pool = self.dma_pool[eng]
        k = pool[self.dma_next[eng] % len(pool)]
        self.dma_next[eng] += 1
        waits = self._deps(eng, reads, writes)
        prev = self.cnt[k]
        if prev and self.known[eng].get(k, 0) < prev:
            self.known[eng][k] = prev
            waits.append((self.sems[k], prev))
        self.cnt[k] += 16
        ev = (k, self.cnt[k])
        self._record(ev, reads, writes)
        self.streams[eng].append((waits, (lambda e: e.dma_start(out=out, in_=in_, **kw)), self.sems[k], 16))
        self.nops += 1

    def barrier(self, full=False):
        def bg(k):
            return (not full) and k[0] == 'dma' and k[1] == 'pool'
        for e in ENGS:
            waits = {}
            for k, v in self.cnt.items():
                if v and not bg(k):
                    self._need(e, (k, v), waits)
            for k, v in waits.items():
                self.known[e][k] = v
            self.streams[e].append(([(self.sems[k], v) for k, v in waits.items()], None, None, 0))
        self.last_w = {r: ev for r, ev in self.last_w.items() if bg(ev[0])}
        self.readers = {}

    def emit(self, block):
        def run(engobj, items):
            for waits, fn, sem, inc in items:
                for s, v in waits:
                    engobj.wait_ge(s, v)
                if fn is None:
                    continue
                ins = fn(engobj)
                if sem is not None:
                    ins.then_inc(sem, inc)

        st = self.streams

        @block.tensor
        def _(e):
            run(e, st['pe'])

        @block.vector
        def _(e):
            run(e, st['dve'])

        @block.scalar
        def _(e):
            run(e, st['act'])

        @block.gpsimd
        def _(e):
            run(e, st['pool'])

        @block.sync
        def _(e):
            run(e, st['sp'])


class K:
    def __init__(self, nc, stack):
        self.nc = nc
        self.P = Prog(nc, stack)
        self.stack = stack
        self.uid = 0

    def sb(self, st, name, shape, dt):
        self.uid += 1
        return st.enter_context(self.nc.sbuf_tensor('%s_u%d' % (name, self.uid), list(shape), dt))

    def ps(self, st, name, shape, dt=F32):
        self.uid += 1
        return st.enter_context(self.nc.psum_tensor('%s_u%d' % (name, self.uid), list(shape), dt))

    def mm(self, out, lhsT, rhs, start, stop, r=(), w=(), sig=None):
        if sig is None:
            sig = stop
        self.P.op('pe', lambda e: e.matmul(out, lhsT=lhsT, rhs=rhs, start=start, stop=stop), r, w, sig=sig)

    def tr(self, out, in_, ident, r=(), w=(), sig=True):
        self.P.op('pe', lambda e: e.transpose(out, in_, ident), r, w, sig=sig)

    def act(self, out, in_, func, r=(), w=(), scale=1.0, bias=0.0, accum_out=None, eng='act'):
        kw = {}
        if accum_out is not None:
            kw['accum_out'] = accum_out
        self.P.op('act', lambda e: e.activation(out=out, in_=in_, func=func, scale=scale, bias=bias, **kw), r, w)

    def ts(self, out, in0, s1, s2, op0, op1=None, r=(), w=(), eng='dve'):
        if op1 is None:
            self.P.op(eng, lambda e: e.tensor_scalar(out, in0, s1, None, op0), r, w)
        else:
            self.P.op(eng, lambda e: e.tensor_scalar(out, in0, s1, s2, op0, op1), r, w)

    def tt(self, out, in0, in1, op, r=(), w=(), eng='dve'):
        self.P.op(eng, lambda e: e.tensor_tensor(out, in0, in1, op), r, w)

    def stt(self, out, in0, scalar, in1, op0, op1, r=(), w=()):
        self.P.op('dve', lambda e: e.scalar_tensor_tensor(out, in0, scalar, in1, op0, op1), r, w)

    def cp(self, out, in_, r=(), w=(), eng='dve'):
        if eng == 'act':
            self.P.op('act', lambda e: e.copy(out, in_), r, w)
        else:
            self.P.op(eng, lambda e: e.tensor_copy(out, in_), r, w)

    def memset(self, ap, val, w=(), eng='dve'):
        self.P.op(eng, lambda e: e.memset(ap, val), (), w)

    def dma(self, out, in_, r=(), w=(), eng='sp'):
        self.P.dma(eng, out, in_, r, w)

    def consts(self, st):
        nc = self.nc
        self.iota_i = self.sb(st, 'iota_i', [128, 128], I32)
        self.ident = self.sb(st, 'ident', [128, 128], F32)
        self.identb = self.sb(st, 'identb', [128, 128], BF16)
        self.onesb = self.sb(st, 'onesb', [128, 128], BF16)
        self.epsc = self.sb(st, 'epsc', [128, 1], F32)
        self.P.op('pool', lambda e: e.iota(self.iota_i[:], pattern=[[1, 128]], base=0, channel_multiplier=-1),
                  (), ['iota_i'])
        self.P.op('dve', lambda e: e.tensor_single_scalar(self.ident[:], self.iota_i[:], 0, ALU.is_equal),
                  ['iota_i'], ['ident'])
        self.cp(self.identb[:], self.ident[:], ['ident'], ['identb'])
        self.memset(self.onesb[:], 1.0, ['onesb'])
        self.memset(self.epsc[:], EPS, ['epsc'])
        self.negshift = self.sb(st, 'negshift', [128, 1], F32)
        self.memset(self.negshift[:], -8.0, ['negshift'])

    def rstd_from_ssq(self, out, ssq, n, r=(), w=()):
        np_ = out.shape[0]
        self.act(out, ssq, AF.Ln, r, w, scale=1.0 / n, bias=self.epsc[0:np_, :])
        self.act(out, out, AF.Exp, w, w, scale=-0.5)


def emit_mod(k, st, cc, wmod, bmod, nm, modT):
    P = k.P
    nch = nm * 32
    with ExitStack() as s:
        cin = k.sb(s, 'm_cin', [64, 128], F32)
        bin_ = [k.sb(s, 'm_bin%d' % i, [96, 128], F32) for i in range(nch // 96)]
        condT = k.sb(s, 'm_condT', [128, 32, 2], BF16)
        biasT = k.sb(s, 'm_biasT', [128, nch], F32)
        wt = [k.sb(s, 'm_wt%d' % i, [128, 32, 512], BF16) for i in range(2)]
        pst = k.ps(s, 'm_pst', [128, 256], F32)
        psm = k.ps(s, 'm_psm', [128, nch, 2], F32)
        k.dma(cin[:], cc, [], ['m_cin'])
        k.tr(pst[:, 0:64], cin[:], k.ident[0:64, 0:64], ['m_cin', 'ident'], ['m_pst'])
        k.act(condT[:].rearrange("p k r -> p r k"), pst[:, 0:64].rearrange("p (r k) -> p r k", r=2), AF.Silu,
              ['m_pst'], ['m_condT'])
        for i in range(nch // 96):
            k.dma(bin_[i][:], bmod[i * 96:(i + 1) * 96, :], [], ['m_bin%d' % i])
            k.tr(pst[:, 0:96], bin_[i][:], k.ident[0:96, 0:96], ['m_bin%d' % i, 'ident', 'm_condT', 'm_biasT'], ['m_pst'])
            k.cp(biasT[:, i * 96:(i + 1) * 96], pst[:, 0:96], ['m_pst'], ['m_biasT'])
        wv = wmod.rearrange("(kc p) n -> p kc n", p=128)
        ntile = nm * 8
        for t in range(ntile):
            wb = wt[t % 2]
            rn = 'm_wt%d' % (t % 2)
            for half in range(2):
                k.dma(wb[:, half * 16:(half + 1) * 16, :], wv[:, half * 16:(half + 1) * 16, t * 512:(t + 1) * 512],
                      [], [rn + 'h%d' % half], eng='pool')
            for j in range(4):
                ch = t * 4 + j
                for kc in range(32):
                    k.mm(psm[:, ch, :], wb[:, kc, j * 128:(j + 1) * 128], condT[:, kc, :], kc == 0, kc == 31,
                         [rn + 'h%d' % (kc // 16), 'm_condT'], ['m_psm'])
        k.tt(modT[:], psm[:], biasT[:].unsqueeze(2).to_broadcast([128, nch, 2]), ALU.add,
             ['m_psm', 'm_biasT'], ['modT'])
    P.barrier()


TGS = [(0, 2, 1)] + [(256 + 512 * j, 4, 0) for j in range(8)]


def emit_norm_hT(k, xsrc, tok0, ntl, r, AB, xt, xn_base, junk, ssq, trp, hT, xcount, hname='hT'):
    for ti in range(ntl):
        xb = xt[xcount[0] % 2]
        xn = xn_base + str(xcount[0] % 2)
        xcount[0] += 1
        t0 = tok0 + ti * 128
        k.dma(xb[:], xsrc[t0:t0 + 128, :], [], [xn])
        k.act(junk[:], xb[:], AF.Square, [xn], ['n_junk', 'n_ssq'], accum_out=ssq[:, 0:1])
        k.rstd_from_ssq(ssq[:, 1:2], ssq[:, 0:1], D, ['n_ssq'], ['n_ssq'])
        k.ts(xb[:], xb[:], ssq[:, 1:2], None, ALU.mult, None, [xn, 'n_ssq'], [xn])
        for kb in range(4):
            tp = trp[kb % 2]
            tn = 'n_trp%d' % (kb % 2)
            for j in range(8):
                kc = kb * 8 + j
                k.tr(tp[:, j, :], xb[:, kc * 128:(kc + 1) * 128], k.ident[:], [xn, 'ident'], [tn], sig=(j == 7))
            for j in range(8):
                kc = kb * 8 + j
                k.ts(hT[:, kc, ti * 128:(ti + 1) * 128], tp[:, j, :], AB[:, 0, kc, r:r + 1], AB[:, 1, kc, r:r + 1],
                     ALU.mult, ALU.add, [tn, 'AB'], [hname])


def emit_AB(k, s, gsrc, modT, m_shift, m_scale, AB, trp):
    gin = k.sb(s, 'ab_gin', [32, 128], F32)
    gT = k.sb(s, 'ab_gT', [128, 32], F32)
    k.dma(gin[:], gsrc, [], ['ab_gin'])
    k.tr(trp[0][:, 0, 0:32], gin[:], k.ident[0:32, 0:32], ['ab_gin', 'ident'], ['n_trp0'])
    k.cp(gT[:], trp[0][:, 0, 0:32], ['n_trp0'], ['ab_gT'])
    k.ts(AB[:, 0], modT[:, m_scale * 32:(m_scale + 1) * 32, :], 1.0, None, ALU.add, None, ['modT'], ['AB'])
    k.tt(AB[:, 0], AB[:, 0], gT[:].unsqueeze(2).to_broadcast([128, 32, 2]), ALU.mult, ['AB', 'ab_gT'], ['AB'])
    k.cp(AB[:, 1], modT[:, m_shift * 32:(m_shift + 1) * 32, :], ['modT', 'AB'], ['AB'])


def emit_cast_weight(k, dst, src, nrows_blk, tag):
    R = src.shape[0]
    names = []
    for r0 in range(0, R, nrows_blk):
        nm = (tag, r0)
        k.dma(dst[r0:r0 + nrows_blk, :], src[r0:r0 + nrows_blk, :], [], [nm], eng='pool')
        names.append(nm)
    return names


def emit_cast_tiled(k, dst4, src, regions, tag, tw=256):
    names = []
    for kc in range(32):
        for (c0, ncol, t0) in regions:
            nm = (tag, kc, c0)
            nfull = ncol // tw
            for g0 in range(0, nfull, 32):
                g1 = min(nfull, g0 + 32)
                nmg = (tag, kc, c0, g0)
                k.dma(dst4[:, t0 + g0:t0 + g1, kc, :], src[kc * 128:(kc + 1) * 128, c0 + g0 * tw:c0 + g1 * tw].rearrange("p (t e) -> p t e", e=tw),
                      [], [nmg], eng='pool')
                names.append(nmg)
            rem = ncol - nfull * tw
            if rem:
                k.dma(dst4[:, t0 + nfull, kc, 0:rem], src[kc * 128:(kc + 1) * 128, c0 + nfull * tw:c0 + ncol], [], [(tag, kc, c0, 'r')], eng='pool')
                names.append((tag, kc, c0, 'r'))
    return names


def emit_inproj(k, xcat, g1, modT, win, NFM, NTM, PT, Vh, win_b):
    P = k.P
    chunks = [(c0, min(128, NFM - c0)) for c0 in range(0, NFM, 128)]
    with ExitStack() as s:
        wnames = k.castnames['win']
        AB = k.sb(s, 'i_AB', [128, 2, 32, 2], F32)
        xt = [k.sb(s, 'i_xt%d' % i, [128, D], F32) for i in range(2)]
        junk = k.sb(s, 'i_junk', [128, D], BF16)
        ssq = k.sb(s, 'i_ssq', [128, 4], F32)
        hTs = [k.sb(s, 'i_hT%d' % i, [128, 32, 512], BF16) for i in range(2)]
        wt = [k.sb(s, 'i_wt%d' % i, [128, 32, 256], BF16) for i in range(3)]
        ost = [k.sb(s, 'i_ost%d' % i, [128, 512], F32) for i in range(4)]
        ostb = [k.sb(s, 'i_ostb%d' % i, [128, 256], BF16) for i in range(2)]
        trp = [k.ps(s, 'i_trp%d' % i, [128, 8, 128], F32) for i in range(2)]
        acc = [k.ps(s, 'i_acc%d' % i, [128, 512], F32) for i in range(2)]
        emit_AB(k, s, g1, modT, 0, 1, AB, trp)
        xcount = [0]
        wi = oi = ai = bi = 0
        for gidx, (tok0, ntl, r) in enumerate(TGS):
            ntok = ntl * 128
            hT = hTs[gidx % 2]
            hname = 'hT%d' % (gidx % 2)
            emit_norm_hT(k, xcat, tok0, ntl, r, AB, xt, 'i_xt', junk, ssq, trp, hT, xcount, hname)
            tiles = [(c0, min(256, NFM - c0)) for c0 in range(0, NFM, 256)] + \
                    [(NFM + c0, min(256, NTM - c0)) for c0 in range(0, NTM, 256)]
            for tix, (c0, wd) in enumerate(tiles):
                wb = wt[wi % 3]
                wn = 'i_wt%d' % (wi % 3)
                wi += 1
                k.dma(wb[:], win_b[:, tix], wnames, [wn])
                if c0 < NFM:
                    for cc0 in range(c0, c0 + wd, 128):
                        c = cc0 // 128
                        cw = chunks[c][1]
                        off = cc0 - c0
                        a = acc[ai % 2]
                        an = 'i_acc%d' % (ai % 2)
                        ai += 1
                        for kc in range(32):
                            k.mm(a[0:cw, 0:ntok], wb[:, kc, off:off + cw], hT[:, kc, 0:ntok], kc == 0, kc == 31,
                                 [wn, hname], [an])
                        o = ost[oi % 4]
                        on = 'i_ost%d' % (oi % 4)
                        oi += 1
                        k.cp(o[0:cw, 0:ntok], a[0:cw, 0:ntok], [an], [on], eng='act')
                        k.dma(PT[c * 128:c * 128 + cw, tok0:tok0 + ntok], o[0:cw, 0:ntok], [on], [('PT', c, tok0)])
                else:
                    tc0 = c0 - NFM
                    for ti in range(ntl):
                        a = acc[ai % 2]
                        an = 'i_acc%d' % (ai % 2)
                        ai += 1
                        for kc in range(32):
                            k.mm(a[:, 0:wd], hT[:, kc, ti * 128:(ti + 1) * 128], wb[:, kc, 0:wd], kc == 0, kc == 31,
                                 [wn, hname], [an])
                        o = ostb[bi % 2]
                        on = 'i_ostb%d' % (bi % 2)
                        bi += 1
                        k.cp(o[:, 0:wd], a[:, 0:wd], [an], [on], eng='act')
                        t0 = tok0 + ti * 128
                        k.dma(Vh[t0:t0 + 128, tc0:tc0 + wd], o[:, 0:wd], [on], [('Vh', t0, tc0)])
    P.barrier()


def emit_hgrn2(k, PT, Vh, lbl, onorm, OF, OB, GT):
    P = k.P
    NBLK = 4
    BL = TT // NBLK
    CPB = BL // 64
    with ExitStack() as s:
        lin = k.sb(s, 'h_lin', [64, 128], F32)
        lT = k.sb(s, 'h_lT', [128, 64], F32)
        lb = k.sb(s, 'h_lb', [128, 2, 16], F32)
        oml = k.sb(s, 'h_oml', [128, 2, 16], F32)
        gin = k.sb(s, 'h_gin', [1, 128], F32)
        gcol = k.sb(s, 'h_gcol', [128, 1], F32)
        maskF = k.sb(s, 'h_maskF', [64, 64], F32)
        maskB = k.sb(s, 'h_maskB', [64, 64], F32)
        smask = k.sb(s, 'h_smask', [128, BL], F32)
        q32 = k.sb(s, 'h_q32', [128, BL], F32)
        f32 = k.sb(s, 'h_f32', [128, BL], F32)
        g32 = k.sb(s, 'h_g32', [128, BL], F32)
        b32 = k.sb(s, 'h_b32', [128, BL], F32)
        t32 = k.sb(s, 'h_t32', [128, BL], F32)
        Qt = [k.sb(s, 'h_Qt%d' % i, [128, TT], BF16) for i in range(2)]
        Kt = [k.sb(s, 'h_Kt%d' % i, [128, TT], BF16) for i in range(2)]
        Qh = [k.sb(s, 'h_Qh%d' % i, [128, TT], BF16) for i in range(2)]
        KhT = k.sb(s, 'h_KhT', [128, BL], BF16)
        Ktok = [k.sb(s, 'h_Ktok%d' % i, [64, NCH, 128], BF16) for i in range(2)]
        Vsb = k.sb(s, 'h_Vsb', [64, NCH, 128], BF16)
        decay = [k.sb(s, 'h_decay%d' % i, [128, NCH], F32) for i in range(2)]
        S32 = [k.sb(s, 'h_S32%d' % i, [128, 128], F32) for i in range(2)]
        Sb = [k.sb(s, 'h_Sb%d' % i, [128, 128], BF16) for i in range(2)]
        attm = [k.sb(s, 'h_attm%d' % i, [64, 64], BF16) for i in range(2)]
        ostg = [k.sb(s, 'h_ostg%d' % i, [128, 256], F32) for i in range(4)]
        ptr = k.ps(s, 'h_ptr', [64, 4, 128], BF16)
        pmisc = k.ps(s, 'h_pmisc', [128, 128], F32)
        attP = [k.ps(s, 'h_attP%d' % i, [64, 64], F32) for i in range(2)]
        oP = [k.ps(s, 'h_oP%d' % i, [128, 64], F32) for i in range(2)]
        kvP = [k.ps(s, 'h_kvP%d' % i, [128, 128], F32) for i in range(2)]
        k.dma(lin[:], lbl, [], ['h_lin'])
        k.tr(pmisc[:, 0:64], lin[:], k.ident[0:64, 0:64], ['h_lin', 'ident'], ['h_pmisc'])
        k.cp(lT[:], pmisc[:, 0:64], ['h_pmisc'], ['h_lT'])
        lv = lT[:].rearrange("p (d sl h) -> p d sl h", d=2, sl=2)
        k.tt(lb[:], lv[:, :, 0, :], lv[:, :, 1, :], ALU.subtract, ['h_lT'], ['h_lb'])
        k.act(lb[:], lb[:], AF.Sigmoid, ['h_lb'], ['h_lb'])
        k.ts(oml[:], lb[:], -1.0, 1.0, ALU.mult, ALU.add, ['h_lb'], ['h_oml'])
        k.dma(gin[:], onorm, [], ['h_gin'])
        k.tr(pmisc[:, 0:1], gin[:], k.ident[0:1, 0:1], ['h_gin', 'ident', 'h_lT'], ['h_pmisc'])
        k.cp(gcol[:], pmisc[:, 0:1], ['h_pmisc'], ['h_gcol'])
        k.P.op('dve', lambda e: e.tensor_single_scalar(maskF[:], k.iota_i[0:64, 0:64], 0, ALU.is_ge), ['iota_i'], ['h_maskF'])
        k.P.op('dve', lambda e: e.tensor_single_scalar(maskB[:], k.iota_i[0:64, 0:64], 0, ALU.is_le), ['iota_i'], ['h_maskB'])
        k.memset(smask[:], 1.0, ['h_smask'])
        k.memset(smask[:].rearrange("p (n c) -> p n c", c=64)[:, :, 0:1], 0.0, ['h_smask'])
        c_q = 128.0 ** -0.5
        for h in range(16):
            k.dma(Vsb[:], Vh[:, h * 128:(h + 1) * 128].rearrange("(n p) v -> p n v", p=64), [], ['h_Vsb'])
            for d in range(2):
                mid, last = (31, 63) if d == 0 else (32, 0)
                for blk in range(NBLK):
                    tsl = slice(blk * BL, (blk + 1) * BL)
                    k.dma(q32[:], PT[h * 128:(h + 1) * 128, tsl], [], ['h_q32'])
                    fc = (16 + h) if d == 0 else (32 + h)
                    k.dma(f32[:], PT[fc * 128:(fc + 1) * 128, tsl], [], ['h_f32'])
                    k.act(q32[:], q32[:], AF.Silu, ['h_q32'], ['h_q32'])
                    k.act(f32[:], f32[:], AF.Sigmoid, ['h_f32'], ['h_f32'])
                    k.ts(f32[:], f32[:], oml[:, d, h:h + 1], lb[:, d, h:h + 1], ALU.mult, ALU.add, ['h_f32', 'h_oml', 'h_lb'], ['h_f32'])
                    k.act(g32[:], f32[:], AF.Ln, ['h_f32'], ['h_g32'])
                    k.ts(f32[:], f32[:], -1.0, 1.0, ALU.mult, ALU.add, ['h_f32', 'h_g32'], ['h_f32'])
                    if d == 0:
                        k.P.op('dve', lambda e: e.tensor_tensor_scan(b32[:], smask[:], g32[:], 0.0, ALU.mult, ALU.add),
                               ['h_smask', 'h_g32'], ['h_b32'])
                    else:
                        k.P.op('dve', lambda e: e.tensor_tensor_scan(b32[:, ::-1], smask[:], g32[:, ::-1], 0.0, ALU.mult, ALU.add),
                               ['h_smask', 'h_g32'], ['h_b32'])
                    b3 = b32[:].rearrange("p (n c) -> p n c", c=64)
                    g3 = g32[:].rearrange("p (n c) -> p n c", c=64)
                    k.tt(g3, b3, b3[:, :, mid:mid + 1].to_broadcast([128, CPB, 64]), ALU.subtract, ['h_b32', 'h_g32'], ['h_g32'])
                    k.act(t32[:], g32[:], AF.Exp, ['h_g32'], ['h_t32'])
                    k.stt(Qt[d][:, tsl], q32[:], c_q, t32[:], ALU.mult, ALU.mult, ['h_q32', 'h_t32'], ['h_Qt%d' % d])
                    k.act(t32[:], g32[:], AF.Exp, ['h_g32', 'h_Qt%d' % d], ['h_t32'], scale=-1.0)
                    k.tt(Kt[d][:, tsl], f32[:], t32[:], ALU.mult, ['h_f32', 'h_t32'], ['h_Kt%d' % d])
                    k.act(t32[:], b32[:], AF.Exp, ['h_b32', 'h_Kt%d' % d], ['h_t32'])
                    k.stt(Qh[d][:, tsl], q32[:], c_q, t32[:], ALU.mult, ALU.mult, ['h_q32', 'h_t32'], ['h_Qh%d' % d])
                    k.tt(g3, b3[:, :, last:last + 1].to_broadcast([128, CPB, 64]), b3, ALU.subtract, ['h_b32', 'h_g32', 'h_Qh%d' % d], ['h_g32'])
                    k.act(t32[:], g32[:], AF.Exp, ['h_g32'], ['h_t32'])
                    k.tt(KhT[:], f32[:], t32[:], ALU.mult, ['h_f32', 'h_t32'], ['h_KhT'])
                    k.act(decay[d][:, blk * CPB:(blk + 1) * CPB], b3[:, :, last], AF.Exp, ['h_b32'], ['h_decay%d' % d])
                    for c in range(CPB):
                        n = blk * CPB + c
                        k.tr(ptr[:, c % 4, :], KhT[:, c * 64:(c + 1) * 64], k.identb[:], ['h_KhT', 'identb'], ['h_ptr'],
                             sig=(c % 4 == 3 or c == CPB - 1))
                        if c % 4 == 3 or c == CPB - 1:
                            n0 = n - (c % 4)
                            k.cp(Ktok[d][:, n0:n + 1, :], ptr[:, 0:(c % 4) + 1, :], ['h_ptr'], ['h_Ktok%d' % d])
            orders = [list(range(NCH)), [3, 2, 1, 0] + list(range(NCH - 1, 3, -1))]
            for idx in range(NCH):
                for d in range(2):
                    n = orders[d][idx]
                    mask = maskF if d == 0 else maskB
                    Odst = OF if d == 0 else OB
                    t0 = n * 64
                    aP = attP[d]
                    an = 'h_attP%d' % d
                    am = attm[d]
                    amn = 'h_attm%d' % d
                    k.mm(aP[:], Kt[d][:, t0:t0 + 64], Qt[d][:, t0:t0 + 64], True, True, ['h_Kt%d' % d, 'h_Qt%d' % d], [an])
                    k.tt(am[:], aP[:], mask[:], ALU.mult, [an, 'h_maskF', 'h_maskB'], [amn])
                    o = oP[d]
                    on = 'h_oP%d' % d
                    k.mm(o[:], Vsb[:, n, :], am[:], True, idx == 0, ['h_Vsb', amn], [on])
                    if idx > 0:
                        k.mm(o[:], Sb[d][:], Qh[d][:, t0:t0 + 64], False, True, ['h_Sb%d' % d, 'h_Qh%d' % d], [on])
                    grp = n // 4
                    og = ostg[d * 2 + grp % 2]
                    ogn = 'h_ostg%d' % (d * 2 + grp % 2)
                    k.cp(og[:, (n % 4) * 64:(n % 4 + 1) * 64], o[:], [on], [ogn], eng='act')
                    last_in_grp = (n % 4 == 3) if d == 0 else (n % 4 == 0)
                    if last_in_grp:
                        k.dma(Odst[h * 128:(h + 1) * 128, grp * 256:(grp + 1) * 256], og[:], [ogn], [('O', d, h, grp)])
                    kv = kvP[d]
                    kvn = 'h_kvP%d' % d
                    k.mm(kv[:], Ktok[d][:, n, :], Vsb[:, n, :], True, True, ['h_Ktok%d' % d, 'h_Vsb'], [kvn])
                    if idx == 0:
                        k.cp(S32[d][:], kv[:], [kvn], ['h_S32%d' % d])
                    else:
                        k.stt(S32[d][:], S32[d][:], decay[d][:, n:n + 1], kv[:], ALU.mult, ALU.add, ['h_S32%d' % d, 'h_decay%d' % d, kvn], ['h_S32%d' % d])
                    k.cp(Sb[d][:], S32[d][:], ['h_S32%d' % d], ['h_Sb%d' % d], eng='act')
    P.barrier()
    with ExitStack() as s:
        gin = k.sb(s, 'r_gin', [1, 128], F32)
        gcol = k.sb(s, 'r_gcol', [128, 1], F32)
        of = [k.sb(s, 'r_of%d' % i, [128, 512], F32) for i in range(2)]
        ob = [k.sb(s, 'r_ob%d' % i, [128, 512], F32) for i in range(2)]
        gt = [k.sb(s, 'r_gt%d' % i, [128, 512], F32) for i in range(2)]
        sq = [k.sb(s, 'r_sq%d' % i, [128, 512], BF16) for i in range(2)]
        rs = [k.sb(s, 'r_rs%d' % i, [128, 512], F32) for i in range(2)]
        ot = [k.sb(s, 'r_ot%d' % i, [128, 512], BF16) for i in range(2)]
        pss = [k.ps(s, 'r_pss%d' % i, [128, 512], F32) for i in range(2)]
        pm = k.ps(s, 'r_pm', [128, 16], F32)
        k.dma(gin[:], onorm, [], ['r_gin'])
        k.tr(pm[:, 0:1], gin[:], k.ident[0:1, 0:1], ['r_gin', 'ident'], ['r_pm'])
        k.cp(gcol[:], pm[:, 0:1], ['r_pm'], ['r_gcol'])
        it = 0
        for h in range(16):
            for (t0, nt) in [(j * 512, 512) for j in range(8)] + [(4096, 256)]:
                i = it % 2
                it += 1
                k.dma(of[i][:, 0:nt], OF[h * 128:(h + 1) * 128, t0:t0 + nt], [], ['r_of%d' % i])
                k.dma(ob[i][:, 0:nt], OB[h * 128:(h + 1) * 128, t0:t0 + nt], [], ['r_ob%d' % i])
                k.dma(gt[i][:, 0:nt], PT[(48 + h) * 128:(49 + h) * 128, t0:t0 + nt], [], ['r_gt%d' % i])
                k.tt(of[i][:, 0:nt], of[i][:, 0:nt], ob[i][:, 0:nt], ALU.add, ['r_of%d' % i, 'r_ob%d' % i], ['r_of%d' % i])
                k.act(sq[i][:, 0:nt], of[i][:, 0:nt], AF.Square, ['r_of%d' % i], ['r_sq%d' % i])
                k.mm(pss[i][:, 0:nt], k.onesb[:], sq[i][:, 0:nt], True, True, ['onesb', 'r_sq%d' % i], ['r_pss%d' % i])
                k.rstd_from_ssq(rs[i][:, 0:nt], pss[i][:, 0:nt], 128, ['r_pss%d' % i], ['r_rs%d' % i])
                k.stt(of[i][:, 0:nt], of[i][:, 0:nt], gcol[:, 0:1], rs[i][:, 0:nt], ALU.mult, ALU.mult,
                      ['r_of%d' % i, 'r_gcol', 'r_rs%d' % i], ['r_of%d' % i])
                k.act(gt[i][:, 0:nt], gt[i][:, 0:nt], AF.Silu, ['r_gt%d' % i], ['r_gt%d' % i])
                k.tt(ot[i][:, 0:nt], of[i][:, 0:nt], gt[i][:, 0:nt], ALU.mult, ['r_of%d' % i, 'r_gt%d' % i], ['r_ot%d' % i])
                k.dma(GT[h * 128:(h + 1) * 128, t0:t0 + nt], ot[i][:, 0:nt], ['r_ot%d' % i], [('GT', h, t0)])
    P.barrier()


def emit_rope_tables(k, s, dim, cosT, sinS, perm, tag):
    half = dim // 2
    nf = dim // 4
    pf = k.sb(s, tag + 'pf', [dim, 4], F32)
    grid = k.sb(s, tag + 'grid', [dim, 64, 64], I32)
    ang = k.sb(s, tag + 'ang', [dim, 4096], F32)
    fr = k.sb(s, tag + 'fr', [dim, 4096], F32)
    fi = k.sb(s, tag + 'fi', [dim, 4096], I32)
    T = tag
    vi = k.sb(s, tag + 'vi', [1, 3, dim], I32)
    vf = k.sb(s, tag + 'vf', [1, 3, dim], F32)
    pp = k.ps(s, tag + 'pp', [128, 4], F32)
    k.P.op('pool', lambda e: e.iota(vi[:, 0, :], pattern=[[0, 2], [0, 2], [1, nf]], base=0, channel_multiplier=0), (), [T + 'vi'])
    k.P.op('pool', lambda e: e.iota(vi[:, 1, :], pattern=[[0, 2], [1, 2], [0, nf]], base=0, channel_multiplier=0), (), [T + 'vi'])
    k.P.op('pool', lambda e: e.iota(vi[:, 2, :], pattern=[[1, 2], [0, 2], [0, nf]], base=0, channel_multiplier=0), (), [T + 'vi'])
    k.cp(vf[:], vi[:], [T + 'vi'], [T + 'vf'])
    for j in range(3):
        k.tr(pp[0:dim, j:j + 1], vf[:, j, :], k.ident[0:1, 0:1], [T + 'vf', 'ident'], [T + 'pp'])
    k.cp(pf[:, 0:3], pp[0:dim, 0:3], [T + 'pp'], [T + 'pf'])
    k.act(pf[:, 0:1], pf[:, 0:1], AF.Exp, [T + 'pf'], [T + 'pf'], scale=-math.log(10000.0) / nf)
    k.ts(pf[:, 3:4], pf[:, 2:3], 2.0, -1.0, ALU.mult, ALU.add, [T + 'pf'], [T + 'pf'])
    k.tt(pf[:, 2:3], pf[:, 1:2], pf[:, 0:1], ALU.mult, [T + 'pf'], [T + 'pf'])
    k.ts(pf[:, 1:2], pf[:, 1:2], -1.0, 1.0, ALU.mult, ALU.add, [T + 'pf'], [T + 'pf'])
    k.tt(pf[:, 1:2], pf[:, 1:2], pf[:, 0:1], ALU.mult, [T + 'pf'], [T + 'pf'])
    k.P.op('pool', lambda e: e.iota(grid[:], pattern=[[1, 64], [0, 64]], base=0, channel_multiplier=0), (), [T + 'grid'])
    k.cp(fr[:], grid[:].rearrange("p a b -> p (a b)"), [T + 'grid'], [T + 'fr'])
    k.ts(ang[:], fr[:], pf[:, 1:2], None, ALU.mult, None, [T + 'fr', T + 'pf'], [T + 'ang'])
    k.P.op('pool', lambda e: e.iota(grid[:], pattern=[[0, 64], [1, 64]], base=0, channel_multiplier=0), [T + 'fr'], [T + 'grid'])
    k.cp(fr[:], grid[:].rearrange("p a b -> p (a b)"), [T + 'grid', T + 'ang'], [T + 'fr'])
    k.stt(ang[:], fr[:], pf[:, 2:3], ang[:], ALU.mult, ALU.add, [T + 'fr', T + 'pf', T + 'ang'], [T + 'ang'])

    def sin_of(dst, shift, post_sign):
        k.ts(fr[:], ang[:], shift, 1.0 / (2 * math.pi), ALU.add, ALU.mult, [T + 'ang', T + 'fr'], [T + 'fr'])
        k.cp(fi[:], fr[:], [T + 'fr'], [T + 'fi'])
        k.cp(dst, fi[:], [T + 'fi'], [T + 'dst'])
        k.tt(fr[:], fr[:], dst, ALU.subtract, [T + 'fr', T + 'dst'], [T + 'fr'])
        k.ts(dst, fr[:], 0.5, None, ALU.is_gt, None, [T + 'fr'], [T + 'dst'])
        k.tt(fr[:], fr[:], dst, ALU.subtract, [T + 'fr', T + 'dst'], [T + 'fr'])
        k.ts(dst, fr[:], -0.5, None, ALU.is_lt, None, [T + 'fr'], [T + 'dst'])
        k.tt(fr[:], fr[:], dst, ALU.add, [T + 'fr', T + 'dst'], [T + 'fr'])
        k.act(dst, fr[:], AF.Sin, [T + 'fr'], [T + 'dst'], scale=2 * math.pi)
        if post_sign:
            k.ts(dst, dst, pf[:, 3:4], None, ALU.mult, None, [T + 'dst', T + 'pf'], [T + 'dst'])

    sin_of(cosT[:], math.pi / 2, False)
    sin_of(sinS[:], 0.0, True)
    k.P.op('dve', lambda e: e.tensor_single_scalar(perm[:], k.iota_i[0:dim, 0:dim], half, ALU.is_equal), ['iota_i'], [T + 'perm'])
    k.P.op('dve', lambda e: e.tensor_single_scalar(fr[:, 0:dim], k.iota_i[0:dim, 0:dim], -half, ALU.is_equal), ['iota_i', T + 'fr'], [T + 'fr'])
    k.tt(perm[:], perm[:], fr[:, 0:dim], ALU.add, [T + 'perm', T + 'fr'], [T + 'perm'])


QBLKS = [(0, 256, 2)] + [(256 + 512 * j, 512, 34) for j in range(8)]


def emit_attn_core(k, maps, Vsb, ndv, scale, vname, sP, oP, rP, pT, finalize):
    it = 0
    for (q0, nq, nkt) in QBLKS:
        for mi, parts in enumerate(maps):
            for kt in range(nkt):
                sp = sP[it % len(sP)]
                spn = 'a_sP%d' % (it % len(sP))
                pt = pT[it % len(pT)]
                ptn = 'a_pT%d' % (it % len(pT))
                it += 1
                for pi_, (Kap, Qap, kd, kn, qn) in enumerate(parts):
                    k.mm(sp[:, 0:nq], Kap[0:kd, kt * 128:(kt + 1) * 128], Qap[0:kd, q0:q0 + nq], pi_ == 0, pi_ == len(parts) - 1,
                         [kn, qn], [spn])
                k.act(pt[:, 0:nq], sp[:, 0:nq], AF.Exp, [spn], [ptn], scale=scale, bias=k.negshift[:, 0:1])
                for dvc in range(ndv):
                    k.mm(oP[dvc][:, 0:nq], Vsb[:, kt, dvc * 128:(dvc + 1) * 128], pt[:, 0:nq], kt == 0, kt == nkt - 1,
                         [vname, ptn], ['a_oP%d' % dvc])
                k.mm(rP[:, 0:nq], k.onesb[:], pt[:, 0:nq], kt == 0, kt == nkt - 1, ['onesb', ptn], ['a_rP'])
            finalize(q0, nq, mi)


BLK9 = [(0, 256)] + [(256 + j * 512, 512) for j in range(8)]


class _Stop(Exception):
    pass


def emit_mla(k, PT, wuq, wukv, gains, GT, QT, KT, KrT, VT, mla_stop=None):
    try:
        _emit_mla(k, PT, wuq, wukv, gains, GT, QT, KT, KrT, VT, mla_stop)
    except _Stop:
        k.P.barrier()


def _emit_mla(k, PT, wuq, wukv, gains, GT, QT, KT, KrT, VT, mla_stop=None):
    P = k.P
    scale = 192.0 ** -0.5
    with ExitStack() as s:
        gi = k.sb(s, 'm_gi', [14, 128], F32)
        gT = k.sb(s, 'm_gT', [128, 14], F32)
        wq = k.sb(s, 'm_wq', [128, 6, 3072], BF16)
        wkv = k.sb(s, 'm_wkv', [128, 4, 4096], BF16)
        perm = k.sb(s, 'm_perm', [64, 64], F32)
        cosb = k.sb(s, 'm_cosb', [64, 4096], BF16)
        sinb = k.sb(s, 'm_sinb', [64, 4096], BF16)
        with ExitStack() as s1:
            cosT = k.sb(s1, 'm_cos', [64, 4096], F32)
            sinS = k.sb(s1, 'm_sin', [64, 4096], F32)
            with ExitStack() as s2:
                emit_rope_tables(k, s2, 64, cosT, sinS, perm, 'rb_')
            P.barrier()
            k.cp(cosb[:], cosT[:], [], ['m_cosb'])
            k.cp(sinb[:], sinS[:], [], ['m_sinb'])
            P.barrier()
        xq = k.sb(s, 'm_xq', [128, 6, 512], F32)
        xkv = k.sb(s, 'm_xkv', [128, 4, 512], F32)
        cqn = k.sb(s, 'm_cqn', [128, 6, 512], BF16)
        ckvn = k.sb(s, 'm_ckvn', [128, 4, 512], BF16)
        sq = [k.sb(s, 'm_sq%d' % i, [128, 512], BF16) for i in range(2)]
        rs = k.sb(s, 'm_rs', [128, 512], F32)
        kx = [k.sb(s, 'm_kx%d' % i, [128, 512], F32) for i in range(2)]
        kr2 = k.sb(s, 'm_kr2', [64, 512], F32)
        outb = [k.sb(s, 'm_outb%d' % i, [128, 512], BF16) for i in range(3)]
        vout = [k.sb(s, 'm_vout%d' % i, [128, 512], BF16) for i in range(2)]
        pss = k.ps(s, 'm_pss', [128, 512], F32)
        pup = [k.ps(s, 'm_pup%d' % i, [128, 512], F32) for i in range(3)]
        pm = k.ps(s, 'm_pm', [128, 512], F32)
        k.dma(gi[:], gains, [], ['m_gi'])
        k.tr(pm[:, 0:14], gi[:], k.ident[0:14, 0:14], ['m_gi', 'ident'], ['m_pm'])
        k.cp(gT[:], pm[:, 0:14], ['m_pm'], ['m_gT'])
        for kc in range(6):
            k.dma(wq[:, kc, :], wuq[kc * 128:(kc + 1) * 128, :], [], [('m_wq', kc)], eng='pool')
        for kc in range(4):
            k.dma(wkv[:, kc, :], wukv[kc * 128:(kc + 1) * 128, :], [], [('m_wkv', kc)], eng='pool')
        WQ = [('m_wq', kc) for kc in range(6)]
        WKV = [('m_wkv', kc) for kc in range(4)]
        cnt = {'u': 0, 'o': 0, 'x': 0, 'v': 0}

        def rms1(src, gcol, n, dst, rows, srcn, dstn):
            nt = src.shape[1]
            k.act(sq[0][0:rows, 0:nt], src, AF.Square, [srcn], ['m_sq0'])
            k.mm(pss[0:rows, 0:nt], k.onesb[0:rows, 0:rows], sq[0][0:rows, 0:nt], True, True, ['onesb', 'm_sq0'], ['m_pss'])
            k.rstd_from_ssq(rs[0:rows, 0:nt], pss[0:rows, 0:nt], n, ['m_pss'], ['m_rs'])
            k.stt(dst, src, gcol, rs[0:rows, 0:nt], ALU.mult, ALU.mult, [srcn, 'm_gT', 'm_rs'], [dstn])

        def rope_fm(x32, xn, t0, nt, dst, dstn):
            if t0 < TC:
                k.cp(dst, x32, [xn], [dstn])
                return
            l0 = t0 - TC
            k.mm(pm[0:64, 0:nt], perm[:], x32, True, True, ['rb_perm', xn], ['m_pm'])
            k.tt(kr2[:, 0:nt], pm[0:64, 0:nt], sinb[:, l0:l0 + nt], ALU.mult, ['m_pm', 'm_sinb'], ['m_kr2'])
            k.tt(x32, x32, cosb[:, l0:l0 + nt], ALU.mult, [xn, 'm_cosb'], [xn])
            k.tt(dst, x32, kr2[:, 0:nt], ALU.add, [xn, 'm_kr2'], [dstn])

        try:
            for (t0, nt) in ([] if mla_stop == 'rope' else BLK9):
                if mla_stop == 'blk0' and t0 > 0:
                    break
                for (c0, ncs, goff, xb, xn, dstb, dn) in ((64, 6, 0, xq, 'm_xq', cqn, 'm_cqn'), (70, 4, 6, xkv, 'm_xkv', ckvn, 'm_ckvn')):
                    for c in range(ncs):
                        k.dma(xb[:, c, 0:nt], PT[(c0 + c) * 128:(c0 + c + 1) * 128, t0:t0 + nt], [], [(xn, c)])
                    for c in range(ncs):
                        sqi = sq[c % 2]
                        k.act(sqi[:, 0:nt], xb[:, c, 0:nt], AF.Square, [(xn, c)], ['m_sq%d' % (c % 2)])
                        k.mm(pss[:, 0:nt], k.onesb[:], sqi[:, 0:nt], c == 0, c == ncs - 1, ['onesb', 'm_sq%d' % (c % 2)], ['m_pss'], sig=True)
                    k.rstd_from_ssq(rs[:, 0:nt], pss[:, 0:nt], ncs * 128, ['m_pss'], ['m_rs'])
                    for c in range(ncs):
                        k.stt(dstb[:, c, 0:nt], xb[:, c, 0:nt], gT[:, goff + c:goff + c + 1], rs[:, 0:nt], ALU.mult, ALU.mult,
                              [(xn, c), 'm_gT', 'm_rs'], [dn])
                if mla_stop == 'n1':
                    raise _Stop()
                x = kx[cnt['x'] % 2]; xn = 'm_kx%d' % (cnt['x'] % 2); cnt['x'] += 1
                k.dma(x[0:64, 0:nt], PT[74 * 128:74 * 128 + 64, t0:t0 + nt], [], [xn])
                rms1(x[0:64, 0:nt], gT[0:64, 13:14], 64, x[0:64, 0:nt], 64, xn, xn)
                o = outb[cnt['o'] % 3]; on = 'm_outb%d' % (cnt['o'] % 3); cnt['o'] += 1
                rope_fm(x[0:64, 0:nt], xn, t0, nt, o[0:64, 0:nt], on)
                k.dma(KrT[:, t0:t0 + nt], o[0:64, 0:nt], [on], [('KrT', t0)])
                if mla_stop == 'kr':
                    raise _Stop()
                for h in range(16):
                    if mla_stop == 'h0' and h > 0:
                        raise _Stop()
                    jobs = [
                        (wkv, WKV, 4, h * 256, 128, ckvn, 'm_ckvn', gT[:, 11:12], 128, False, KT[h * 128:(h + 1) * 128, t0:t0 + nt]),
                        (wq, WQ, 6, h * 192, 128, cqn, 'm_cqn', gT[:, 10:11], 128, False, QT[h * 192:h * 192 + 128, t0:t0 + nt]),
                        (wq, WQ, 6, h * 192 + 128, 64, cqn, 'm_cqn', gT[0:64, 12:13], 64, True, QT[h * 192 + 128:h * 192 + 192, t0:t0 + nt]),
                    ]
                    for (W, WN, nk, col0, rows, src, srcn, gcol, n, rope, dst) in jobs:
                        pu = pup[cnt['u'] % 3]; pun = 'm_pup%d' % (cnt['u'] % 3); cnt['u'] += 1
                        for kc in range(nk):
                            k.mm(pu[0:rows, 0:nt], W[:, kc, col0:col0 + rows], src[:, kc, 0:nt], kc == 0, kc == nk - 1, WN + [srcn], [pun])
                        x = kx[cnt['x'] % 2]; xn = 'm_kx%d' % (cnt['x'] % 2); cnt['x'] += 1
                        k.cp(x[0:rows, 0:nt], pu[0:rows, 0:nt], [pun], [xn], eng='act')
                        o = outb[cnt['o'] % 3]; on = 'm_outb%d' % (cnt['o'] % 3); cnt['o'] += 1
                        if rope:
                            rms1(x[0:rows, 0:nt], gcol, n, x[0:rows, 0:nt], rows, xn, xn)
                            rope_fm(x[0:rows, 0:nt], xn, t0, nt, o[0:rows, 0:nt], on)
                        else:
                            rms1(x[0:rows, 0:nt], gcol, n, o[0:rows, 0:nt], rows, xn, on)
                        k.dma(dst, o[0:rows, 0:nt], [on], [('mq', h, col0, t0)])
                for tl in range(nt // 128):
                    for hg in range(4):
                        pu = pup[cnt['u'] % 3]; pun = 'm_pup%d' % (cnt['u'] % 3); cnt['u'] += 1
                        for hh in range(4):
                            h = hg * 4 + hh
                            for kc in range(4):
                                k.mm(pu[:, hh * 128:(hh + 1) * 128], ckvn[:, kc, tl * 128:(tl + 1) * 128],
                                     wkv[:, kc, h * 256 + 128:h * 256 + 256], kc == 0, kc == 3, WKV + ['m_ckvn'], [pun], sig=(kc == 3 and hh == 3))
                        vo = vout[cnt['v'] % 2]; von = 'm_vout%d' % (cnt['v'] % 2); cnt['v'] += 1
                        k.cp(vo[:], pu[:], [pun], [von], eng='act')
                        k.dma(VT[t0 + tl * 128:t0 + (tl + 1) * 128, hg * 512:(hg + 1) * 512], vo[:], [von], [('VT', t0, tl, hg)])
        except _Stop:
            pass
    P.barrier()
    if mla_stop is not None:
        raise _Stop()
    with ExitStack() as s:
        Kr = k.sb(s, 'a_Kr', [64, TT], BF16)
        Kn = [k.sb(s, 'a_Kn%d' % i, [128, TT], BF16) for i in range(2)]
        Qn = [k.sb(s, 'a_Qn%d' % i, [128, TT], BF16) for i in range(2)]
        Qr = [k.sb(s, 'a_Qr%d' % i, [64, TT], BF16) for i in range(2)]
        Vsb = [k.sb(s, 'a_Vsb%d' % i, [128, 34, 128], BF16) for i in range(2)]
        pT = [k.sb(s, 'a_pT%d' % i, [128, 512], BF16) for i in range(4)]
        rinv = k.sb(s, 'a_rinv', [128, 512], F32)
        ob = [k.sb(s, 'a_ob%d' % i, [128, 512], BF16) for i in range(2)]
        sP = [k.ps(s, 'a_sP%d' % i, [128, 512], F32) for i in range(4)]
        oP = [k.ps(s, 'a_oP0', [128, 512], F32)]
        rP = k.ps(s, 'a_rP', [128, 512], F32)
        k.dma(Kr[:], KrT, [], ['a_Kr'])
        oi = [0]
        for h in range(16):
            i = h % 2
            k.dma(Kn[i][:], KT[h * 128:(h + 1) * 128, :], [], ['a_Kn%d' % i])
            k.dma(Qn[i][:], QT[h * 192:h * 192 + 128, :], [], ['a_Qn%d' % i])
            k.dma(Qr[i][:], QT[h * 192 + 128:h * 192 + 192, :], [], ['a_Qr%d' % i])
            k.dma(Vsb[i][:], VT[:, h * 128:(h + 1) * 128].rearrange("(t p) v -> p t v", p=128), [], ['a_V%d' % i])

            def fin(q0, nq, mi, h=h):
                k.P.op('dve', lambda e: e.reciprocal(rinv[:, 0:nq], rP[:, 0:nq]), ['a_rP'], ['a_rinv'])
                o = ob[oi[0] % 2]
                on = 'a_ob%d' % (oi[0] % 2)
                oi[0] += 1
                k.tt(o[:, 0:nq], oP[0][:, 0:nq], rinv[:, 0:nq], ALU.mult, ['a_oP0', 'a_rinv'], [on])
                k.dma(GT[2048 + h * 128:2048 + (h + 1) * 128, q0:q0 + nq], o[:, 0:nq], [on], [('GT2', h, q0)])

            parts = [(Kn[i], Qn[i], 128, 'a_Kn%d' % i, 'a_Qn%d' % i), (Kr, Qr[i], 64, 'a_Kr', 'a_Qr%d' % i)]
            emit_attn_core(k, [parts], Vsb[i], 1, scale, 'a_V%d' % i, sP, oP, rP, pT, fin)
    P.barrier()


def emit_bcast_mod(k, s, modT, m, r, out, pbank, pname, tag):
    onesf = k.sb(s, tag + 'onesf', [128, 128], F32)
    dg = [k.sb(s, tag + 'dg%d' % i, [128, 128], F32) for i in range(2)]
    k.memset(onesf[:], 1.0, [tag + 'onesf'])
    for kc in range(32):
        d = dg[kc % 2]
        dn = tag + 'dg%d' % (kc % 2)
        k.ts(d[:], k.ident[:], modT[:, m * 32 + kc, r:r + 1], None, ALU.mult, None, ['ident', 'modT'], [dn])
        k.mm(pbank[:, (kc % 4) * 128:(kc % 4 + 1) * 128], onesf[:], d[:], True, True, [tag + 'onesf', dn], [pname])
        if kc % 4 == 3:
            k.cp(out[:, (kc - 3) * 128:(kc + 1) * 128], pbank[:, 0:512], [pname], [tag + 'out'])


def emit_outproj(k, xcat, GT, wout, wout_b, modT, m_gate, xmid, ntok_first=0):
    P = k.P
    with ExitStack() as s:
        wn = k.castnames['wout']
        modB = [k.sb(s, 'o_modB%d' % r, [128, D], F32) for r in range(2)]
        wt = [k.sb(s, 'o_wt%d' % i, [128, 32, 512], BF16) for i in range(2)]
        gtt = [k.sb(s, 'o_gt%d' % i, [128, 32, 512], BF16) for i in range(2)]
        xt = [k.sb(s, 'o_xt%d' % i, [128, 512], F32) for i in range(3)]
        pb = [k.ps(s, 'o_pb%d' % i, [128, 512], F32) for i in range(4)]
        for r in range(2):
            with ExitStack() as s2:
                emit_bcast_mod(k, s2, modT, m_gate, r, modB[r], pb[3], 'o_pb3', 'ob%d_' % r)
            P.barrier()
        wv = wout_b.rearrange("(kc p) n -> p kc n", p=128)
        gv = GT.rearrange("(kc p) t -> p kc t", p=128)
        gi = xi = ai = 0
        for db in range(8):
            w = wt[db % 2]
            wnm = 'o_wt%d' % (db % 2)
            for hf in range(2):
                k.dma(w[:, hf * 16:(hf + 1) * 16, :], wv[:, hf * 16:(hf + 1) * 16, db * 512:(db + 1) * 512], wn, [(wnm, hf)])
            for (tok0, ntl, r) in TGS:
                g = gtt[gi % 2]; gn = 'o_gt%d' % (gi % 2); gi += 1
                k.dma(g[:, :, 0:ntl * 128], gv[:, :, tok0:tok0 + ntl * 128], [], [gn])
                for ti in range(ntl):
                    t0 = tok0 + ti * 128
                    x = xt[xi % 3]; xn = 'o_xt%d' % (xi % 3); xi += 1
                    k.dma(x[:], xcat[t0:t0 + 128, db * 512:(db + 1) * 512], [], [xn])
                    a = pb[ai % 3]; an = 'o_pb%d' % (ai % 3); ai += 1
                    for kc in range(32):
                        k.mm(a[:], g[:, kc, ti * 128:(ti + 1) * 128], w[:, kc, :], kc == 0, kc == 31, [gn, (wnm, kc // 16)], [an])
                    k.tt(a[:], a[:], modB[r][:, db * 512:(db + 1) * 512], ALU.mult, [an], [an])
                    k.tt(x[:], x[:], a[:], ALU.add, [xn, an], [xn])
                    k.dma(xmid[t0:t0 + 128, db * 512:(db + 1) * 512], x[:], [xn], [('xmid', t0, db)])
    P.barrier()


def emit_norm2(k, xmid, g2, modT, m_shift, m_scale, H2T):
    P = k.P
    with ExitStack() as s:
        AB = k.sb(s, 'n_AB', [128, 2, 32, 2], F32)
        xt = [k.sb(s, 'n_xt%d' % i, [128, D], F32) for i in range(2)]
        junk = k.sb(s, 'n_junk', [128, D], BF16)
        ssq = k.sb(s, 'n_ssq', [128, 4], F32)
        hT = [k.sb(s, 'n_hT%d' % i, [128, 32, 512], BF16) for i in range(2)]
        trp = [k.ps(s, 'n_trp%d' % i, [128, 8, 128], F32) for i in range(2)]
        emit_AB(k, s, g2, modT, m_shift, m_scale, AB, trp)
        hv = H2T.rearrange("(kc p) t -> p kc t", p=128)
        xcount = [0]
        for gi, (tok0, ntl, r) in enumerate(TGS):
            h = hT[gi % 2]
            emit_norm_hT(k, xmid, tok0, ntl, r, AB, xt, 'n_xt', junk, ssq, trp, h, xcount, 'hT%d' % (gi % 2))
            k.dma(hv[:, :, tok0:tok0 + ntl * 128], h[:, :, 0:ntl * 128], ['hT%d' % (gi % 2)], [('H2T', tok0)])
    P.barrier()


PIECE = 256
NEG = -1.0e30


def emit_peer(k, xmid, H2T, wq, wq_b, skT, puT, puT_b, pv, pv_b, modT, m_gate, xout, pieces):
    P = k.P
    with ExitStack() as s:
        wqn = k.castnames['wq']
        pun = k.castnames['puT']
        pvn = k.castnames['pv']
        sk = k.sb(s, 'p_sk', [128, 16, 128], F32)
        mcol = k.sb(s, 'p_mcol', [128, 32, 2], F32)
        iotaF = k.sb(s, 'p_iotaF', [128, 128], F32)
        thr = k.sb(s, 'p_thr', [128, 16], F32)
        iota16 = k.sb(s, 'p_iota16', [128, 16], F32)
        h2 = k.sb(s, 'p_h2', [128, 32, PIECE], BF16)
        wt = [k.sb(s, 'p_wt%d' % i, [128, 32, 128], BF16) for i in range(4)]
        vt = [k.sb(s, 'p_vt%d' % i, [128, 1536], BF16) for i in range(4)]
        qT = k.sb(s, 'p_qT', [128, 16, PIECE], F32)
        ssb = k.sb(s, 'p_ssb', [128, 16, 128], F32)
        s2 = k.sb(s, 'p_s2', [128, 16, 128], F32)
        c16 = k.sb(s, 'p_c16', [128, 16, 16], F32)
        ix = k.sb(s, 'p_ix', [128, 16, 16], U32)
        ixf = k.sb(s, 'p_ixf', [128, 16, 16], F32)
        cand = ssb[:].rearrange("p (h t) c -> p h (t c)", t=2)
        cand2 = s2[:].rearrange("p (h t) c -> p h (t c)", t=2)
        top = k.sb(s, 'p_top', [128, 8, 16], F32)
        pos = k.sb(s, 'p_pos', [128, 8, 16], U32)
        posf = k.sb(s, 'p_posf', [128, 8, 16], F32)
        w4 = k.sb(s, 'p_w4', [128, 8, 16, 16], F32)
        ak = k.sb(s, 'p_ak', [128, 8, 16], F32)
        bk = k.sb(s, 'p_bk', [128, 8, 16], F32)
        ijg = k.sb(s, 'p_ijg', [128, 3, 128], F32)
        zs = k.sb(s, 'p_zs', [128, 8], F32)
        ijgT = k.sb(s, 'p_ijgT', [128, 3, PIECE], F32)
        P1 = k.sb(s, 'p_P1', [128, 32, 128], BF16)
        P2 = k.sb(s, 'p_P2', [128, 32, 128], BF16)
        GA = k.sb(s, 'p_GA', [128, 128, PIECE], BF16)
        gl = [k.sb(s, 'p_gl%d' % i, [128, PIECE], F32) for i in range(2)]
        ys = [k.sb(s, 'p_ys%d' % i, [128, 128], F32) for i in range(2)]
        xo = [k.sb(s, 'p_xo%d' % i, [128, 512], F32) for i in range(1)]
        pb = [k.ps(s, 'p_pb%d' % i, [128, 512], F32) for i in range(8)]
        PB = ['p_pb%d' % i for i in range(8)]
        k.dma(sk[:], skT, [], ['p_sk'])
        k.cp(mcol[:], modT[:, m_gate * 32:(m_gate + 1) * 32, :], ['modT'], ['p_mcol'])
        k.P.op('pool', lambda e: e.iota(s2[:, 0, :].bitcast(I32), pattern=[[1, 128]], base=0, channel_multiplier=0), (), ['p_s2'])
        k.cp(iotaF[:], s2[:, 0, :].bitcast(I32), ['p_s2'], ['p_iotaF'])
        k.ts(thr[:], iotaF[:, 0:16], 16.0, 16.0, ALU.mult, ALU.add, ['p_iotaF'], ['p_thr'])
        k.cp(iota16[:], iotaF[:, 0:16], ['p_iotaF'], ['p_iota16'])
        h2v = H2T.rearrange("(kc p) t -> p kc t", p=128)
        wqv = wq_b
        uv = puT_b
        vv = pv_b.rearrange("(i j) n -> j i n", j=128)
        wi = [0]

        def wtile(src, c0, names):
            w = wt[wi[0] % 4]; wn = 'p_wt%d' % (wi[0] % 4); wi[0] += 1
            k.dma(w[:], src[:, c0 // 128], names, [wn])
            return w, wn

        for t0 in pieces:
            r = 1 if t0 < TC else 0
            k.dma(h2[:], h2v[:, :, t0:t0 + PIECE], [], ['p_h2'])
            for j in range(16):
                w, wn = wtile(wqv, j * 128, wqn)
                a = pb[j % 2]
                for kc in range(32):
                    k.mm(a[:, 0:PIECE], w[:, kc, :], h2[:, kc, :], kc == 0, kc == 31, [wn, 'p_h2'], [PB[j % 2]])
                k.cp(qT[:, j, :], a[:, 0:PIECE], [PB[j % 2]], ['p_qT'], eng='act')
            for tl in range(PIECE // 128):
                for j in range(16):
                    b = pb[2 + (j // 4) % 2]; bn = PB[2 + (j // 4) % 2]
                    k.mm(b[:, (j % 4) * 128:(j % 4 + 1) * 128], qT[:, j, tl * 128:(tl + 1) * 128], sk[:, j, :], True, True,
                         ['p_qT', 'p_sk'], [bn], sig=(j % 4 == 3))
                    if j % 4 == 3:
                        k.cp(ssb[:, j - 3:j + 1, :], b[:, :].rearrange("p (a c) -> p a c", a=4), [bn], ['p_ssb'])
                for j in range(16):
                    k.P.op('dve', lambda e, j=j: e.max(out=c16[:, j, 0:8], in_=ssb[:, j, :]), ['p_ssb'], ['p_c16'])
                    k.P.op('dve', lambda e, j=j: e.max_index(out=ix[:, j, 0:8], in_max=c16[:, j, 0:8], in_values=ssb[:, j, :]), ['p_ssb', 'p_c16'], ['p_ix'])
                    k.P.op('dve', lambda e, j=j: e.match_replace(out=s2[:, j, :], in_to_replace=c16[:, j, 0:8], in_values=ssb[:, j, :], imm_value=NEG),
                           ['p_ssb', 'p_c16'], ['p_s2'])
                    k.P.op('dve', lambda e, j=j: e.max(out=c16[:, j, 8:16], in_=s2[:, j, :]), ['p_s2'], ['p_c16'])
                    k.P.op('dve', lambda e, j=j: e.max_index(out=ix[:, j, 8:16], in_max=c16[:, j, 8:16], in_values=s2[:, j, :]), ['p_s2', 'p_c16'], ['p_ix'])
                k.cp(ixf[:], ix[:], ['p_ix'], ['p_ixf'])
                c4 = c16[:].rearrange("p (h t) a -> p h t a", t=2)
                i4 = ixf[:].rearrange("p (h t) a -> p h t a", t=2)
                candv = cand.rearrange("p h (a b) -> p h a b", a=16)
                k.tt(candv, c4[:, :, 0, :].unsqueeze(3).to_broadcast([128, 8, 16, 16]),
                     c4[:, :, 1, :].unsqueeze(2).to_broadcast([128, 8, 16, 16]), ALU.add, ['p_c16'], ['p_ssb'])
                for h in range(8):
                    k.P.op('dve', lambda e, h=h: e.max(out=top[:, h, 0:8], in_=cand[:, h, :]), ['p_ssb'], ['p_top'])
                    k.P.op('dve', lambda e, h=h: e.max_index(out=pos[:, h, 0:8], in_max=top[:, h, 0:8], in_values=cand[:, h, :]), ['p_ssb', 'p_top'], ['p_pos'])
                    k.P.op('dve', lambda e, h=h: e.match_replace(out=cand2[:, h, :], in_to_replace=top[:, h, 0:8], in_values=cand[:, h, :], imm_value=NEG),
                           ['p_ssb', 'p_top'], ['p_s2'])
                    k.P.op('dve', lambda e, h=h: e.max(out=top[:, h, 8:16], in_=cand2[:, h, :]), ['p_s2'], ['p_top'])
                    k.P.op('dve', lambda e, h=h: e.max_index(out=pos[:, h, 8:16], in_max=top[:, h, 8:16], in_values=cand2[:, h, :]), ['p_s2', 'p_top'], ['p_pos'])
                k.cp(posf[:], pos[:], ['p_pos'], ['p_posf'])
                k.tt(w4[:], posf[:].unsqueeze(3).to_broadcast([128, 8, 16, 16]),
                     thr[:].unsqueeze(1).unsqueeze(1).to_broadcast([128, 8, 16, 16]), ALU.is_ge, ['p_posf', 'p_thr'], ['p_w4'])
                k.P.op('dve', lambda e: e.tensor_reduce(out=ak[:], in_=w4[:], axis=AX.X, op=ALU.add), ['p_w4'], ['p_ak'])
                k.stt(bk[:], ak[:], -16.0, posf[:], ALU.mult, ALU.add, ['p_ak', 'p_posf'], ['p_bk'])
                for (sel, side, dsti) in ((ak, 0, 0), (bk, 1, 1)):
                    k.tt(w4[:], sel[:].unsqueeze(3).to_broadcast([128, 8, 16, 16]),
                         iota16[:].unsqueeze(1).unsqueeze(1).to_broadcast([128, 8, 16, 16]), ALU.is_equal, ['p_ak', 'p_bk', 'p_iota16'], ['p_w4'])
                    k.tt(w4[:], w4[:], i4[:, :, side, :].unsqueeze(2).to_broadcast([128, 8, 16, 16]), ALU.mult, ['p_w4', 'p_ixf'], ['p_w4'])
                    k.P.op('dve', lambda e, dsti=dsti: e.tensor_reduce(out=ijg[:, dsti, :].rearrange("p (h a) -> p h a", h=8), in_=w4[:], axis=AX.X, op=ALU.add),
                           ['p_w4'], ['p_ijg'])
                g3 = ijg[:, 2, :].rearrange("p (h a) -> p h a", h=8)
                k.tt(g3, top[:], top[:, :, 0:1].to_broadcast([128, 8, 16]), ALU.subtract, ['p_top'], ['p_ijg'])
                k.act(g3, g3, AF.Exp, ['p_ijg'], ['p_ijg'])
                k.P.op('dve', lambda e: e.tensor_reduce(out=zs[:], in_=g3, axis=AX.X, op=ALU.add), ['p_ijg'], ['p_zs'])
                k.P.op('dve', lambda e: e.reciprocal(zs[:], zs[:]), ['p_zs'], ['p_zs'])
                k.tt(g3, g3, zs[:].unsqueeze(2).to_broadcast([128, 8, 16]), ALU.mult, ['p_ijg', 'p_zs'], ['p_ijg'])
                for q in range(3):
                    k.tr(pb[6][:, q * 128:(q + 1) * 128], ijg[:, q, :], k.ident[:], ['p_ijg', 'ident'], [PB[6]], sig=(q == 2))
                k.cp(ijgT[:, :, tl * 128:(tl + 1) * 128], pb[6][:, 0:384].rearrange("p (q t) -> p q t", q=3), [PB[6]], ['p_ijgT'])
            for sb_ in range(PIECE // 32):
                for t in range(32):
                    tg = sb_ * 32 + t
                    k.ts(P1[:, t, :], iotaF[:], ijgT[:, 0, tg:tg + 1], ijgT[:, 2, tg:tg + 1], ALU.is_equal, ALU.mult,
                         ['p_iotaF', 'p_ijgT'], ['p_P1'])
                    k.ts(P2[:, t, :], iotaF[:], ijgT[:, 1, tg:tg + 1], None, ALU.is_equal, None, ['p_iotaF', 'p_ijgT'], ['p_P2'])
                for tq in range(8):
                    b = pb[4 + tq % 2]; bn = PB[4 + tq % 2]
                    for u in range(4):
                        t = tq * 4 + u
                        k.mm(b[:, u * 128:(u + 1) * 128], P2[:, t, :], P1[:, t, :], True, True, ['p_P1', 'p_P2'], [bn], sig=(u == 3))
                    tb = sb_ * 32 + tq * 4
                    k.cp(GA[:, :, tb:tb + 4].rearrange("p i t -> p t i"), b[:, :].rearrange("p (t i) -> p t i", t=4), [bn], ['p_GA'])
            for i in range(128):
                w, wn = wtile(uv, i * 128, pun)
                a = pb[i % 2]
                for kc in range(32):
                    k.mm(a[:, 0:PIECE], w[:, kc, :], h2[:, kc, :], kc == 0, kc == 31, [wn, 'p_h2'], [PB[i % 2]])
                g = gl[i % 2]; gn = 'p_gl%d' % (i % 2)
                k.act(g[:], a[:, 0:PIECE], AF.Gelu, [PB[i % 2]], [gn])
                k.tt(GA[:, i, :], GA[:, i, :], g[:], ALU.mult, ['p_GA', gn], ['p_GA'])
            pbT = pb[6]
            for (dc0, ndc) in ((0, 12), (12, 12), (24, 8)):
                ncol = ndc * 128
                for i in range(128):
                    v = vt[i % 4]; vn = 'p_vt%d' % (i % 4)
                    k.dma(v[:, 0:ncol], vv[:, i, dc0 * 128:dc0 * 128 + ncol], pvn, [vn])
                    for dc in range(ndc):
                        acc = pb[dc // 2][:, (dc % 2) * 256:(dc % 2 + 1) * 256]
                        k.mm(acc, v[:, dc * 128:(dc + 1) * 128], GA[:, i, :], (i == 0 and dc % 2 == 0), i == 127,
                             [vn, 'p_GA'], [PB[dc // 2]], sig=(i == 127 or dc == ndc - 1))
                for dq in range(ndc // 4):
                    for tl in range(PIECE // 128):
                        x = xo[0]; xn = 'p_xo0'
                        tcur = t0 + tl * 128
                        col0 = dc0 * 128 + dq * 512
                        k.dma(x[:], xmid[tcur:tcur + 128, col0:col0 + 512], [], [xn])
                        for u in range(4):
                            dc = dq * 4 + u
                            dglob = dc0 + dc
                            y = ys[u % 2]; yn = 'p_ys%d' % (u % 2)
                            k.ts(y[:, 0:128], pb[dc // 2][:, (dc % 2) * 256 + tl * 128:(dc % 2) * 256 + (tl + 1) * 128],
                                 mcol[:, dglob, r:r + 1], None, ALU.mult, None, [PB[dc // 2], 'p_mcol'], [yn])
                            k.P.op('pe', lambda e, y=y, u=u: e.transpose(pbT[:, u * 128:(u + 1) * 128], y[:, 0:128], k.ident[:]),
                                   [yn, 'ident'], [PB[6]], sig=True)
                        k.tt(x[:], x[:], pbT[:, 0:512], ALU.add, [xn, PB[6]], [xn])
                        dst = xout(tcur)
                        if dst is not None:
                            k.dma(dst[:, col0:col0 + 512], x[:], [xn], [('xout', tcur, col0)])
    P.barrier()


def build_program(n_layers=2, debug=None, stop_after=None, peer_pieces=None, only=None, mla_stop=None, force_not_last=False, layer_list=None):
    nc = bass.Bass("TRN2", target_bir_lowering=False)

    def din(name, shape, dt=F32):
        return nc.dram_tensor(name, list(shape), dt, kind="ExternalInput").ap()

    def dscr(name, shape, dt, out=False):
        if out:
            return nc.dram_tensor(name, list(shape), dt, kind="ExternalOutput").ap()
        return nc.dram_tensor(name, list(shape), dt).ap()

    debug = debug or ()
    shapes = {'xcat': [TT, D], 'cc': [64, 128], 'lbl': [64, 128], 'onorm_a': [1, 128], 'wuq': [768, 3072],
              'wukv': [512, 4096], 'mla_g': [14, 128], 'dgains': [4, 128], 'clam': [1, 512], 'convw': [4, 2048],
              'convb': [1, 2048], 'wgate': [2, 2, 16, 128, 128], 'bgate': [64, 128], 'dlam': [32, 128]}
    for l in range(n_layers):
        shapes.update({'wmod%d' % l: [D, 6 * D], 'bmod%d' % l: [192, 128], 'g1_%d' % l: [32, 128], 'g2_%d' % l: [32, 128],
                       'win%d' % l: [D, 11584 if l % 2 == 0 else 10240], 'wout%d' % l: [D, D], 'pwq%d' % l: [D, 2048],
                       'psk%d' % l: [128, 16, 128], 'puT%d' % l: [D, 16384], 'pv%d' % l: [16384, D]})

    class LazyIn(dict):
        def __missing__(self, key):
            self[key] = din(key, shapes[key])
            return self[key]

    I = LazyIn()
    y = nc.dram_tensor('y', [TL, D], F32, kind="ExternalOutput").ap()
    win_b = dscr('win_b', [128, 46, 32, 256], BF16)
    PT = dscr('PT', [75 * 128, TT], F32, out=('PT' in debug))
    Vh = dscr('Vh', [TT, 2048], BF16, out=('Vh' in debug))
    OF = dscr('OF', [2048, TT], F32)
    OB = dscr('OB', [2048, TT], F32)
    GT = dscr('GT', [D, TT], BF16, out=('GT' in debug))
    QT = dscr('QT', [16 * 192, TT], BF16)
    KT = dscr('KT', [16 * 128, TT], BF16)
    KrT = dscr('KrT', [64, TT], BF16)
    VT = dscr('VT', [TT, 2048], BF16)
    wout_b = dscr('wout_b', [D, D], BF16)
    xmid = dscr('xmid', [TT, D], F32, out=('xmid' in debug))
    H2T = dscr('H2T', [D, TT], BF16, out=('H2T' in debug))
    wq_b = dscr('wq_b', [128, 16, 32, 128], BF16)
    puT_b = dscr('puT_b', [128, 128, 32, 128], BF16)
    pv_b = dscr('pv_b', [16384, D], BF16)
    xnext = dscr('xnext', [TT, D], F32, out=('xnext' in debug))
    with ExitStack() as st:
        k = K(nc, st)
        block = st.enter_context(nc.Block())
        k.consts(st)
        modT = k.sb(st, 'modT', [128, 192, 2], F32)
        k.P.barrier()

        def run():
            if only == 'mla':
                emit_mla(k, PT, I['wuq'], I['wukv'], I['mla_g'], GT, QT, KT, KrT, VT, mla_stop)
                return
            xin = I['xcat']
            for l in (layer_list if layer_list is not None else range(n_layers)):
                last = (l == n_layers - 1) and not force_not_last
                emit_mod(k, st, I['cc'], I['wmod%d' % l], I['bmod%d' % l], 6, modT)
                if stop_after == 'mod':
                    return
                NFM_l = 9536 if l % 2 == 0 else 8192
                k.castnames = {}
                k.castnames['win'] = emit_cast_tiled(k, win_b, I['win%d' % l], [(0, NFM_l, 0), (NFM_l, 2048, (NFM_l + 255) // 256)], ('win_b', l))
                k.castnames['wout'] = emit_cast_weight(k, wout_b, I['wout%d' % l], 512, ('wout_b', l))
                k.castnames['wq'] = emit_cast_tiled(k, wq_b, I['pwq%d' % l], [(0, 2048, 0)], ('wq_b', l), tw=128)
                k.castnames['puT'] = emit_cast_tiled(k, puT_b, I['puT%d' % l], [(0, 16384, 0)], ('puT_b', l), tw=128)
                k.castnames['pv'] = emit_cast_weight(k, pv_b, I['pv%d' % l], 512, ('pv_b', l))
                if l % 2 == 0:
                    emit_inproj(k, xin, I['g1_%d' % l], modT, I['win%d' % l], 9536, 2048, PT, Vh, win_b)
                    if stop_after == 'inproj':
                        return
                    emit_hgrn2(k, PT, Vh, I['lbl'], I['onorm_a'], OF, OB, GT)
                    if stop_after == 'hgrn2':
                        return
                    emit_mla(k, PT, I['wuq'], I['wukv'], I['mla_g'], GT, QT, KT, KrT, VT)
                    if stop_after == 'mla':
                        return
                else:
                    emit_inproj(k, xin, I['g1_%d' % l], modT, I['win%d' % l], 8192, 2048, PT, Vh, win_b)
                    if stop_after == 'inproj':
                        return
                    lam_init = 0.8 - 0.6 * math.exp(-0.3 * l)
                    emit_diffattn(k, PT, Vh, I['dgains'], I['clam'], lam_init, GT)
                    if stop_after == 'diff':
                        return
                    emit_rglru(k, PT, I['convw'], I['convb'], I['wgate'], I['bgate'], I['dlam'], GT)
                    if stop_after == 'rglru':
                        return
                emit_outproj(k, xin, GT, I['wout%d' % l], wout_b, modT, 2, xmid)
                if stop_after == 'outproj':
                    return
                emit_norm2(k, xmid, I['g2_%d' % l], modT, 3, 4, H2T)
                if stop_after == 'norm2':
                    return
                if last:
                    pieces = [TC + 256 * i for i in range(16)]
                    xo = lambda t0: y[t0 - TC:t0 - TC + 128, :]
                else:
                    pieces = [256 * i for i in range(17)]
                    xo = lambda t0: xnext[t0:t0 + 128, :]
                if peer_pieces is not None:
                    pieces = peer_pieces
                emit_peer(k, xmid, H2T, I['pwq%d' % l], wq_b, I['psk%d' % l], I['puT%d' % l], puT_b, I['pv%d' % l], pv_b,
                          modT, 5, xo, pieces)
                xin = xnext

        run()
        k.P.barrier(full=True)
        print('nops', k.P.nops, {e: len(v) for e, v in k.P.streams.items()})
        k.P.emit(block)
    return nc


def prep_inputs(inp, b, n_layers=2):
    m = {}
    m['xcat'] = np.ascontiguousarray(np.concatenate([inp['ctx'][b], inp['x'][b]], axis=0))
    m['cc'] = np.ascontiguousarray(np.concatenate([inp['c'][b].reshape(32, 128), inp['c_ctx'].reshape(32, 128)], 0))
    for l in range(n_layers):
        m['wmod%d' % l] = np.ascontiguousarray(inp['w_mod'][l])
        m['bmod%d' % l] = np.ascontiguousarray(inp['b_mod'][l].reshape(192, 128))
        m['g1_%d' % l] = np.ascontiguousarray(inp['norm1_g'][l].reshape(32, 128))
        m['g2_%d' % l] = np.ascontiguousarray(inp['norm2_g'][l].reshape(32, 128))
        if l % 2 == 0:
            w = inp['e_w_in'][l // 2]
            cols = np.concatenate([np.arange(0, 2048), np.arange(2048, 4096), np.arange(4096, 6144), np.arange(8192, 10240),
                                   np.arange(10240, 11584), np.arange(6144, 8192)])
            m['win%d' % l] = np.ascontiguousarray(w[:, cols])
            m['wout%d' % l] = np.ascontiguousarray(inp['e_w_out'][l // 2])
        else:
            w = inp['o_w_in'][l // 2]
            cols = np.concatenate([np.arange(0, 4096), np.arange(6144, 10240), np.arange(4096, 6144)])
            m['win%d' % l] = np.ascontiguousarray(w[:, cols])
            m['wout%d' % l] = np.ascontiguousarray(inp['o_w_out'][l // 2])
        m['pwq%d' % l] = np.ascontiguousarray(inp['p_w_q'][l])
        m['psk%d' % l] = np.ascontiguousarray(inp['p_subkeys'][l].reshape(16, 128, 128).transpose(2, 0, 1))
        m['puT%d' % l] = np.ascontiguousarray(inp['p_u'][l].T)
        m['pv%d' % l] = np.ascontiguousarray(inp['p_v'][l])
    m['lbl'] = np.ascontiguousarray(inp['a_lb_logits'].reshape(64, 128))
    m['onorm_a'] = np.ascontiguousarray(inp['a_onorm_g'][0].reshape(1, 128))
    m['wuq'] = np.ascontiguousarray(inp['b_w_uq'][0])
    m['wukv'] = np.ascontiguousarray(inp['b_w_ukv'][0])
    pad = lambda v: np.concatenate([v, np.zeros(128 - v.shape[0], np.float32)])
    m['mla_g'] = np.ascontiguousarray(np.stack(
        [inp['b_cq_g'][0][i * 128:(i + 1) * 128] for i in range(6)] + [inp['b_ckv_g'][0][i * 128:(i + 1) * 128] for i in range(4)] +
        [inp['b_qn_g'][0], inp['b_kn_g'][0], pad(inp['b_qr_g'][0]), pad(inp['b_kr_g'][0])]).astype(np.float32))
    if n_layers > 1:
        m['dgains'] = np.ascontiguousarray(np.stack([inp['c_qn_g'][0], inp['c_kn_g'][0], inp['c_onorm_g'][0][0:128],
                                                     inp['c_onorm_g'][0][128:256]]).astype(np.float32))
        m['clam'] = np.ascontiguousarray(inp['c_lam'][0].reshape(1, 512))
        m['convw'] = np.ascontiguousarray(inp['d_conv_w'][0])
        m['convb'] = np.ascontiguousarray(inp['d_conv_b'][0].reshape(1, 2048))
        m['wgate'] = np.ascontiguousarray(inp['d_w_gate'][0])
        m['bgate'] = np.ascontiguousarray(inp['d_b_gate'][0].reshape(64, 128))
        m['dlam'] = np.ascontiguousarray(inp['d_lambda'][0].reshape(32, 128))
    return m


def emit_diffattn(k, PT, Vh, dgains, clam, lam_init, GT):
    P = k.P
    scale = 128.0 ** -0.5
    with ExitStack() as s:
        gi = k.sb(s, 'd_gi', [4, 128], F32)
        gT = k.sb(s, 'd_gT', [128, 4], F32)
        lam_in = k.sb(s, 'd_lamin', [1, 512], F32)
        lam_w = k.sb(s, 'd_lamw', [1, 8], F32)
        ones1 = k.sb(s, 'd_ones1', [1, 128], F32)
        neglam = k.sb(s, 'd_neglam', [128, 1], F32)
        perm = k.sb(s, 'd_perm', [128, 128], F32)
        cosb = k.sb(s, 'd_cosb', [128, 4096], BF16)
        sinb = k.sb(s, 'd_sinb', [128, 4096], BF16)
        with ExitStack() as s1:
            cosT = k.sb(s1, 'd_cos', [128, 4096], F32)
            sinS = k.sb(s1, 'd_sin', [128, 4096], F32)
            with ExitStack() as s2:
                emit_rope_tables(k, s2, 128, cosT, sinS, perm, 'rc_')
            P.barrier()
            k.cp(cosb[:], cosT[:], [], ['d_cosb'])
            k.cp(sinb[:], sinS[:], [], ['d_sinb'])
            P.barrier()
        QK = [k.sb(s, 'd_QK%d' % i, [128, TT], BF16) for i in range(4)]
        Vsb = k.sb(s, 'd_Vsb', [128, 34, 256], BF16)
        kx = [k.sb(s, 'd_kx%d' % i, [128, 512], F32) for i in range(2)]
        kr2 = k.sb(s, 'd_kr2', [128, 512], F32)
        sq = [k.sb(s, 'd_sq%d' % i, [128, 512], BF16) for i in range(2)]
        rs = k.sb(s, 'd_rs', [128, 512], F32)
        pT = [k.sb(s, 'd_pT%d' % i, [128, 512], BF16) for i in range(3)]
        rinv = k.sb(s, 'd_rinv', [128, 512], F32)
        acc = [k.sb(s, 'd_acc%d' % i, [128, 512], F32) for i in range(2)]
        tmp = k.sb(s, 'd_tmp', [128, 512], F32)
        ob = [k.sb(s, 'd_ob%d' % i, [128, 512], BF16) for i in range(2)]
        sP = [k.ps(s, 'd_sP%d' % i, [128, 512], F32) for i in range(3)]
        oP = [k.ps(s, 'd_oP%d' % i, [128, 512], F32) for i in range(2)]
        rP = k.ps(s, 'd_rP', [128, 512], F32)
        pss = k.ps(s, 'd_pss', [128, 512], F32)
        pm = k.ps(s, 'd_pm', [128, 512], F32)
        k.dma(gi[:], dgains, [], ['d_gi'])
        k.tr(pm[:, 0:4], gi[:], k.ident[0:4, 0:4], ['d_gi', 'ident'], ['d_pm'])
        k.cp(gT[:], pm[:, 0:4], ['d_pm'], ['d_gT'])
        k.ts(gT[:, 2:4], gT[:, 2:4], 1.0 - lam_init, None, ALU.mult, None, ['d_gT'], ['d_gT'])
        k.dma(lam_in[:], clam, [], ['d_lamin'])
        k.tt(lam_in[:, 0:128], lam_in[:, 0:128], lam_in[:, 128:256], ALU.mult, ['d_lamin'], ['d_lamin'])
        k.tt(lam_in[:, 256:384], lam_in[:, 256:384], lam_in[:, 384:512], ALU.mult, ['d_lamin'], ['d_lamin'])
        k.P.op('dve', lambda e: e.tensor_reduce(out=lam_w[:, 0:1], in_=lam_in[:, 0:128], axis=AX.X, op=ALU.add), ['d_lamin'], ['d_lamw'])
        k.P.op('dve', lambda e: e.tensor_reduce(out=lam_w[:, 1:2], in_=lam_in[:, 256:384], axis=AX.X, op=ALU.add), ['d_lamin'], ['d_lamw'])
        k.act(lam_w[:, 0:2], lam_w[:, 0:2], AF.Exp, ['d_lamw'], ['d_lamw'])
        k.tt(lam_w[:, 2:3], lam_w[:, 1:2], lam_w[:, 0:1], ALU.subtract, ['d_lamw'], ['d_lamw'])
        k.ts(lam_w[:, 2:3], lam_w[:, 2:3], -lam_init, None, ALU.add, None, ['d_lamw'], ['d_lamw'])
        k.memset(ones1[:], 1.0, ['d_ones1'])
        k.mm(pm[:, 8:9], ones1[:], lam_w[:, 2:3], True, True, ['d_ones1', 'd_lamw', 'd_gT'], ['d_pm'])
        k.cp(neglam[:], pm[:, 8:9], ['d_pm'], ['d_neglam'])
        cnt = {'x': 0, 'o': 0}

        def rms1(src, gcol, n, dst, rows, srcn, dstn):
            nt = src.shape[1]
            k.act(sq[0][0:rows, 0:nt], src, AF.Square, [srcn], ['d_sq0'])
            k.mm(pss[0:rows, 0:nt], k.onesb[0:rows, 0:rows], sq[0][0:rows, 0:nt], True, True, ['onesb', 'd_sq0'], ['d_pss'])
            k.rstd_from_ssq(rs[0:rows, 0:nt], pss[0:rows, 0:nt], n, ['d_pss'], ['d_rs'])
            k.stt(dst, src, gcol, rs[0:rows, 0:nt], ALU.mult, ALU.mult, [srcn, 'd_gT', 'd_rs'], [dstn])

        def rope_fm(x32, xn, t0, nt, dst, dstn):
            if t0 < TC:
                k.cp(dst, x32, [xn], [dstn])
                return
            l0 = t0 - TC
            k.mm(pm[:, 0:nt], perm[:], x32, True, True, ['rc_perm', xn], ['d_pm'])
            k.tt(kr2[:, 0:nt], pm[:, 0:nt], sinb[:, l0:l0 + nt], ALU.mult, ['d_pm', 'd_sinb'], ['d_kr2'])
            k.tt(x32, x32, cosb[:, l0:l0 + nt], ALU.mult, [xn, 'd_cosb'], [xn])
            k.tt(dst, x32, kr2[:, 0:nt], ALU.add, [xn, 'd_kr2'], [dstn])

        for hh in range(8):
            for qi in range(4):
                ch = (0 if qi < 2 else 16) + hh * 2 + (qi % 2)
                gcol = gT[:, 0:1] if qi < 2 else gT[:, 1:2]
                for (t0, nt) in BLK9:
                    x = kx[cnt['x'] % 2]; xn = 'd_kx%d' % (cnt['x'] % 2); cnt['x'] += 1
                    k.dma(x[:, 0:nt], PT[ch * 128:(ch + 1) * 128, t0:t0 + nt], [], [xn])
                    rms1(x[:, 0:nt], gcol, 128, x[:, 0:nt], 128, xn, xn)
                    rope_fm(x[:, 0:nt], xn, t0, nt, QK[qi][:, t0:t0 + nt], 'd_QK%d' % qi)
            k.dma(Vsb[:], Vh[:, hh * 256:(hh + 1) * 256].rearrange("(t p) v -> p t v", p=128), [], ['d_V'])

            def fin(q0, nq, mi, hh=hh):
                k.P.op('dve', lambda e: e.reciprocal(rinv[:, 0:nq], rP[:, 0:nq]), ['a_rP'], ['d_rinv'])
                for dvc in range(2):
                    if mi == 0:
                        k.tt(acc[dvc][:, 0:nq], oP[dvc][:, 0:nq], rinv[:, 0:nq], ALU.mult, ['a_oP%d' % dvc, 'd_rinv'], ['d_acc%d' % dvc])
                    else:
                        k.tt(tmp[:, 0:nq], oP[dvc][:, 0:nq], rinv[:, 0:nq], ALU.mult, ['a_oP%d' % dvc, 'd_rinv'], ['d_tmp'])
                        k.stt(acc[dvc][:, 0:nq], tmp[:, 0:nq], neglam[:, 0:1], acc[dvc][:, 0:nq], ALU.mult, ALU.add,
                              ['d_tmp', 'd_neglam', 'd_acc%d' % dvc], ['d_acc%d' % dvc])
                if mi == 1:
                    for dvc in range(2):
                        k.act(sq[dvc][:, 0:nq], acc[dvc][:, 0:nq], AF.Square, ['d_acc%d' % dvc], ['d_sq%d' % dvc])
                        k.mm(pss[:, 0:nq], k.onesb[:], sq[dvc][:, 0:nq], dvc == 0, dvc == 1, ['onesb', 'd_sq%d' % dvc], ['d_pss'], sig=True)
                    k.rstd_from_ssq(rs[:, 0:nq], pss[:, 0:nq], 256, ['d_pss'], ['d_rs'])
                    for dvc in range(2):
                        o = ob[cnt['o'] % 2]; on = 'd_ob%d' % (cnt['o'] % 2); cnt['o'] += 1
                        k.stt(o[:, 0:nq], acc[dvc][:, 0:nq], gT[:, 2 + dvc:3 + dvc], rs[:, 0:nq], ALU.mult, ALU.mult,
                              ['d_acc%d' % dvc, 'd_gT', 'd_rs'], [on])
                        k.dma(GT[hh * 256 + dvc * 128:hh * 256 + (dvc + 1) * 128, q0:q0 + nq], o[:, 0:nq], [on], [('GTd', hh, dvc, q0)])

            maps = [[(QK[2], QK[0], 128, 'd_QK2', 'd_QK0')], [(QK[3], QK[1], 128, 'd_QK3', 'd_QK1')]]
            emit_attn_core(k, maps, Vsb, 2, scale, 'd_V', sP, oP, rP, pT, fin)
    P.barrier()


def emit_rglru(k, PT, convw, convb, wgate, bgate, dlam, GT):
    P = k.P
    with ExitStack() as s:
        cin = k.sb(s, 'g_cin', [80, 128], F32)
        cT = k.sb(s, 'g_cT', [128, 80], F32)
        bin_ = k.sb(s, 'g_bin', [96, 128], F32)
        bT = k.sb(s, 'g_bT', [128, 96], F32)
        sp8 = k.sb(s, 'g_sp8', [128, 32], F32)
        wg = k.sb(s, 'g_wg', [128, 4, 128], BF16)
        x32 = k.sb(s, 'g_x32', [128, TT], F32)
        xc = k.sb(s, 'g_xc', [128, TT], F32)
        xcb = k.sb(s, 'g_xcb', [128, TT], BF16)
        rg = k.sb(s, 'g_rg', [128, TT], F32)
        ig = k.sb(s, 'g_ig', [128, TT], F32)
        t1 = k.sb(s, 'g_t1', [128, TT], F32)
        hf = k.sb(s, 'g_hf', [128, TT], F32)
        hb = k.sb(s, 'g_hb', [128, TT], F32)
        ob = k.sb(s, 'g_ob', [128, TT], BF16)
        pz = [k.ps(s, 'g_pz%d' % i, [128, 512], F32) for i in range(4)]
        pm = k.ps(s, 'g_pm', [128, 128], F32)
        k.dma(cin[0:64, :], convw.rearrange("j (b c) -> (j b) c", c=128), [], ['g_cin'])
        k.dma(cin[64:80, :], convb.rearrange("o (b c) -> (o b) c", c=128), [], ['g_cin2'])
        k.tr(pm[:, 0:80], cin[:], k.ident[0:80, 0:80], ['g_cin', 'g_cin2', 'ident'], ['g_pm'])
        k.cp(cT[:], pm[:, 0:80], ['g_pm'], ['g_cT'])
        k.dma(bin_[0:64, :], bgate, [], ['g_bin'])
        k.dma(bin_[64:96, :], dlam, [], ['g_bin2'])
        k.tr(pm[:, 0:96], bin_[:], k.ident[0:96, 0:96], ['g_bin', 'g_bin2', 'ident', 'g_cT'], ['g_pm'])
        k.cp(bT[:], pm[:, 0:96], ['g_pm'], ['g_bT'])
        k.act(sp8[:], bT[:, 64:96], AF.Exp, ['g_bT'], ['g_sp8'], scale=-1.0)
        k.act(sp8[:], sp8[:], AF.Ln, ['g_sp8'], ['g_sp8'], bias=1.0)
        k.ts(sp8[:], sp8[:], -8.0, None, ALU.mult, None, ['g_sp8'], ['g_sp8'])
        SEG = [(0, TC), (TC, TT)]
        for bk in range(16):
            k.dma(x32[:], PT[(48 + bk) * 128:(49 + bk) * 128, :], [], ['g_x32'])
            for d in range(2):
                for g in range(2):
                    k.dma(wg[:, d * 2 + g, :], wgate[d, g, bk], [], [('g_wg', d, g)], eng='pool')
            for (a, b) in SEG:
                k.ts(xc[:, a:b], x32[:, a:b], cT[:, 2 * 16 + bk:2 * 16 + bk + 1], cT[:, 64 + bk:64 + bk + 1], ALU.mult, ALU.add,
                     ['g_x32', 'g_cT'], ['g_xc'])
                k.stt(xc[:, a + 2:b], x32[:, a:b - 2], cT[:, 0 * 16 + bk:0 * 16 + bk + 1], xc[:, a + 2:b], ALU.mult, ALU.add,
                      ['g_x32', 'g_cT', 'g_xc'], ['g_xc'])
                k.stt(xc[:, a + 1:b], x32[:, a:b - 1], cT[:, 1 * 16 + bk:1 * 16 + bk + 1], xc[:, a + 1:b], ALU.mult, ALU.add,
                      ['g_x32', 'g_cT', 'g_xc'], ['g_xc'])
                k.stt(xc[:, a:b - 1], x32[:, a + 1:b], cT[:, 3 * 16 + bk:3 * 16 + bk + 1], xc[:, a:b - 1], ALU.mult, ALU.add,
                      ['g_x32', 'g_cT', 'g_xc'], ['g_xc'])
            k.cp(xcb[:], xc[:], ['g_xc'], ['g_xcb'], eng='act')
            for d in range(2):
                for g in range(2):
                    dst = rg if g == 0 else ig
                    dn = 'g_rg' if g == 0 else 'g_ig'
                    bcol = bT[:, (d * 2 + g) * 16 + bk:(d * 2 + g) * 16 + bk + 1]
                    for bi_, (t0, nt) in enumerate(BLK9):
                        pzz = pz[bi_ % 4]; pn = 'g_pz%d' % (bi_ % 4)
                        k.mm(pzz[:, 0:nt], wg[:, d * 2 + g, :], xcb[:, t0:t0 + nt], True, True, [('g_wg', d, g), 'g_xcb'], [pn])
                        k.act(dst[:, t0:t0 + nt], pzz[:, 0:nt], AF.Sigmoid, [pn, 'g_bT'], [dn], bias=bcol)
                k.act(rg[:], rg[:], AF.Exp, ['g_rg', 'g_sp8'], ['g_rg'], scale=sp8[:, d * 16 + bk:d * 16 + bk + 1])
                k.tt(t1[:], rg[:], rg[:], ALU.mult, ['g_rg'], ['g_t1'])
                k.ts(t1[:], t1[:], -1.0, 1.0, ALU.mult, ALU.add, ['g_t1'], ['g_t1'])
                k.act(t1[:], t1[:], AF.Sqrt, ['g_t1'], ['g_t1'])
                k.tt(ig[:], ig[:], xc[:], ALU.mult, ['g_ig', 'g_xc'], ['g_ig'])
                k.tt(ig[:], ig[:], t1[:], ALU.mult, ['g_ig', 'g_t1'], ['g_ig'])
                if d == 0:
                    k.P.op('dve', lambda e: e.tensor_tensor_scan(hf[:], rg[:], ig[:], 0.0, ALU.mult, ALU.add), ['g_rg', 'g_ig'], ['g_hf'])
                else:
                    k.P.op('dve', lambda e: e.tensor_tensor_scan(hb[:, 0:TC][:, ::-1], rg[:, 0:TC][:, ::-1], ig[:, 0:TC][:, ::-1], 0.0, ALU.mult, ALU.add),
                           ['g_rg', 'g_ig'], ['g_hb'])
                    k.P.op('dve', lambda e: e.tensor_tensor_scan(hb[:, TC:TT][:, ::-1], rg[:, TC:TT][:, ::-1], ig[:, TC:TT][:, ::-1], hb[:, 0:1], ALU.mult, ALU.add),
                           ['g_rg', 'g_ig', 'g_hb'], ['g_hb'])
            k.dma(x32[:], PT[(32 + bk) * 128:(33 + bk) * 128, :], ['g_xc'], ['g_x32'])
            k.act(x32[:], x32[:], AF.Gelu, ['g_x32'], ['g_x32'])
            k.tt(hf[:], hf[:], hb[:], ALU.add, ['g_hf', 'g_hb'], ['g_hf'])
            k.tt(ob[:], hf[:], x32[:], ALU.mult, ['g_hf', 'g_x32'], ['g_ob'])
            k.dma(GT[2048 + bk * 128:2048 + (bk + 1) * 128, :], ob[:], ['g_ob'], [('GTr', bk)])
    P.barrier()


_NC_CACHE = {}


def kernel(**inputs):
    inp = {k_: np.asarray(v) for k_, v in inputs.items()}
    B = inp['x'].shape[0]
    if 'nc' not in _NC_CACHE:
        _NC_CACHE['nc'] = build_program(n_layers=2)
    nc = _NC_CACHE['nc']
    in_maps = [prep_inputs(inp, b, n_layers=2) for b in range(B)]
    res = run_bass_kernel_spmd(nc, in_maps, core_ids=list(range(B)))
    out = np.stack([np.asarray(res.results[b]['y'], dtype=np.float32) for b in range(B)], axis=0)
    return out
```

```python
import math
from contextlib import ExitStack
import numpy as np
import concourse.bass as bass
import concourse.mybir as mybir
from concourse.bass_utils import run_bass_kernel_spmd

F32 = mybir.dt.float32
BF16 = mybir.dt.bfloat16
U32 = mybir.dt.uint32
I32 = mybir.dt.int32
AF = mybir.ActivationFunctionType
ALU = mybir.AluOpType
AX = mybir.AxisListType

ENGS = ('pe', 'dve', 'act', 'pool', 'sp')
D = 4096
TC = 256
TL = 4096
TT = TC + TL
EPS = 1e-6
NCH = TT // 64


class Prog:
    def __init__(self, nc, stack, n_dma_sems=(('sp', 16), ('pool', 8), ('act', 8))):
        self.nc = nc
        self.streams = {e: [] for e in ENGS}
        self.sems = {}
        self.cnt = {}
        for e in ENGS:
            self.sems[('eng', e)] = stack.enter_context(nc.semaphore('s_' + e))
            self.cnt[('eng', e)] = 0
        self.dma_pool = {}
        self.dma_next = {}
        for e, n in n_dma_sems:
            ids = []
            for i in range(n):
                k = ('dma', e, i)
                self.sems[k] = stack.enter_context(nc.semaphore('d_%s%d' % (e, i)))
                self.cnt[k] = 0
                ids.append(k)
            self.dma_pool[e] = ids
            self.dma_next[e] = 0
        self.known = {e: {} for e in ENGS}
        self.last_w = {}
        self.readers = {}
        self.nops = 0

    def _need(self, eng, ev, waits):
        if ev is None:
            return
        k, v = ev
        if k == ('eng', eng) and eng == 'pe':
            return
        if self.known[eng].get(k, 0) >= v:
            return
        if waits.get(k, 0) < v:
            waits[k] = v

    def _deps(self, eng, reads, writes):
        waits = {}
        for r in reads:
            self._need(eng, self.last_w.get(r), waits)
        for w in writes:
            self._need(eng, self.last_w.get(w), waits)
            for ev in self.readers.get(w, ()):
                self._need(eng, ev, waits)
        for k, v in waits.items():
            self.known[eng][k] = v
        return [(self.sems[k], v) for k, v in waits.items()]

    def _record(self, ev, reads, writes):
        for r in reads:
            self.readers.setdefault(r, []).append(ev)
        for w in writes:
            self.last_w[w] = ev
            self.readers[w] = []

    def op(self, eng, fn, reads=(), writes=(), sig=True):
        reads = tuple(reads)
        writes = tuple(writes)
        assert sig or eng == 'pe'
        waits = self._deps(eng, reads, writes)
        k = ('eng', eng)
        if sig:
            self.cnt[k] += 1
            ev = (k, self.cnt[k])
        else:
            ev = (k, self.cnt[k] + 1)
        self._record(ev, reads, writes)
        self.streams[eng].append((waits, fn, self.sems[k] if sig else None, 1))
        self.nops += 1

    def dma(self, eng, out, in_, reads=(), writes=(), **kw):
        reads = tuple(reads)
        writes = tuple(writes)
        pool = self.dma_pool[eng]
        k = pool[self.dma_next[eng] % len(pool)]
        self.dma_next[eng] += 1
        waits = self._deps(eng, reads, writes)
        prev = self.cnt[k]
        if prev and self.known[eng].get(k, 0) < prev:
            self.known[eng][k] = prev
            waits.append((self.sems[k], prev))
        self.cnt[k] += 16
        ev = (k, self.cnt[k])
        self._record(ev, reads, writes)
        self.streams[eng].append((waits, (lambda e: e.dma_start(out=out, in_=in_, **kw)), self.sems[k], 16))
        self.nops += 1

    def barrier(self, full=False):
        def bg(k):
            return (not full) and k[0] == 'dma' and k[1] == 'pool'
        for e in ENGS:
            waits = {}
            for k, v in self.cnt.items():
                if v and not bg(k):
                    self._need(e, (k, v), waits)
            for k, v in waits.items():
                self.known[e][k] = v
            self.streams[e].append(([(self.sems[k], v) for k, v in waits.items()], None, None, 0))
        self.last_w = {r: ev for r, ev in self.last_w.items() if bg(ev[0])}
        self.readers = {}

    def emit(self, block):
        def run(engobj, items):
            for waits, fn, sem, inc in items:
                for s, v in waits:
                    engobj.wait_ge(s, v)
                if fn is None:
                    continue
                ins = fn(engobj)
                if sem is not None:
                    ins.then_inc(sem, inc)

        st = self.streams

        @block.tensor
        def _(e):
            run(e, st['pe'])

        @block.vector
        def _(e):
            run(e, st['dve'])

        @block.scalar
        def _(e):
            run(e, st['act'])

        @block.gpsimd
        def _(e):
            run(e, st['pool'])

        @block.sync
        def _(e):
            run(e, st['sp'])


class K:
    def __init__(self, nc, stack):
        self.nc = nc
        self.P = Prog(nc, stack)
        self.stack = stack
        self.uid = 0

    def sb(self, st, name, shape, dt):
        self.uid += 1
        return st.enter_context(self.nc.sbuf_tensor('%s_u%d' % (name, self.uid), list(shape), dt))

    def ps(self, st, name, shape, dt=F32):
        self.uid += 1
        return st.enter_context(self.nc.psum_tensor('%s_u%d' % (name, self.uid), list(shape), dt))

    def mm(self, out, lhsT, rhs, start, stop, r=(), w=(), sig=None):
        if sig is None:
            sig = stop
        self.P.op('pe', lambda e: e.matmul(out, lhsT=lhsT, rhs=rhs, start=start, stop=stop), r, w, sig=sig)

    def tr(self, out, in_, ident, r=(), w=(), sig=True):
        self.P.op('pe', lambda e: e.transpose(out, in_, ident), r, w, sig=sig)

    def act(self, out, in_, func, r=(), w=(), scale=1.0, bias=0.0, accum_out=None, eng='act'):
        kw = {}
        if accum_out is not None:
            kw['accum_out'] = accum_out
        self.P.op('act', lambda e: e.activation(out=out, in_=in_, func=func, scale=scale, bias=bias, **kw), r, w)

    def ts(self, out, in0, s1, s2, op0, op1=None, r=(), w=(), eng='dve'):
        if op1 is None:
            self.P.op(eng, lambda e: e.tensor_scalar(out, in0, s1, None, op0), r, w)
        else:
            self.P.op(eng, lambda e: e.tensor_scalar(out, in0, s1, s2, op0, op1), r, w)

    def tt(self, out, in0, in1, op, r=(), w=(), eng='dve'):
        self.P.op(eng, lambda e: e.tensor_tensor(out, in0, in1, op), r, w)

    def stt(self, out, in0, scalar, in1, op0, op1, r=(), w=()):
        self.P.op('dve', lambda e: e.scalar_tensor_tensor(out, in0, scalar, in1, op0, op1), r, w)

    def cp(self, out, in_, r=(), w=(), eng='dve'):
        if eng == 'act':
            self.P.op('act', lambda e: e.copy(out, in_), r, w)
        else:
            self.P.op(eng, lambda e: e.tensor_copy(out, in_), r, w)

    def memset(self, ap, val, w=(), eng='dve'):
        self.P.op(eng, lambda e: e.memset(ap, val), (), w)

    def dma(self, out, in_, r=(), w=(), eng='sp'):
        self.P.dma(eng, out, in_, r, w)

    def consts(self, st):
        nc = self.nc
        self.iota_i = self.sb(st, 'iota_i', [128, 128], I32)
        self.ident = self.sb(st, 'ident', [128, 128], F32)
        self.identb = self.sb(st, 'identb', [128, 128], BF16)
        self.onesb = self.sb(st, 'onesb', [128, 128], BF16)
        self.epsc = self.sb(st, 'epsc', [128, 1], F32)
        self.P.op('pool', lambda e: e.iota(self.iota_i[:], pattern=[[1, 128]], base=0, channel_multiplier=-1),
                  (), ['iota_i'])
        self.P.op('dve', lambda e: e.tensor_single_scalar(self.ident[:], self.iota_i[:], 0, ALU.is_equal),
                  ['iota_i'], ['ident'])
        self.cp(self.identb[:], self.ident[:], ['ident'], ['identb'])
        self.memset(self.onesb[:], 1.0, ['onesb'])
        self.memset(self.epsc[:], EPS, ['epsc'])
        self.negshift = self.sb(st, 'negshift', [128, 1], F32)
        self.memset(self.negshift[:], -8.0, ['negshift'])

    def rstd_from_ssq(self, out, ssq, n, r=(), w=()):
        np_ = out.shape[0]
        self.act(out, ssq, AF.Ln, r, w, scale=1.0 / n, bias=self.epsc[0:np_, :])
        self.act(out, out, AF.Exp, w, w, scale=-0.5)


def emit_mod(k, st, cc, wmod, bmod, nm, modT):
    P = k.P
    nch = nm * 32
    with ExitStack() as s:
        cin = k.sb(s, 'm_cin', [64, 128], F32)
        bin_ = [k.sb(s, 'm_bin%d' % i, [96, 128], F32) for i in range(nch // 96)]
        condT = k.sb(s, 'm_condT', [128, 32, 2], BF16)
        biasT = k.sb(s, 'm_biasT', [128, nch], F32)
        wt = [k.sb(s, 'm_wt%d' % i, [128, 32, 512], BF16) for i in range(2)]
        pst = k.ps(s, 'm_pst', [128, 256], F32)
        psm = k.ps(s, 'm_psm', [128, nch, 2], F32)
        k.dma(cin[:], cc, [], ['m_cin'])
        k.tr(pst[:, 0:64], cin[:], k.ident[0:64, 0:64], ['m_cin', 'ident'], ['m_pst'])
        k.act(condT[:].rearrange("p k r -> p r k"), pst[:, 0:64].rearrange("p (r k) -> p r k", r=2), AF.Silu,
              ['m_pst'], ['m_condT'])
        for i in range(nch // 96):
            k.dma(bin_[i][:], bmod[i * 96:(i + 1) * 96, :], [], ['m_bin%d' % i])
            k.tr(pst[:, 0:96], bin_[i][:], k.ident[0:96, 0:96], ['m_bin%d' % i, 'ident', 'm_condT', 'm_biasT'], ['m_pst'])
            k.cp(biasT[:, i * 96:(i + 1) * 96], pst[:, 0:96], ['m_pst'], ['m_biasT'])
        wv = wmod.rearrange("(kc p) n -> p kc n", p=128)
        ntile = nm * 8
        for t in range(ntile):
            wb = wt[t % 2]
            rn = 'm_wt%d' % (t % 2)
            for half in range(2):
                k.dma(wb[:, half * 16:(half + 1) * 16, :], wv[:, half * 16:(half + 1) * 16, t * 512:(t + 1) * 512],
                      [], [rn + 'h%d' % half], eng='pool')
            for j in range(4):
                ch = t * 4 + j
                for kc in range(32):
                    k.mm(psm[:, ch, :], wb[:, kc, j * 128:(j + 1) * 128], condT[:, kc, :], kc == 0, kc == 31,
                         [rn + 'h%d' % (kc // 16), 'm_condT'], ['m_psm'])
        k.tt(modT[:], psm[:], biasT[:].unsqueeze(2).to_broadcast([128, nch, 2]), ALU.add,
             ['m_psm', 'm_biasT'], ['modT'])
    P.barrier()


TGS = [(0, 2, 1)] + [(256 + 512 * j, 4, 0) for j in range(8)]


def emit_norm_hT(k, xsrc, tok0, ntl, r, AB, xt, xn_base, junk, ssq, trp, hT, xcount, hname='hT'):
    for ti in range(ntl):
        xb = xt[xcount[0] % 2]
        xn = xn_base + str(xcount[0] % 2)
        xcount[0] += 1
        t0 = tok0 + ti * 128
        k.dma(xb[:], xsrc[t0:t0 + 128, :], [], [xn])
        k.act(junk[:], xb[:], AF.Square, [xn], ['n_junk', 'n_ssq'], accum_out=ssq[:, 0:1])
        k.rstd_from_ssq(ssq[:, 1:2], ssq[:, 0:1], D, ['n_ssq'], ['n_ssq'])
        k.ts(xb[:], xb[:], ssq[:, 1:2], None, ALU.mult, None, [xn, 'n_ssq'], [xn])
        for kb in range(4):
            tp = trp[kb % 2]
            tn = 'n_trp%d' % (kb % 2)
            for j in range(8):
                kc = kb * 8 + j
                k.tr(tp[:, j, :], xb[:, kc * 128:(kc + 1) * 128], k.ident[:], [xn, 'ident'], [tn], sig=(j == 7))
            for j in range(8):
                kc = kb * 8 + j
                k.ts(hT[:, kc, ti * 128:(ti + 1) * 128], tp[:, j, :], AB[:, 0, kc, r:r + 1], AB[:, 1, kc, r:r + 1],
                     ALU.mult, ALU.add, [tn, 'AB'], [hname])


def emit_AB(k, s, gsrc, modT, m_shift, m_scale, AB, trp):
    gin = k.sb(s, 'ab_gin', [32, 128], F32)
    gT = k.sb(s, 'ab_gT', [128, 32], F32)
    k.dma(gin[:], gsrc, [], ['ab_gin'])
    k.tr(trp[0][:, 0, 0:32], gin[:], k.ident[0:32, 0:32], ['ab_gin', 'ident'], ['n_trp0'])
    k.cp(gT[:], trp[0][:, 0, 0:32], ['n_trp0'], ['ab_gT'])
    k.ts(AB[:, 0], modT[:, m_scale * 32:(m_scale + 1) * 32, :], 1.0, None, ALU.add, None, ['modT'], ['AB'])
    k.tt(AB[:, 0], AB[:, 0], gT[:].unsqueeze(2).to_broadcast([128, 32, 2]), ALU.mult, ['AB', 'ab_gT'], ['AB'])
    k.cp(AB[:, 1], modT[:, m_shift * 32:(m_shift + 1) * 32, :], ['modT', 'AB'], ['AB'])


def emit_cast_weight(k, dst, src, nrows_blk, tag):
    R = src.shape[0]
    names = []
    for r0 in range(0, R, nrows_blk):
        nm = (tag, r0)
        k.dma(dst[r0:r0 + nrows_blk, :], src[r0:r0 + nrows_blk, :], [], [nm], eng='pool')
        names.append(nm)
    return names


def emit_cast_tiled(k, dst4, src, regions, tag, tw=256):
    names = []
    for kc in range(32):
        for (c0, ncol, t0) in regions:
            nm = (tag, kc, c0)
            nfull = ncol // tw
            for g0 in range(0, nfull, 32):
                g1 = min(nfull, g0 + 32)
                nmg = (tag, kc, c0, g0)
                k.dma(dst4[:, t0 + g0:t0 + g1, kc, :], src[kc * 128:(kc + 1) * 128, c0 + g0 * tw:c0 + g1 * tw].rearrange("p (t e) -> p t e", e=tw),
                      [], [nmg], eng='pool')
                names.append(nmg)
            rem = ncol - nfull * tw
            if rem:
                k.dma(dst4[:, t0 + nfull, kc, 0:rem], src[kc * 128:(kc + 1) * 128, c0 + nfull * tw:c0 + ncol], [], [(tag, kc, c0, 'r')], eng='pool')
                names.append((tag, kc, c0, 'r'))
    return names


def emit_inproj(k, xcat, g1, modT, win, NFM, NTM, PT, Vh, win_b):
    P = k.P
    chunks = [(c0, min(128, NFM - c0)) for c0 in range(0, NFM, 128)]
    with ExitStack() as s:
        wnames = k.castnames['win']
        AB = k.sb(s, 'i_AB', [128, 2, 32, 2], F32)
        xt = [k.sb(s, 'i_xt%d' % i, [128, D], F32) for i in range(2)]
        junk = k.sb(s, 'i_junk', [128, D], BF16)
        ssq = k.sb(s, 'i_ssq', [128, 4], F32)
        hTs = [k.sb(s, 'i_hT%d' % i, [128, 32, 512], BF16) for i in range(2)]
        wt = [k.sb(s, 'i_wt%d' % i, [128, 32, 256], BF16) for i in range(3)]
        ost = [k.sb(s, 'i_ost%d' % i, [128, 512], F32) for i in range(4)]
        ostb = [k.sb(s, 'i_ostb%d' % i, [128, 256], BF16) for i in range(2)]
        trp = [k.ps(s, 'i_trp%d' % i, [128, 8, 128], F32) for i in range(2)]
        acc = [k.ps(s, 'i_acc%d' % i, [128, 512], F32) for i in range(2)]
        emit_AB(k, s, g1, modT, 0, 1, AB, trp)
        xcount = [0]
        wi = oi = ai = bi = 0
        for gidx, (tok0, ntl, r) in enumerate(TGS):
            ntok = ntl * 128
            hT = hTs[gidx % 2]
            hname = 'hT%d' % (gidx % 2)
            emit_norm_hT(k, xcat, tok0, ntl, r, AB, xt, 'i_xt', junk, ssq, trp, hT, xcount, hname)
            tiles = [(c0, min(256, NFM - c0)) for c0 in range(0, NFM, 256)] + \
                    [(NFM + c0, min(256, NTM - c0)) for c0 in range(0, NTM, 256)]
            for tix, (c0, wd) in enumerate(tiles):
                wb = wt[wi % 3]
                wn = 'i_wt%d' % (wi % 3)
                wi += 1
                k.dma(wb[:], win_b[:, tix], wnames, [wn])
                if c0 < NFM:
                    for cc0 in range(c0, c0 + wd, 128):
                        c = cc0 // 128
                        cw = chunks[c][1]
                        off = cc0 - c0
                        a = acc[ai % 2]
                        an = 'i_acc%d' % (ai % 2)
                        ai += 1
                        for kc in range(32):
                            k.mm(a[0:cw, 0:ntok], wb[:, kc, off:off + cw], hT[:, kc, 0:ntok], kc == 0, kc == 31,
                                 [wn, hname], [an])
                        o = ost[oi % 4]
                        on = 'i_ost%d' % (oi % 4)
                        oi += 1
                        k.cp(o[0:cw, 0:ntok], a[0:cw, 0:ntok], [an], [on], eng='act')
                        k.dma(PT[c * 128:c * 128 + cw, tok0:tok0 + ntok], o[0:cw, 0:ntok], [on], [('PT', c, tok0)], eng='act')
                else:
                    tc0 = c0 - NFM
                    for ti in range(ntl):
                        a = acc[ai % 2]
                        an = 'i_acc%d' % (ai % 2)
                        ai += 1
                        for kc in range(32):
                            k.mm(a[:, 0:wd], hT[:, kc, ti * 128:(ti + 1) * 128], wb[:, kc, 0:wd], kc == 0, kc == 31,
                                 [wn, hname], [an])
                        o = ostb[bi % 2]
                        on = 'i_ostb%d' % (bi % 2)
                        bi += 1
                        k.cp(o[:, 0:wd], a[:, 0:wd], [an], [on], eng='act')
                        t0 = tok0 + ti * 128
                        k.dma(Vh[t0:t0 + 128, tc0:tc0 + wd], o[:, 0:wd], [on], [('Vh', t0, tc0)], eng='act')
    P.barrier()


def emit_hgrn2(k, PT, Vh, lbl, onorm, OF, OB, GT):
    P = k.P
    NBLK = 4
    BL = TT // NBLK
    CPB = BL // 64
    with ExitStack() as s:
        lin = k.sb(s, 'h_lin', [64, 128], F32)
        lT = k.sb(s, 'h_lT', [128, 64], F32)
        lb = k.sb(s, 'h_lb', [128, 2, 16], F32)
        oml = k.sb(s, 'h_oml', [128, 2, 16], F32)
        gin = k.sb(s, 'h_gin', [1, 128], F32)
        gcol = k.sb(s, 'h_gcol', [128, 1], F32)
        maskF = k.sb(s, 'h_maskF', [64, 64], F32)
        maskB = k.sb(s, 'h_maskB', [64, 64], F32)
        smask = k.sb(s, 'h_smask', [128, BL], F32)
        q32 = k.sb(s, 'h_q32', [128, BL], F32)
        f32 = k.sb(s, 'h_f32', [128, BL], F32)
        g32 = k.sb(s, 'h_g32', [128, BL], F32)
        b32 = k.sb(s, 'h_b32', [128, BL], F32)
        t32 = k.sb(s, 'h_t32', [128, BL], F32)
        Qt = [k.sb(s, 'h_Qt%d' % i, [128, TT], BF16) for i in range(2)]
        Kt = [k.sb(s, 'h_Kt%d' % i, [128, TT], BF16) for i in range(2)]
        Qh = [k.sb(s, 'h_Qh%d' % i, [128, TT], BF16) for i in range(2)]
        KhT = k.sb(s, 'h_KhT', [128, BL], BF16)
        Ktok = [k.sb(s, 'h_Ktok%d' % i, [64, NCH, 128], BF16) for i in range(2)]
        Vsb = k.sb(s, 'h_Vsb', [64, NCH, 128], BF16)
        decay = [k.sb(s, 'h_decay%d' % i, [128, NCH], F32) for i in range(2)]
        S32 = [k.sb(s, 'h_S32%d' % i, [128, 128], F32) for i in range(2)]
        Sb = [k.sb(s, 'h_Sb%d' % i, [128, 128], BF16) for i in range(2)]
        attm = [k.sb(s, 'h_attm%d' % i, [64, 64], BF16) for i in range(2)]
        ostg = [k.sb(s, 'h_ostg%d' % i, [128, 256], F32) for i in range(4)]
        ptr = k.ps(s, 'h_ptr', [64, 4, 128], BF16)
        pmisc = k.ps(s, 'h_pmisc', [128, 128], F32)
        attP = [k.ps(s, 'h_attP%d' % i, [64, 64], F32) for i in range(2)]
        oP = [k.ps(s, 'h_oP%d' % i, [128, 64], F32) for i in range(2)]
        kvP = [k.ps(s, 'h_kvP%d' % i, [128, 128], F32) for i in range(2)]
        k.dma(lin[:], lbl, [], ['h_lin'])
        k.tr(pmisc[:, 0:64], lin[:], k.ident[0:64, 0:64], ['h_lin', 'ident'], ['h_pmisc'])
        k.cp(lT[:], pmisc[:, 0:64], ['h_pmisc'], ['h_lT'])
        lv = lT[:].rearrange("p (d sl h) -> p d sl h", d=2, sl=2)
        k.tt(lb[:], lv[:, :, 0, :], lv[:, :, 1, :], ALU.subtract, ['h_lT'], ['h_lb'])
        k.act(lb[:], lb[:], AF.Sigmoid, ['h_lb'], ['h_lb'])
        k.ts(oml[:], lb[:], -1.0, 1.0, ALU.mult, ALU.add, ['h_lb'], ['h_oml'])
        k.dma(gin[:], onorm, [], ['h_gin'])
        k.tr(pmisc[:, 0:1], gin[:], k.ident[0:1, 0:1], ['h_gin', 'ident', 'h_lT'], ['h_pmisc'])
        k.cp(gcol[:], pmisc[:, 0:1], ['h_pmisc'], ['h_gcol'])
        k.P.op('dve', lambda e: e.tensor_single_scalar(maskF[:], k.iota_i[0:64, 0:64], 0, ALU.is_ge), ['iota_i'], ['h_maskF'])
        k.P.op('dve', lambda e: e.tensor_single_scalar(maskB[:], k.iota_i[0:64, 0:64], 0, ALU.is_le), ['iota_i'], ['h_maskB'])
        k.memset(smask[:], 1.0, ['h_smask'])
        k.memset(smask[:].rearrange("p (n c) -> p n c", c=64)[:, :, 0:1], 0.0, ['h_smask'])
        c_q = 128.0 ** -0.5
        for h in range(16):
            k.dma(Vsb[:], Vh[:, h * 128:(h + 1) * 128].rearrange("(n p) v -> p n v", p=64), [], ['h_Vsb'])
            for d in range(2):
                mid, last = (31, 63) if d == 0 else (32, 0)
                for blk in range(NBLK):
                    tsl = slice(blk * BL, (blk + 1) * BL)
                    k.dma(q32[:], PT[h * 128:(h + 1) * 128, tsl], [], ['h_q32'])
                    fc = (16 + h) if d == 0 else (32 + h)
                    k.dma(f32[:], PT[fc * 128:(fc + 1) * 128, tsl], [], ['h_f32'])
                    k.act(q32[:], q32[:], AF.Silu, ['h_q32'], ['h_q32'])
                    k.act(f32[:], f32[:], AF.Sigmoid, ['h_f32'], ['h_f32'])
                    k.ts(f32[:], f32[:], oml[:, d, h:h + 1], lb[:, d, h:h + 1], ALU.mult, ALU.add, ['h_f32', 'h_oml', 'h_lb'], ['h_f32'])
                    k.act(g32[:], f32[:], AF.Ln, ['h_f32'], ['h_g32'])
                    k.ts(f32[:], f32[:], -1.0, 1.0, ALU.mult, ALU.add, ['h_f32', 'h_g32'], ['h_f32'])
                    if d == 0:
                        k.P.op('dve', lambda e: e.tensor_tensor_scan(b32[:], smask[:], g32[:], 0.0, ALU.mult, ALU.add),
                               ['h_smask', 'h_g32'], ['h_b32'])
                    else:
                        k.P.op('dve', lambda e: e.tensor_tensor_scan(b32[:, ::-1], smask[:], g32[:, ::-1], 0.0, ALU.mult, ALU.add),
                               ['h_smask', 'h_g32'], ['h_b32'])
                    b3 = b32[:].rearrange("p (n c) -> p n c", c=64)
                    g3 = g32[:].rearrange("p (n c) -> p n c", c=64)
                    k.tt(g3, b3, b3[:, :, mid:mid + 1].to_broadcast([128, CPB, 64]), ALU.subtract, ['h_b32', 'h_g32'], ['h_g32'])
                    k.act(t32[:], g32[:], AF.Exp, ['h_g32'], ['h_t32'])
                    k.stt(Qt[d][:, tsl], q32[:], c_q, t32[:], ALU.mult, ALU.mult, ['h_q32', 'h_t32'], ['h_Qt%d' % d])
                    k.act(t32[:], g32[:], AF.Exp, ['h_g32', 'h_Qt%d' % d], ['h_t32'], scale=-1.0)
                    k.tt(Kt[d][:, tsl], f32[:], t32[:], ALU.mult, ['h_f32', 'h_t32'], ['h_Kt%d' % d])
                    k.act(t32[:], b32[:], AF.Exp, ['h_b32', 'h_Kt%d' % d], ['h_t32'])
                    k.stt(Qh[d][:, tsl], q32[:], c_q, t32[:], ALU.mult, ALU.mult, ['h_q32', 'h_t32'], ['h_Qh%d' % d])
                    k.tt(g3, b3[:, :, last:last + 1].to_broadcast([128, CPB, 64]), b3, ALU.subtract, ['h_b32', 'h_g32', 'h_Qh%d' % d], ['h_g32'])
                    k.act(t32[:], g32[:], AF.Exp, ['h_g32'], ['h_t32'])
                    k.tt(KhT[:], f32[:], t32[:], ALU.mult, ['h_f32', 'h_t32'], ['h_KhT'])
                    k.act(decay[d][:, blk * CPB:(blk + 1) * CPB], b3[:, :, last], AF.Exp, ['h_b32'], ['h_decay%d' % d])
                    for c in range(CPB):
                        n = blk * CPB + c
                        k.tr(ptr[:, c % 4, :], KhT[:, c * 64:(c + 1) * 64], k.identb[:], ['h_KhT', 'identb'], ['h_ptr'],
                             sig=(c % 4 == 3 or c == CPB - 1))
                        if c % 4 == 3 or c == CPB - 1:
                            n0 = n - (c % 4)
                            k.cp(Ktok[d][:, n0:n + 1, :], ptr[:, 0:(c % 4) + 1, :], ['h_ptr'], ['h_Ktok%d' % d])
            orders = [list(range(NCH)), [3, 2, 1, 0] + list(range(NCH - 1, 3, -1))]
            for idx in range(NCH):
                for d in range(2):
                    n = orders[d][idx]
                    mask = maskF if d == 0 else maskB
                    Odst = OF if d == 0 else OB
                    t0 = n * 64
                    aP = attP[d]
                    an = 'h_attP%d' % d
                    am = attm[d]
                    amn = 'h_attm%d' % d
                    k.mm(aP[:], Kt[d][:, t0:t0 + 64], Qt[d][:, t0:t0 + 64], True, True, ['h_Kt%d' % d, 'h_Qt%d' % d], [an])
                    k.tt(am[:], aP[:], mask[:], ALU.mult, [an, 'h_maskF', 'h_maskB'], [amn])
                    o = oP[d]
                    on = 'h_oP%d' % d
                    k.mm(o[:], Vsb[:, n, :], am[:], True, idx == 0, ['h_Vsb', amn], [on])
                    if idx > 0:
                        k.mm(o[:], Sb[d][:], Qh[d][:, t0:t0 + 64], False, True, ['h_Sb%d' % d, 'h_Qh%d' % d], [on])
                    grp = n // 4
                    og = ostg[d * 2 + grp % 2]
                    ogn = 'h_ostg%d' % (d * 2 + grp % 2)
                    k.cp(og[:, (n % 4) * 64:(n % 4 + 1) * 64], o[:], [on], [ogn], eng='act')
                    last_in_grp = (n % 4 == 3) if d == 0 else (n % 4 == 0)
                    if last_in_grp:
                        k.dma(Odst[h * 128:(h + 1) * 128, grp * 256:(grp + 1) * 256], og[:], [ogn], [('O', d, h, grp)], eng='act')
                    kv = kvP[d]
                    kvn = 'h_kvP%d' % d
                    k.mm(kv[:], Ktok[d][:, n, :], Vsb[:, n, :], True, True, ['h_Ktok%d' % d, 'h_Vsb'], [kvn])
                    if idx == 0:
                        k.cp(S32[d][:], kv[:], [kvn], ['h_S32%d' % d])
                    else:
                        k.stt(S32[d][:], S32[d][:], decay[d][:, n:n + 1], kv[:], ALU.mult, ALU.add, ['h_S32%d' % d, 'h_decay%d' % d, kvn], ['h_S32%d' % d])
                    k.cp(Sb[d][:], S32[d][:], ['h_S32%d' % d], ['h_Sb%d' % d], eng='act')
    P.barrier()
    with ExitStack() as s:
        gin = k.sb(s, 'r_gin', [1, 128], F32)
        gcol = k.sb(s, 'r_gcol', [128, 1], F32)
        of = [k.sb(s, 'r_of%d' % i, [128, 512], F32) for i in range(2)]
        ob = [k.sb(s, 'r_ob%d' % i, [128, 512], F32) for i in range(2)]
        gt = [k.sb(s, 'r_gt%d' % i, [128, 512], F32) for i in range(2)]
        sq = [k.sb(s, 'r_sq%d' % i, [128, 512], BF16) for i in range(2)]
        rs = [k.sb(s, 'r_rs%d' % i, [128, 512], F32) for i in range(2)]
        ot = [k.sb(s, 'r_ot%d' % i, [128, 512], BF16) for i in range(2)]
        pss = [k.ps(s, 'r_pss%d' % i, [128, 512], F32) for i in range(2)]
        pm = k.ps(s, 'r_pm', [128, 16], F32)
        k.dma(gin[:], onorm, [], ['r_gin'])
        k.tr(pm[:, 0:1], gin[:], k.ident[0:1, 0:1], ['r_gin', 'ident'], ['r_pm'])
        k.cp(gcol[:], pm[:, 0:1], ['r_pm'], ['r_gcol'])
        it = 0
        for h in range(16):
            for (t0, nt) in [(j * 512, 512) for j in range(8)] + [(4096, 256)]:
                i = it % 2
                it += 1
                k.dma(of[i][:, 0:nt], OF[h * 128:(h + 1) * 128, t0:t0 + nt], [], ['r_of%d' % i])
                k.dma(ob[i][:, 0:nt], OB[h * 128:(h + 1) * 128, t0:t0 + nt], [], ['r_ob%d' % i])
                k.dma(gt[i][:, 0:nt], PT[(48 + h) * 128:(49 + h) * 128, t0:t0 + nt], [], ['r_gt%d' % i])
                k.tt(of[i][:, 0:nt], of[i][:, 0:nt], ob[i][:, 0:nt], ALU.add, ['r_of%d' % i, 'r_ob%d' % i], ['r_of%d' % i])
                k.act(sq[i][:, 0:nt], of[i][:, 0:nt], AF.Square, ['r_of%d' % i], ['r_sq%d' % i])
                k.mm(pss[i][:, 0:nt], k.onesb[:], sq[i][:, 0:nt], True, True, ['onesb', 'r_sq%d' % i], ['r_pss%d' % i])
                k.rstd_from_ssq(rs[i][:, 0:nt], pss[i][:, 0:nt], 128, ['r_pss%d' % i], ['r_rs%d' % i])
                k.stt(of[i][:, 0:nt], of[i][:, 0:nt], gcol[:, 0:1], rs[i][:, 0:nt], ALU.mult, ALU.mult,
                      ['r_of%d' % i, 'r_gcol', 'r_rs%d' % i], ['r_of%d' % i])
                k.act(gt[i][:, 0:nt], gt[i][:, 0:nt], AF.Silu, ['r_gt%d' % i], ['r_gt%d' % i])
                k.tt(ot[i][:, 0:nt], of[i][:, 0:nt], gt[i][:, 0:nt], ALU.mult, ['r_of%d' % i, 'r_gt%d' % i], ['r_ot%d' % i])
                k.dma(GT[h * 128:(h + 1) * 128, t0:t0 + nt], ot[i][:, 0:nt], ['r_ot%d' % i], [('GT', h, t0)])
    P.barrier()


def emit_rope_tables(k, s, dim, cosT, sinS, perm, tag):
    half = dim // 2
    nf = dim // 4
    pf = k.sb(s, tag + 'pf', [dim, 4], F32)
    grid = k.sb(s, tag + 'grid', [dim, 64, 64], I32)
    ang = k.sb(s, tag + 'ang', [dim, 4096], F32)
    fr = k.sb(s, tag + 'fr', [dim, 4096], F32)
    fi = k.sb(s, tag + 'fi', [dim, 4096], I32)
    T = tag
    vi = k.sb(s, tag + 'vi', [1, 3, dim], I32)
    vf = k.sb(s, tag + 'vf', [1, 3, dim], F32)
    pp = k.ps(s, tag + 'pp', [128, 4], F32)
    k.P.op('pool', lambda e: e.iota(vi[:, 0, :], pattern=[[0, 2], [0, 2], [1, nf]], base=0, channel_multiplier=0), (), [T + 'vi'])
    k.P.op('pool', lambda e: e.iota(vi[:, 1, :], pattern=[[0, 2], [1, 2], [0, nf]], base=0, channel_multiplier=0), (), [T + 'vi'])
    k.P.op('pool', lambda e: e.iota(vi[:, 2, :], pattern=[[1, 2], [0, 2], [0, nf]], base=0, channel_multiplier=0), (), [T + 'vi'])
    k.cp(vf[:], vi[:], [T + 'vi'], [T + 'vf'])
    for j in range(3):
        k.tr(pp[0:dim, j:j + 1], vf[:, j, :], k.ident[0:1, 0:1], [T + 'vf', 'ident'], [T + 'pp'])
    k.cp(pf[:, 0:3], pp[0:dim, 0:3], [T + 'pp'], [T + 'pf'])
    k.act(pf[:, 0:1], pf[:, 0:1], AF.Exp, [T + 'pf'], [T + 'pf'], scale=-math.log(10000.0) / nf)
    k.ts(pf[:, 3:4], pf[:, 2:3], 2.0, -1.0, ALU.mult, ALU.add, [T + 'pf'], [T + 'pf'])
    k.tt(pf[:, 2:3], pf[:, 1:2], pf[:, 0:1], ALU.mult, [T + 'pf'], [T + 'pf'])
    k.ts(pf[:, 1:2], pf[:, 1:2], -1.0, 1.0, ALU.mult, ALU.add, [T + 'pf'], [T + 'pf'])
    k.tt(pf[:, 1:2], pf[:, 1:2], pf[:, 0:1], ALU.mult, [T + 'pf'], [T + 'pf'])
    k.P.op('pool', lambda e: e.iota(grid[:], pattern=[[1, 64], [0, 64]], base=0, channel_multiplier=0), (), [T + 'grid'])
    k.cp(fr[:], grid[:].rearrange("p a b -> p (a b)"), [T + 'grid'], [T + 'fr'])
    k.ts(ang[:], fr[:], pf[:, 1:2], None, ALU.mult, None, [T + 'fr', T + 'pf'], [T + 'ang'])
    k.P.op('pool', lambda e: e.iota(grid[:], pattern=[[0, 64], [1, 64]], base=0, channel_multiplier=0), [T + 'fr'], [T + 'grid'])
    k.cp(fr[:], grid[:].rearrange("p a b -> p (a b)"), [T + 'grid', T + 'ang'], [T + 'fr'])
    k.stt(ang[:], fr[:], pf[:, 2:3], ang[:], ALU.mult, ALU.add, [T + 'fr', T + 'pf', T + 'ang'], [T + 'ang'])

    def sin_of(dst, shift, post_sign):
        k.ts(fr[:], ang[:], shift, 1.0 / (2 * math.pi), ALU.add, ALU.mult, [T + 'ang', T + 'fr'], [T + 'fr'])
        k.cp(fi[:], fr[:], [T + 'fr'], [T + 'fi'])
        k.cp(dst, fi[:], [T + 'fi'], [T + 'dst'])
        k.tt(fr[:], fr[:], dst, ALU.subtract, [T + 'fr', T + 'dst'], [T + 'fr'])
        k.ts(dst, fr[:], 0.5, None, ALU.is_gt, None, [T + 'fr'], [T + 'dst'])
        k.tt(fr[:], fr[:], dst, ALU.subtract, [T + 'fr', T + 'dst'], [T + 'fr'])
        k.ts(dst, fr[:], -0.5, None, ALU.is_lt, None, [T + 'fr'], [T + 'dst'])
        k.tt(fr[:], fr[:], dst, ALU.add, [T + 'fr', T + 'dst'], [T + 'fr'])
        k.act(dst, fr[:], AF.Sin, [T + 'fr'], [T + 'dst'], scale=2 * math.pi)
        if post_sign:
            k.ts(dst, dst, pf[:, 3:4], None, ALU.mult, None, [T + 'dst', T + 'pf'], [T + 'dst'])

    sin_of(cosT[:], math.pi / 2, False)
    sin_of(sinS[:], 0.0, True)
    k.P.op('dve', lambda e: e.tensor_single_scalar(perm[:], k.iota_i[0:dim, 0:dim], half, ALU.is_equal), ['iota_i'], [T + 'perm'])
    k.P.op('dve', lambda e: e.tensor_single_scalar(fr[:, 0:dim], k.iota_i[0:dim, 0:dim], -half, ALU.is_equal), ['iota_i', T + 'fr'], [T + 'fr'])
    k.tt(perm[:], perm[:], fr[:, 0:dim], ALU.add, [T + 'perm', T + 'fr'], [T + 'perm'])


QBLKS = [(0, 256, 2)] + [(256 + 512 * j, 512, 34) for j in range(8)]


def emit_attn_core(k, maps, Vsb, ndv, scale, vname, sP, oP, rP, pT, finalize):
    it = 0
    for (q0, nq, nkt) in QBLKS:
        for mi, parts in enumerate(maps):
            for kt in range(nkt):
                sp = sP[it % len(sP)]
                spn = 'a_sP%d' % (it % len(sP))
                pt = pT[it % len(pT)]
                ptn = 'a_pT%d' % (it % len(pT))
                it += 1
                for pi_, (Kap, Qap, kd, kn, qn) in enumerate(parts):
                    k.mm(sp[:, 0:nq], Kap[0:kd, kt * 128:(kt + 1) * 128], Qap[0:kd, q0:q0 + nq], pi_ == 0, pi_ == len(parts) - 1,
                         [kn, qn], [spn])
                k.act(pt[:, 0:nq], sp[:, 0:nq], AF.Exp, [spn], [ptn], scale=scale, bias=k.negshift[:, 0:1])
                for dvc in range(ndv):
                    k.mm(oP[dvc][:, 0:nq], Vsb[:, kt, dvc * 128:(dvc + 1) * 128], pt[:, 0:nq], kt == 0, kt == nkt - 1,
                         [vname, ptn], ['a_oP%d' % dvc])
                k.mm(rP[:, 0:nq], k.onesb[:], pt[:, 0:nq], kt == 0, kt == nkt - 1, ['onesb', ptn], ['a_rP'])
            finalize(q0, nq, mi)


BLK9 = [(0, 256)] + [(256 + j * 512, 512) for j in range(8)]


class _Stop(Exception):
    pass


def emit_mla(k, PT, wuq, wukv, gains, GT, QT, KT, KrT, VT, mla_stop=None):
    try:
        _emit_mla(k, PT, wuq, wukv, gains, GT, QT, KT, KrT, VT, mla_stop)
    except _Stop:
        k.P.barrier()


def _emit_mla(k, PT, wuq, wukv, gains, GT, QT, KT, KrT, VT, mla_stop=None):
    P = k.P
    scale = 192.0 ** -0.5
    with ExitStack() as s:
        gi = k.sb(s, 'm_gi', [14, 128], F32)
        gT = k.sb(s, 'm_gT', [128, 14], F32)
        wq = k.sb(s, 'm_wq', [128, 6, 3072], BF16)
        wkv = k.sb(s, 'm_wkv', [128, 4, 4096], BF16)
        perm = k.sb(s, 'm_perm', [64, 64], F32)
        cosb = k.sb(s, 'm_cosb', [64, 4096], BF16)
        sinb = k.sb(s, 'm_sinb', [64, 4096], BF16)
        with ExitStack() as s1:
            cosT = k.sb(s1, 'm_cos', [64, 4096], F32)
            sinS = k.sb(s1, 'm_sin', [64, 4096], F32)
            with ExitStack() as s2:
                emit_rope_tables(k, s2, 64, cosT, sinS, perm, 'rb_')
            P.barrier()
            k.cp(cosb[:], cosT[:], [], ['m_cosb'])
            k.cp(sinb[:], sinS[:], [], ['m_sinb'])
            P.barrier()
        xq = k.sb(s, 'm_xq', [128, 6, 512], F32)
        xkv = k.sb(s, 'm_xkv', [128, 4, 512], F32)
        cqn = k.sb(s, 'm_cqn', [128, 6, 512], BF16)
        ckvn = k.sb(s, 'm_ckvn', [128, 4, 512], BF16)
        sq = [k.sb(s, 'm_sq%d' % i, [128, 512], BF16) for i in range(2)]
        rs = k.sb(s, 'm_rs', [128, 512], F32)
        kx = [k.sb(s, 'm_kx%d' % i, [128, 512], F32) for i in range(2)]
        kr2 = k.sb(s, 'm_kr2', [64, 512], F32)
        outb = [k.sb(s, 'm_outb%d' % i, [128, 512], BF16) for i in range(3)]
        vout = [k.sb(s, 'm_vout%d' % i, [128, 512], BF16) for i in range(2)]
        pss = k.ps(s, 'm_pss', [128, 512], F32)
        pup = [k.ps(s, 'm_pup%d' % i, [128, 512], F32) for i in range(3)]
        pm = k.ps(s, 'm_pm', [128, 512], F32)
        k.dma(gi[:], gains, [], ['m_gi'])
        k.tr(pm[:, 0:14], gi[:], k.ident[0:14, 0:14], ['m_gi', 'ident'], ['m_pm'])
        k.cp(gT[:], pm[:, 0:14], ['m_pm'], ['m_gT'])
        for kc in range(6):
            k.dma(wq[:, kc, :], wuq[kc * 128:(kc + 1) * 128, :], [], [('m_wq', kc)], eng='pool')
        for kc in range(4):
            k.dma(wkv[:, kc, :], wukv[kc * 128:(kc + 1) * 128, :], [], [('m_wkv', kc)], eng='pool')
        WQ = [('m_wq', kc) for kc in range(6)]
        WKV = [('m_wkv', kc) for kc in range(4)]
        cnt = {'u': 0, 'o': 0, 'x': 0, 'v': 0}

        def rms1(src, gcol, n, dst, rows, srcn, dstn):
            nt = src.shape[1]
            k.act(sq[0][0:rows, 0:nt], src, AF.Square, [srcn], ['m_sq0'])
            k.mm(pss[0:rows, 0:nt], k.onesb[0:rows, 0:rows], sq[0][0:rows, 0:nt], True, True, ['onesb', 'm_sq0'], ['m_pss'])
            k.rstd_from_ssq(rs[0:rows, 0:nt], pss[0:rows, 0:nt], n, ['m_pss'], ['m_rs'])
            k.stt(dst, src, gcol, rs[0:rows, 0:nt], ALU.mult, ALU.mult, [srcn, 'm_gT', 'm_rs'], [dstn])

        def rope_fm(x32, xn, t0, nt, dst, dstn):
            if t0 < TC:
                k.cp(dst, x32, [xn], [dstn])
                return
            l0 = t0 - TC
            k.mm(pm[0:64, 0:nt], perm[:], x32, True, True, ['rb_perm', xn], ['m_pm'])
            k.tt(kr2[:, 0:nt], pm[0:64, 0:nt], sinb[:, l0:l0 + nt], ALU.mult, ['m_pm', 'm_sinb'], ['m_kr2'])
            k.tt(x32, x32, cosb[:, l0:l0 + nt], ALU.mult, [xn, 'm_cosb'], [xn])
            k.tt(dst, x32, kr2[:, 0:nt], ALU.add, [xn, 'm_kr2'], [dstn])

        try:
            for (t0, nt) in ([] if mla_stop == 'rope' else BLK9):
                if mla_stop == 'blk0' and t0 > 0:
                    break
                for (c0, ncs, goff, xb, xn, dstb, dn) in ((64, 6, 0, xq, 'm_xq', cqn, 'm_cqn'), (70, 4, 6, xkv, 'm_xkv', ckvn, 'm_ckvn')):
                    for c in range(ncs):
                        k.dma(xb[:, c, 0:nt], PT[(c0 + c) * 128:(c0 + c + 1) * 128, t0:t0 + nt], [], [(xn, c)])
                    for c in range(ncs):
                        sqi = sq[c % 2]
                        k.act(sqi[:, 0:nt], xb[:, c, 0:nt], AF.Square, [(xn, c)], ['m_sq%d' % (c % 2)])
                        k.mm(pss[:, 0:nt], k.onesb[:], sqi[:, 0:nt], c == 0, c == ncs - 1, ['onesb', 'm_sq%d' % (c % 2)], ['m_pss'], sig=True)
                    k.rstd_from_ssq(rs[:, 0:nt], pss[:, 0:nt], ncs * 128, ['m_pss'], ['m_rs'])
                    for c in range(ncs):
                        k.stt(dstb[:, c, 0:nt], xb[:, c, 0:nt], gT[:, goff + c:goff + c + 1], rs[:, 0:nt], ALU.mult, ALU.mult,
                              [(xn, c), 'm_gT', 'm_rs'], [dn])
                if mla_stop == 'n1':
                    raise _Stop()
                x = kx[cnt['x'] % 2]; xn = 'm_kx%d' % (cnt['x'] % 2); cnt['x'] += 1
                k.dma(x[0:64, 0:nt], PT[74 * 128:74 * 128 + 64, t0:t0 + nt], [], [xn])
                rms1(x[0:64, 0:nt], gT[0:64, 13:14], 64, x[0:64, 0:nt], 64, xn, xn)
                o = outb[cnt['o'] % 3]; on = 'm_outb%d' % (cnt['o'] % 3); cnt['o'] += 1
                rope_fm(x[0:64, 0:nt], xn, t0, nt, o[0:64, 0:nt], on)
                k.dma(KrT[:, t0:t0 + nt], o[0:64, 0:nt], [on], [('KrT', t0)])
                if mla_stop == 'kr':
                    raise _Stop()
                for h in range(16):
                    if mla_stop == 'h0' and h > 0:
                        raise _Stop()
                    jobs = [
                        (wkv, WKV, 4, h * 256, 128, ckvn, 'm_ckvn', gT[:, 11:12], 128, False, KT[h * 128:(h + 1) * 128, t0:t0 + nt]),
                        (wq, WQ, 6, h * 192, 128, cqn, 'm_cqn', gT[:, 10:11], 128, False, QT[h * 192:h * 192 + 128, t0:t0 + nt]),
                        (wq, WQ, 6, h * 192 + 128, 64, cqn, 'm_cqn', gT[0:64, 12:13], 64, True, QT[h * 192 + 128:h * 192 + 192, t0:t0 + nt]),
                    ]
                    for (W, WN, nk, col0, rows, src, srcn, gcol, n, rope, dst) in jobs:
                        pu = pup[cnt['u'] % 3]; pun = 'm_pup%d' % (cnt['u'] % 3); cnt['u'] += 1
                        for kc in range(nk):
                            k.mm(pu[0:rows, 0:nt], W[:, kc, col0:col0 + rows], src[:, kc, 0:nt], kc == 0, kc == nk - 1, WN + [srcn], [pun])
                        x = kx[cnt['x'] % 2]; xn = 'm_kx%d' % (cnt['x'] % 2); cnt['x'] += 1
                        k.cp(x[0:rows, 0:nt], pu[0:rows, 0:nt], [pun], [xn], eng='act')
                        o = outb[cnt['o'] % 3]; on = 'm_outb%d' % (cnt['o'] % 3); cnt['o'] += 1
                        if rope:
                            rms1(x[0:rows, 0:nt], gcol, n, x[0:rows, 0:nt], rows, xn, xn)
                            rope_fm(x[0:rows, 0:nt], xn, t0, nt, o[0:rows, 0:nt], on)
                        else:
                            rms1(x[0:rows, 0:nt], gcol, n, o[0:rows, 0:nt], rows, xn, on)
                        k.dma(dst, o[0:rows, 0:nt], [on], [('mq', h, col0, t0)])
                for tl in range(nt // 128):
                    for hg in range(4):
                        pu = pup[cnt['u'] % 3]; pun = 'm_pup%d' % (cnt['u'] % 3); cnt['u'] += 1
                        for hh in range(4):
                            h = hg * 4 + hh
                            for kc in range(4):
                                k.mm(pu[:, hh * 128:(hh + 1) * 128], ckvn[:, kc, tl * 128:(tl + 1) * 128],
                                     wkv[:, kc, h * 256 + 128:h * 256 + 256], kc == 0, kc == 3, WKV + ['m_ckvn'], [pun], sig=(kc == 3 and hh == 3))
                        vo = vout[cnt['v'] % 2]; von = 'm_vout%d' % (cnt['v'] % 2); cnt['v'] += 1
                        k.cp(vo[:], pu[:], [pun], [von], eng='act')
                        k.dma(VT[t0 + tl * 128:t0 + (tl + 1) * 128, hg * 512:(hg + 1) * 512], vo[:], [von], [('VT', t0, tl, hg)], eng='act')
        except _Stop:
            pass
    P.barrier()
    if mla_stop is not None:
        raise _Stop()
    with ExitStack() as s:
        Kr = k.sb(s, 'a_Kr', [64, TT], BF16)
        Kn = [k.sb(s, 'a_Kn%d' % i, [128, TT], BF16) for i in range(2)]
        Qn = [k.sb(s, 'a_Qn%d' % i, [128, TT], BF16) for i in range(2)]
        Qr = [k.sb(s, 'a_Qr%d' % i, [64, TT], BF16) for i in range(2)]
        Vsb = [k.sb(s, 'a_Vsb%d' % i, [128, 34, 128], BF16) for i in range(2)]
        pT = [k.sb(s, 'a_pT%d' % i, [128, 512], BF16) for i in range(4)]
        rinv = k.sb(s, 'a_rinv', [128, 512], F32)
        ob = [k.sb(s, 'a_ob%d' % i, [128, 512], BF16) for i in range(2)]
        sP = [k.ps(s, 'a_sP%d' % i, [128, 512], F32) for i in range(4)]
        oP = [k.ps(s, 'a_oP0', [128, 512], F32)]
        rP = k.ps(s, 'a_rP', [128, 512], F32)
        k.dma(Kr[:], KrT, [], ['a_Kr'])
        oi = [0]
        for h in range(16):
            i = h % 2
            k.dma(Kn[i][:], KT[h * 128:(h + 1) * 128, :], [], ['a_Kn%d' % i])
            k.dma(Qn[i][:], QT[h * 192:h * 192 + 128, :], [], ['a_Qn%d' % i])
            k.dma(Qr[i][:], QT[h * 192 + 128:h * 192 + 192, :], [], ['a_Qr%d' % i])
            k.dma(Vsb[i][:], VT[:, h * 128:(h + 1) * 128].rearrange("(t p) v -> p t v", p=128), [], ['a_V%d' % i])

            def fin(q0, nq, mi, h=h):
                k.P.op('dve', lambda e: e.reciprocal(rinv[:, 0:nq], rP[:, 0:nq]), ['a_rP'], ['a_rinv'])
                o = ob[oi[0] % 2]
                on = 'a_ob%d' % (oi[0] % 2)
                oi[0] += 1
                k.tt(o[:, 0:nq], oP[0][:, 0:nq], rinv[:, 0:nq], ALU.mult, ['a_oP0', 'a_rinv'], [on])
                k.dma(GT[2048 + h * 128:2048 + (h + 1) * 128, q0:q0 + nq], o[:, 0:nq], [on], [('GT2', h, q0)])

            parts = [(Kn[i], Qn[i], 128, 'a_Kn%d' % i, 'a_Qn%d' % i), (Kr, Qr[i], 64, 'a_Kr', 'a_Qr%d' % i)]
            emit_attn_core(k, [parts], Vsb[i], 1, scale, 'a_V%d' % i, sP, oP, rP, pT, fin)
    P.barrier()


def emit_bcast_mod(k, s, modT, m, r, out, pbank, pname, tag):
    onesf = k.sb(s, tag + 'onesf', [128, 128], F32)
    dg = [k.sb(s, tag + 'dg%d' % i, [128, 128], F32) for i in range(2)]
    k.memset(onesf[:], 1.0, [tag + 'onesf'])
    for kc in range(32):
        d = dg[kc % 2]
        dn = tag + 'dg%d' % (kc % 2)
        k.ts(d[:], k.ident[:], modT[:, m * 32 + kc, r:r + 1], None, ALU.mult, None, ['ident', 'modT'], [dn])
        k.mm(pbank[:, (kc % 4) * 128:(kc % 4 + 1) * 128], onesf[:], d[:], True, True, [tag + 'onesf', dn], [pname])
        if kc % 4 == 3:
            k.cp(out[:, (kc - 3) * 128:(kc + 1) * 128], pbank[:, 0:512], [pname], [tag + 'out'])


def emit_outproj(k, xcat, GT, wout, wout_b, modT, m_gate, xmid, ntok_first=0):
    P = k.P
    with ExitStack() as s:
        wn = k.castnames['wout']
        modB = [k.sb(s, 'o_modB%d' % r, [128, D], F32) for r in range(2)]
        wt = [k.sb(s, 'o_wt%d' % i, [128, 32, 512], BF16) for i in range(2)]
        gtt = [k.sb(s, 'o_gt%d' % i, [128, 32, 512], BF16) for i in range(2)]
        xt = [k.sb(s, 'o_xt%d' % i, [128, 512], F32) for i in range(3)]
        pb = [k.ps(s, 'o_pb%d' % i, [128, 512], F32) for i in range(4)]
        for r in range(2):
            with ExitStack() as s2:
                emit_bcast_mod(k, s2, modT, m_gate, r, modB[r], pb[3], 'o_pb3', 'ob%d_' % r)
            P.barrier()
        wv = wout_b.rearrange("(kc p) n -> p kc n", p=128)
        gv = GT.rearrange("(kc p) t -> p kc t", p=128)
        gi = xi = ai = 0
        for db in range(8):
            w = wt[db % 2]
            wnm = 'o_wt%d' % (db % 2)
            for hf in range(2):
                k.dma(w[:, hf * 16:(hf + 1) * 16, :], wv[:, hf * 16:(hf + 1) * 16, db * 512:(db + 1) * 512], wn, [(wnm, hf)])
            for (tok0, ntl, r) in TGS:
                g = gtt[gi % 2]; gn = 'o_gt%d' % (gi % 2); gi += 1
                k.dma(g[:, :, 0:ntl * 128], gv[:, :, tok0:tok0 + ntl * 128], [], [gn])
                for ti in range(ntl):
                    t0 = tok0 + ti * 128
                    x = xt[xi % 3]; xn = 'o_xt%d' % (xi % 3); xi += 1
                    k.dma(x[:], xcat[t0:t0 + 128, db * 512:(db + 1) * 512], [], [xn])
                    a = pb[ai % 3]; an = 'o_pb%d' % (ai % 3); ai += 1
                    for kc in range(32):
                        k.mm(a[:], g[:, kc, ti * 128:(ti + 1) * 128], w[:, kc, :], kc == 0, kc == 31, [gn, (wnm, kc // 16)], [an])
                    k.tt(a[:], a[:], modB[r][:, db * 512:(db + 1) * 512], ALU.mult, [an], [an])
                    k.tt(x[:], x[:], a[:], ALU.add, [xn, an], [xn])
                    k.dma(xmid[t0:t0 + 128, db * 512:(db + 1) * 512], x[:], [xn], [('xmid', t0, db)], eng='act')
    P.barrier()


def emit_norm2(k, xmid, g2, modT, m_shift, m_scale, H2T):
    P = k.P
    with ExitStack() as s:
        AB = k.sb(s, 'n_AB', [128, 2, 32, 2], F32)
        xt = [k.sb(s, 'n_xt%d' % i, [128, D], F32) for i in range(2)]
        junk = k.sb(s, 'n_junk', [128, D], BF16)
        ssq = k.sb(s, 'n_ssq', [128, 4], F32)
        hT = [k.sb(s, 'n_hT%d' % i, [128, 32, 512], BF16) for i in range(2)]
        trp = [k.ps(s, 'n_trp%d' % i, [128, 8, 128], F32) for i in range(2)]
        emit_AB(k, s, g2, modT, m_shift, m_scale, AB, trp)
        hv = H2T.rearrange("(kc p) t -> p kc t", p=128)
        xcount = [0]
        for gi, (tok0, ntl, r) in enumerate(TGS):
            h = hT[gi % 2]
            emit_norm_hT(k, xmid, tok0, ntl, r, AB, xt, 'n_xt', junk, ssq, trp, h, xcount, 'hT%d' % (gi % 2))
            k.dma(hv[:, :, tok0:tok0 + ntl * 128], h[:, :, 0:ntl * 128], ['hT%d' % (gi % 2)], [('H2T', tok0)])
    P.barrier()


PIECE = 256
NEG = -1.0e30


def emit_peer(k, xmid, H2T, wq, wq_b, skT, puT, puT_b, pv, pv_b, modT, m_gate, xout, pieces):
    P = k.P
    with ExitStack() as s:
        wqn = k.castnames['wq']
        pun = k.castnames['puT']
        pvn = k.castnames['pv']
        sk = k.sb(s, 'p_sk', [128, 16, 128], F32)
        mcol = k.sb(s, 'p_mcol', [128, 32, 2], F32)
        iotaF = k.sb(s, 'p_iotaF', [128, 128], F32)
        thr = k.sb(s, 'p_thr', [128, 16], F32)
        iota16 = k.sb(s, 'p_iota16', [128, 16], F32)
        h2 = k.sb(s, 'p_h2', [128, 32, PIECE], BF16)
        wt = [k.sb(s, 'p_wt%d' % i, [128, 32, 128], BF16) for i in range(4)]
        vt = [k.sb(s, 'p_vt%d' % i, [128, 1536], BF16) for i in range(4)]
        qT = k.sb(s, 'p_qT', [128, 16, PIECE], F32)
        ssb = k.sb(s, 'p_ssb', [128, 16, 128], F32)
        s2 = k.sb(s, 'p_s2', [128, 16, 128], F32)
        c16 = k.sb(s, 'p_c16', [128, 16, 16], F32)
        ix = k.sb(s, 'p_ix', [128, 16, 16], U32)
        ixf = k.sb(s, 'p_ixf', [128, 16, 16], F32)
        cand = ssb[:].rearrange("p (h t) c -> p h (t c)", t=2)
        cand2 = s2[:].rearrange("p (h t) c -> p h (t c)", t=2)
        top = k.sb(s, 'p_top', [128, 8, 16], F32)
        pos = k.sb(s, 'p_pos', [128, 8, 16], U32)
        posf = k.sb(s, 'p_posf', [128, 8, 16], F32)
        w4 = k.sb(s, 'p_w4', [128, 8, 16, 16], F32)
        ak = k.sb(s, 'p_ak', [128, 8, 16], F32)
        bk = k.sb(s, 'p_bk', [128, 8, 16], F32)
        ijg = k.sb(s, 'p_ijg', [128, 3, 128], F32)
        zs = k.sb(s, 'p_zs', [128, 8], F32)
        ijgT = k.sb(s, 'p_ijgT', [128, 3, PIECE], F32)
        P1 = k.sb(s, 'p_P1', [128, 32, 128], BF16)
        P2 = k.sb(s, 'p_P2', [128, 32, 128], BF16)
        GA = k.sb(s, 'p_GA', [128, 128, PIECE], BF16)
        gl = [k.sb(s, 'p_gl%d' % i, [128, PIECE], F32) for i in range(2)]
        ys = [k.sb(s, 'p_ys%d' % i, [128, 128], F32) for i in range(2)]
        xo = [k.sb(s, 'p_xo%d' % i, [128, 512], F32) for i in range(1)]
        pb = [k.ps(s, 'p_pb%d' % i, [128, 512], F32) for i in range(8)]
        PB = ['p_pb%d' % i for i in range(8)]
        k.dma(sk[:], skT, [], ['p_sk'])
        k.cp(mcol[:], modT[:, m_gate * 32:(m_gate + 1) * 32, :], ['modT'], ['p_mcol'])
        k.P.op('pool', lambda e: e.iota(s2[:, 0, :].bitcast(I32), pattern=[[1, 128]], base=0, channel_multiplier=0), (), ['p_s2'])
        k.cp(iotaF[:], s2[:, 0, :].bitcast(I32), ['p_s2'], ['p_iotaF'])
        k.ts(thr[:], iotaF[:, 0:16], 16.0, 16.0, ALU.mult, ALU.add, ['p_iotaF'], ['p_thr'])
        k.cp(iota16[:], iotaF[:, 0:16], ['p_iotaF'], ['p_iota16'])
        h2v = H2T.rearrange("(kc p) t -> p kc t", p=128)
        wqv = wq_b
        uv = puT_b
        vv = pv_b.rearrange("(i j) n -> j i n", j=128)
        wi = [0]

        def wtile(src, c0, names):
            w = wt[wi[0] % 4]; wn = 'p_wt%d' % (wi[0] % 4); wi[0] += 1
            k.dma(w[:], src[:, c0 // 128], names, [wn])
            return w, wn

        for t0 in pieces:
            r = 1 if t0 < TC else 0
            k.dma(h2[:], h2v[:, :, t0:t0 + PIECE], [], ['p_h2'])
            for j in range(16):
                w, wn = wtile(wqv, j * 128, wqn)
                a = pb[j % 2]
                for kc in range(32):
                    k.mm(a[:, 0:PIECE], w[:, kc, :], h2[:, kc, :], kc == 0, kc == 31, [wn, 'p_h2'], [PB[j % 2]])
                k.cp(qT[:, j, :], a[:, 0:PIECE], [PB[j % 2]], ['p_qT'], eng='act')
            for tl in range(PIECE // 128):
                for j in range(16):
                    b = pb[2 + (j // 4) % 2]; bn = PB[2 + (j // 4) % 2]
                    k.mm(b[:, (j % 4) * 128:(j % 4 + 1) * 128], qT[:, j, tl * 128:(tl + 1) * 128], sk[:, j, :], True, True,
                         ['p_qT', 'p_sk'], [bn], sig=(j % 4 == 3))
                    if j % 4 == 3:
                        k.cp(ssb[:, j - 3:j + 1, :], b[:, :].rearrange("p (a c) -> p a c", a=4), [bn], ['p_ssb'])
                for j in range(16):
                    k.P.op('dve', lambda e, j=j: e.max(out=c16[:, j, 0:8], in_=ssb[:, j, :]), ['p_ssb'], ['p_c16'])
                    k.P.op('dve', lambda e, j=j: e.max_index(out=ix[:, j, 0:8], in_max=c16[:, j, 0:8], in_values=ssb[:, j, :]), ['p_ssb', 'p_c16'], ['p_ix'])
                    k.P.op('dve', lambda e, j=j: e.match_replace(out=s2[:, j, :], in_to_replace=c16[:, j, 0:8], in_values=ssb[:, j, :], imm_value=NEG),
                           ['p_ssb', 'p_c16'], ['p_s2'])
                    k.P.op('dve', lambda e, j=j: e.max(out=c16[:, j, 8:16], in_=s2[:, j, :]), ['p_s2'], ['p_c16'])
                    k.P.op('dve', lambda e, j=j: e.max_index(out=ix[:, j, 8:16], in_max=c16[:, j, 8:16], in_values=s2[:, j, :]), ['p_s2', 'p_c16'], ['p_ix'])
                k.cp(ixf[:], ix[:], ['p_ix'], ['p_ixf'])
                c4 = c16[:].rearrange("p (h t) a -> p h t a", t=2)
                i4 = ixf[:].rearrange("p (h t) a -> p h t a", t=2)
                candv = cand.rearrange("p h (a b) -> p h a b", a=16)
                k.tt(candv, c4[:, :, 0, :].unsqueeze(3).to_broadcast([128, 8, 16, 16]),
                     c4[:, :, 1, :].unsqueeze(2).to_broadcast([128, 8, 16, 16]), ALU.add, ['p_c16'], ['p_ssb'])
                for h in range(8):
                    k.P.op('dve', lambda e, h=h: e.max(out=top[:, h, 0:8], in_=cand[:, h, :]), ['p_ssb'], ['p_top'])
                    k.P.op('dve', lambda e, h=h: e.max_index(out=pos[:, h, 0:8], in_max=top[:, h, 0:8], in_values=cand[:, h, :]), ['p_ssb', 'p_top'], ['p_pos'])
                    k.P.op('dve', lambda e, h=h: e.match_replace(out=cand2[:, h, :], in_to_replace=top[:, h, 0:8], in_values=cand[:, h, :], imm_value=NEG),
                           ['p_ssb', 'p_top'], ['p_s2'])
                    k.P.op('dve', lambda e, h=h: e.max(out=top[:, h, 8:16], in_=cand2[:, h, :]), ['p_s2'], ['p_top'])
                    k.P.op('dve', lambda e, h=h: e.max_index(out=pos[:, h, 8:16], in_max=top[:, h, 8:16], in_values=cand2[:, h, :]), ['p_s2', 'p_top'], ['p_pos'])
                k.cp(posf[:], pos[:], ['p_pos'], ['p_posf'])
                k.tt(w4[:], posf[:].unsqueeze(3).to_broadcast([128, 8, 16, 16]),
                     thr[:].unsqueeze(1).unsqueeze(1).to_broadcast([128, 8, 16, 16]), ALU.is_ge, ['p_posf', 'p_thr'], ['p_w4'])
                k.P.op('dve', lambda e: e.tensor_reduce(out=ak[:], in_=w4[:], axis=AX.X, op=ALU.add), ['p_w4'], ['p_ak'])
                k.stt(bk[:], ak[:], -16.0, posf[:], ALU.mult, ALU.add, ['p_ak', 'p_posf'], ['p_bk'])
                for (sel, side, dsti) in ((ak, 0, 0), (bk, 1, 1)):
                    k.tt(w4[:], sel[:].unsqueeze(3).to_broadcast([128, 8, 16, 16]),
                         iota16[:].unsqueeze(1).unsqueeze(1).to_broadcast([128, 8, 16, 16]), ALU.is_equal, ['p_ak', 'p_bk', 'p_iota16'], ['p_w4'])
                    k.tt(w4[:], w4[:], i4[:, :, side, :].unsqueeze(2).to_broadcast([128, 8, 16, 16]), ALU.mult, ['p_w4', 'p_ixf'], ['p_w4'])
                    k.P.op('dve', lambda e, dsti=dsti: e.tensor_reduce(out=ijg[:, dsti, :].rearrange("p (h a) -> p h a", h=8), in_=w4[:], axis=AX.X, op=ALU.add),
                           ['p_w4'], ['p_ijg'])
                g3 = ijg[:, 2, :].rearrange("p (h a) -> p h a", h=8)
                k.tt(g3, top[:], top[:, :, 0:1].to_broadcast([128, 8, 16]), ALU.subtract, ['p_top'], ['p_ijg'])
                k.act(g3, g3, AF.Exp, ['p_ijg'], ['p_ijg'])
                k.P.op('dve', lambda e: e.tensor_reduce(out=zs[:], in_=g3, axis=AX.X, op=ALU.add), ['p_ijg'], ['p_zs'])
                k.P.op('dve', lambda e: e.reciprocal(zs[:], zs[:]), ['p_zs'], ['p_zs'])
                k.tt(g3, g3, zs[:].unsqueeze(2).to_broadcast([128, 8, 16]), ALU.mult, ['p_ijg', 'p_zs'], ['p_ijg'])
                for q in range(3):
                    k.tr(pb[6][:, q * 128:(q + 1) * 128], ijg[:, q, :], k.ident[:], ['p_ijg', 'ident'], [PB[6]], sig=(q == 2))
                k.cp(ijgT[:, :, tl * 128:(tl + 1) * 128], pb[6][:, 0:384].rearrange("p (q t) -> p q t", q=3), [PB[6]], ['p_ijgT'])
            for sb_ in range(PIECE // 32):
                for t in range(32):
                    tg = sb_ * 32 + t
                    k.ts(P1[:, t, :], iotaF[:], ijgT[:, 0, tg:tg + 1], ijgT[:, 2, tg:tg + 1], ALU.is_equal, ALU.mult,
                         ['p_iotaF', 'p_ijgT'], ['p_P1'])
                    k.ts(P2[:, t, :], iotaF[:], ijgT[:, 1, tg:tg + 1], None, ALU.is_equal, None, ['p_iotaF', 'p_ijgT'], ['p_P2'])
                for tq in range(8):
                    b = pb[4 + tq % 2]; bn = PB[4 + tq % 2]
                    for u in range(4):
                        t = tq * 4 + u
                        k.mm(b[:, u * 128:(u + 1) * 128], P2[:, t, :], P1[:, t, :], True, True, ['p_P1', 'p_P2'], [bn], sig=(u == 3))
                    tb = sb_ * 32 + tq * 4
                    k.cp(GA[:, :, tb:tb + 4].rearrange("p i t -> p t i"), b[:, :].rearrange("p (t i) -> p t i", t=4), [bn], ['p_GA'])
            for i in range(128):
                w, wn = wtile(uv, i * 128, pun)
                a = pb[i % 2]
                for kc in range(32):
                    k.mm(a[:, 0:PIECE], w[:, kc, :], h2[:, kc, :], kc == 0, kc == 31, [wn, 'p_h2'], [PB[i % 2]])
                g = gl[i % 2]; gn = 'p_gl%d' % (i % 2)
                k.act(g[:], a[:, 0:PIECE], AF.Gelu, [PB[i % 2]], [gn])
                k.tt(GA[:, i, :], GA[:, i, :], g[:], ALU.mult, ['p_GA', gn], ['p_GA'])
            pbT = pb[6]
            for (dc0, ndc) in ((0, 12), (12, 12), (24, 8)):
                ncol = ndc * 128
                for i in range(128):
                    v = vt[i % 4]; vn = 'p_vt%d' % (i % 4)
                    k.dma(v[:, 0:ncol], vv[:, i, dc0 * 128:dc0 * 128 + ncol], pvn, [vn])
                    for dc in range(ndc):
                        acc = pb[dc // 2][:, (dc % 2) * 256:(dc % 2 + 1) * 256]
                        k.mm(acc, v[:, dc * 128:(dc + 1) * 128], GA[:, i, :], (i == 0 and dc % 2 == 0), i == 127,
                             [vn, 'p_GA'], [PB[dc // 2]], sig=(i == 127 or dc == ndc - 1))
                for dq in range(ndc // 4):
                    for tl in range(PIECE // 128):
                        x = xo[0]; xn = 'p_xo0'
                        tcur = t0 + tl * 128
                        col0 = dc0 * 128 + dq * 512
                        k.dma(x[:], xmid[tcur:tcur + 128, col0:col0 + 512], [], [xn])
                        for u in range(4):
                            dc = dq * 4 + u
                            dglob = dc0 + dc
                            y = ys[u % 2]; yn = 'p_ys%d' % (u % 2)
                            k.ts(y[:, 0:128], pb[dc // 2][:, (dc % 2) * 256 + tl * 128:(dc % 2) * 256 + (tl + 1) * 128],
                                 mcol[:, dglob, r:r + 1], None, ALU.mult, None, [PB[dc // 2], 'p_mcol'], [yn])
                            k.P.op('pe', lambda e, y=y, u=u: e.transpose(pbT[:, u * 128:(u + 1) * 128], y[:, 0:128], k.ident[:]),
                                   [yn, 'ident'], [PB[6]], sig=True)
                        k.tt(x[:], x[:], pbT[:, 0:512], ALU.add, [xn, PB[6]], [xn])
                        dst = xout(tcur)
                        if dst is not None:
                            k.dma(dst[:, col0:col0 + 512], x[:], [xn], [('xout', tcur, col0)], eng='act')
    P.barrier()


def build_program(n_layers=2, debug=None, stop_after=None, peer_pieces=None, only=None, mla_stop=None, force_not_last=False, layer_list=None):
    nc = bass.Bass("TRN2", target_bir_lowering=False)

    def din(name, shape, dt=F32):
        return nc.dram_tensor(name, list(shape), dt, kind="ExternalInput").ap()

    def dscr(name, shape, dt, out=False):
        if out:
            return nc.dram_tensor(name, list(shape), dt, kind="ExternalOutput").ap()
        return nc.dram_tensor(name, list(shape), dt).ap()

    debug = debug or ()
    shapes = {'xcat': [TT, D], 'cc': [64, 128], 'lbl': [64, 128], 'onorm_a': [1, 128], 'wuq': [768, 3072],
              'wukv': [512, 4096], 'mla_g': [14, 128], 'dgains': [4, 128], 'clam': [1, 512], 'convw': [4, 2048],
              'convb': [1, 2048], 'wgate': [2, 2, 16, 128, 128], 'bgate': [64, 128], 'dlam': [32, 128]}
    for l in range(n_layers):
        shapes.update({'wmod%d' % l: [D, 6 * D], 'bmod%d' % l: [192, 128], 'g1_%d' % l: [32, 128], 'g2_%d' % l: [32, 128],
                       'win%d' % l: [D, 11584 if l % 2 == 0 else 10240], 'wout%d' % l: [D, D], 'pwq%d' % l: [D, 2048],
                       'psk%d' % l: [128, 16, 128], 'puT%d' % l: [D, 16384], 'pv%d' % l: [16384, D]})

    class LazyIn(dict):
        def __missing__(self, key):
            self[key] = din(key, shapes[key])
            return self[key]

    I = LazyIn()
    y = nc.dram_tensor('y', [TL, D], F32, kind="ExternalOutput").ap()
    win_b = dscr('win_b', [128, 46, 32, 256], BF16)
    PT = dscr('PT', [75 * 128, TT], F32, out=('PT' in debug))
    Vh = dscr('Vh', [TT, 2048], BF16, out=('Vh' in debug))
    OF = dscr('OF', [2048, TT], F32)
    OB = dscr('OB', [2048, TT], F32)
    GT = dscr('GT', [D, TT], BF16, out=('GT' in debug))
    QT = dscr('QT', [16 * 192, TT], BF16)
    KT = dscr('KT', [16 * 128, TT], BF16)
    KrT = dscr('KrT', [64, TT], BF16)
    VT = dscr('VT', [TT, 2048], BF16)
    wout_b = dscr('wout_b', [D, D], BF16)
    xmid = dscr('xmid', [TT, D], F32, out=('xmid' in debug))
    H2T = dscr('H2T', [D, TT], BF16, out=('H2T' in debug))
    wq_b = dscr('wq_b', [128, 16, 32, 128], BF16)
    puT_b = dscr('puT_b', [128, 128, 32, 128], BF16)
    pv_b = dscr('pv_b', [16384, D], BF16)
    xnext = dscr('xnext', [TT, D], F32, out=('xnext' in debug))
    with ExitStack() as st:
        k = K(nc, st)
        block = st.enter_context(nc.Block())
        k.consts(st)
        modT = k.sb(st, 'modT', [128, 192, 2], F32)
        k.P.barrier()

        def run():
            if only == 'mla':
                emit_mla(k, PT, I['wuq'], I['wukv'], I['mla_g'], GT, QT, KT, KrT, VT, mla_stop)
                return
            xin = I['xcat']
            for l in (layer_list if layer_list is not None else range(n_layers)):
                last = (l == n_layers - 1) and not force_not_last
                emit_mod(k, st, I['cc'], I['wmod%d' % l], I['bmod%d' % l], 6, modT)
                if stop_after == 'mod':
                    return
                NFM_l = 9536 if l % 2 == 0 else 8192
                k.castnames = {}
                k.castnames['win'] = emit_cast_tiled(k, win_b, I['win%d' % l], [(0, NFM_l, 0), (NFM_l, 2048, (NFM_l + 255) // 256)], ('win_b', l))
                k.castnames['wout'] = emit_cast_weight(k, wout_b, I['wout%d' % l], 512, ('wout_b', l))
                k.castnames['wq'] = emit_cast_tiled(k, wq_b, I['pwq%d' % l], [(0, 2048, 0)], ('wq_b', l), tw=128)
                k.castnames['puT'] = emit_cast_tiled(k, puT_b, I['puT%d' % l], [(0, 16384, 0)], ('puT_b', l), tw=128)
                k.castnames['pv'] = emit_cast_weight(k, pv_b, I['pv%d' % l], 512, ('pv_b', l))
                if l % 2 == 0:
                    emit_inproj(k, xin, I['g1_%d' % l], modT, I['win%d' % l], 9536, 2048, PT, Vh, win_b)
                    if stop_after == 'inproj':
                        return
                    emit_hgrn2(k, PT, Vh, I['lbl'], I['onorm_a'], OF, OB, GT)
                    if stop_after == 'hgrn2':
                        return
                    emit_mla(k, PT, I['wuq'], I['wukv'], I['mla_g'], GT, QT, KT, KrT, VT)
                    if stop_after == 'mla':
                        return
                else:
                    emit_inproj(k, xin, I['g1_%d' % l], modT, I['win%d' % l], 8192, 2048, PT, Vh, win_b)
                    if stop_after == 'inproj':
                        return
                    lam_init = 0.8 - 0.6 * math.exp(-0.3 * l)
                    emit_diffattn(k, PT, Vh, I['dgains'], I['clam'], lam_init, GT)
                    if stop_after == 'diff':
                        return
                    emit_rglru(k, PT, I['convw'], I['convb'], I['wgate'], I['bgate'], I['dlam'], GT)
                    if stop_after == 'rglru':
                        return
                emit_outproj(k, xin, GT, I['wout%d' % l], wout_b, modT, 2, xmid)
                if stop_after == 'outproj':
                    return
                emit_norm2(k, xmid, I['g2_%d' % l], modT, 3, 4, H2T)
                if stop_after == 'norm2':
                    return
                if last:
                    pieces = [TC + 256 * i for i in range(16)]
                    xo = lambda t0: y[t0 - TC:t0 - TC + 128, :]
                else:
                    pieces = [256 * i for i in range(17)]
                    xo = lambda t0: xnext[t0:t0 + 128, :]
                if peer_pieces is not None:
                    pieces = peer_pieces
                emit_peer(k, xmid, H2T, I['pwq%d' % l], wq_b, I['psk%d' % l], I['puT%d' % l], puT_b, I['pv%d' % l], pv_b,
                          modT, 5, xo, pieces)
                xin = xnext

        run()
        k.P.barrier(full=True)
        print('nops', k.P.nops, {e: len(v) for e, v in k.P.streams.items()})
        k.P.emit(block)
    return nc


def prep_inputs(inp, b, n_layers=2):
    m = {}
    m['xcat'] = np.ascontiguousarray(np.concatenate([inp['ctx'][b], inp['x'][b]], axis=0))
    m['cc'] = np.ascontiguousarray(np.concatenate([inp['c'][b].reshape(32, 128), inp['c_ctx'].reshape(32, 128)], 0))
    for l in range(n_layers):
        m['wmod%d' % l] = np.ascontiguousarray(inp['w_mod'][l])
        m['bmod%d' % l] = np.ascontiguousarray(inp['b_mod'][l].reshape(192, 128))
        m['g1_%d' % l] = np.ascontiguousarray(inp['norm1_g'][l].reshape(32, 128))
        m['g2_%d' % l] = np.ascontiguousarray(inp['norm2_g'][l].reshape(32, 128))
        if l % 2 == 0:
            w = inp['e_w_in'][l // 2]
            cols = np.concatenate([np.arange(0, 2048), np.arange(2048, 4096), np.arange(4096, 6144), np.arange(8192, 10240),
                                   np.arange(10240, 11584), np.arange(6144, 8192)])
            m['win%d' % l] = np.ascontiguousarray(w[:, cols])
            m['wout%d' % l] = np.ascontiguousarray(inp['e_w_out'][l // 2])
        else:
            w = inp['o_w_in'][l // 2]
            cols = np.concatenate([np.arange(0, 4096), np.arange(6144, 10240), np.arange(4096, 6144)])
            m['win%d' % l] = np.ascontiguousarray(w[:, cols])
            m['wout%d' % l] = np.ascontiguousarray(inp['o_w_out'][l // 2])
        m['pwq%d' % l] = np.ascontiguousarray(inp['p_w_q'][l])
        m['psk%d' % l] = np.ascontiguousarray(inp['p_subkeys'][l].reshape(16, 128, 128).transpose(2, 0, 1))
        m['puT%d' % l] = np.ascontiguousarray(inp['p_u'][l].T)
        m['pv%d' % l] = np.ascontiguousarray(inp['p_v'][l])
    m['lbl'] = np.ascontiguousarray(inp['a_lb_logits'].reshape(64, 128))
    m['onorm_a'] = np.ascontiguousarray(inp['a_onorm_g'][0].reshape(1, 128))
    m['wuq'] = np.ascontiguousarray(inp['b_w_uq'][0])
    m['wukv'] = np.ascontiguousarray(inp['b_w_ukv'][0])
    pad = lambda v: np.concatenate([v, np.zeros(128 - v.shape[0], np.float32)])
    m['mla_g'] = np.ascontiguousarray(np.stack(
        [inp['b_cq_g'][0][i * 128:(i + 1) * 128] for i in range(6)] + [inp['b_ckv_g'][0][i * 128:(i + 1) * 128] for i in range(4)] +
        [inp['b_qn_g'][0], inp['b_kn_g'][0], pad(inp['b_qr_g'][0]), pad(inp['b_kr_g'][0])]).astype(np.float32))
    if n_layers > 1:
        m['dgains'] = np.ascontiguousarray(np.stack([inp['c_qn_g'][0], inp['c_kn_g'][0], inp['c_onorm_g'][0][0:128],
                                                     inp['c_onorm_g'][0][128:256]]).astype(np.float32))
        m['clam'] = np.ascontiguousarray(inp['c_lam'][0].reshape(1, 512))
        m['convw'] = np.ascontiguousarray(inp['d_conv_w'][0])
        m['convb'] = np.ascontiguousarray(inp['d_conv_b'][0].reshape(1, 2048))
        m['wgate'] = np.ascontiguousarray(inp['d_w_gate'][0])
        m['bgate'] = np.ascontiguousarray(inp['d_b_gate'][0].reshape(64, 128))
        m['dlam'] = np.ascontiguousarray(inp['d_lambda'][0].reshape(32, 128))
    return m


def emit_diffattn(k, PT, Vh, dgains, clam, lam_init, GT):
    P = k.P
    scale = 128.0 ** -0.5
    with ExitStack() as s:
        gi = k.sb(s, 'd_gi', [4, 128], F32)
        gT = k.sb(s, 'd_gT', [128, 4], F32)
        lam_in = k.sb(s, 'd_lamin', [1, 512], F32)
        lam_w = k.sb(s, 'd_lamw', [1, 8], F32)
        ones1 = k.sb(s, 'd_ones1', [1, 128], F32)
        neglam = k.sb(s, 'd_neglam', [128, 1], F32)
        perm = k.sb(s, 'd_perm', [128, 128], F32)
        cosb = k.sb(s, 'd_cosb', [128, 4096], BF16)
        sinb = k.sb(s, 'd_sinb', [128, 4096], BF16)
        with ExitStack() as s1:
            cosT = k.sb(s1, 'd_cos', [128, 4096], F32)
            sinS = k.sb(s1, 'd_sin', [128, 4096], F32)
            with ExitStack() as s2:
                emit_rope_tables(k, s2, 128, cosT, sinS, perm, 'rc_')
            P.barrier()
            k.cp(cosb[:], cosT[:], [], ['d_cosb'])
            k.cp(sinb[:], sinS[:], [], ['d_sinb'])
            P.barrier()
        QK = [k.sb(s, 'd_QK%d' % i, [128, TT], BF16) for i in range(4)]
        Vsb = k.sb(s, 'd_Vsb', [128, 34, 256], BF16)
        kx = [k.sb(s, 'd_kx%d' % i, [128, 512], F32) for i in range(2)]
        kr2 = k.sb(s, 'd_kr2', [128, 512], F32)
        sq = [k.sb(s, 'd_sq%d' % i, [128, 512], BF16) for i in range(2)]
        rs = k.sb(s, 'd_rs', [128, 512], F32)
        pT = [k.sb(s, 'd_pT%d' % i, [128, 512], BF16) for i in range(3)]
        rinv = k.sb(s, 'd_rinv', [128, 512], F32)
        acc = [k.sb(s, 'd_acc%d' % i, [128, 512], F32) for i in range(2)]
        tmp = k.sb(s, 'd_tmp', [128, 512], F32)
        ob = [k.sb(s, 'd_ob%d' % i, [128, 512], BF16) for i in range(2)]
        sP = [k.ps(s, 'd_sP%d' % i, [128, 512], F32) for i in range(3)]
        oP = [k.ps(s, 'd_oP%d' % i, [128, 512], F32) for i in range(2)]
        rP = k.ps(s, 'd_rP', [128, 512], F32)
        pss = k.ps(s, 'd_pss', [128, 512], F32)
        pm = k.ps(s, 'd_pm', [128, 512], F32)
        k.dma(gi[:], dgains, [], ['d_gi'])
        k.tr(pm[:, 0:4], gi[:], k.ident[0:4, 0:4], ['d_gi', 'ident'], ['d_pm'])
        k.cp(gT[:], pm[:, 0:4], ['d_pm'], ['d_gT'])
        k.ts(gT[:, 2:4], gT[:, 2:4], 1.0 - lam_init, None, ALU.mult, None, ['d_gT'], ['d_gT'])
        k.dma(lam_in[:], clam, [], ['d_lamin'])
        k.tt(lam_in[:, 0:128], lam_in[:, 0:128], lam_in[:, 128:256], ALU.mult, ['d_lamin'], ['d_lamin'])
        k.tt(lam_in[:, 256:384], lam_in[:, 256:384], lam_in[:, 384:512], ALU.mult, ['d_lamin'], ['d_lamin'])
        k.P.op('dve', lambda e: e.tensor_reduce(out=lam_w[:, 0:1], in_=lam_in[:, 0:128], axis=AX.X, op=ALU.add), ['d_lamin'], ['d_lamw'])
        k.P.op('dve', lambda e: e.tensor_reduce(out=lam_w[:, 1:2], in_=lam_in[:, 256:384], axis=AX.X, op=ALU.add), ['d_lamin'], ['d_lamw'])
        k.act(lam_w[:, 0:2], lam_w[:, 0:2], AF.Exp, ['d_lamw'], ['d_lamw'])
        k.tt(lam_w[:, 2:3], lam_w[:, 1:2], lam_w[:, 0:1], ALU.subtract, ['d_lamw'], ['d_lamw'])
        k.ts(lam_w[:, 2:3], lam_w[:, 2:3], -lam_init, None, ALU.add, None, ['d_lamw'], ['d_lamw'])
        k.memset(ones1[:], 1.0, ['d_ones1'])
        k.mm(pm[:, 8:9], ones1[:], lam_w[:, 2:3], True, True, ['d_ones1', 'd_lamw', 'd_gT'], ['d_pm'])
        k.cp(neglam[:], pm[:, 8:9], ['d_pm'], ['d_neglam'])
        cnt = {'x': 0, 'o': 0}

        def rms1(src, gcol, n, dst, rows, srcn, dstn):
            nt = src.shape[1]
            k.act(sq[0][0:rows, 0:nt], src, AF.Square, [srcn], ['d_sq0'])
            k.mm(pss[0:rows, 0:nt], k.onesb[0:rows, 0:rows], sq[0][0:rows, 0:nt], True, True, ['onesb', 'd_sq0'], ['d_pss'])
            k.rstd_from_ssq(rs[0:rows, 0:nt], pss[0:rows, 0:nt], n, ['d_pss'], ['d_rs'])
            k.stt(dst, src, gcol, rs[0:rows, 0:nt], ALU.mult, ALU.mult, [srcn, 'd_gT', 'd_rs'], [dstn])

        def rope_fm(x32, xn, t0, nt, dst, dstn):
            if t0 < TC:
                k.cp(dst, x32, [xn], [dstn])
                return
            l0 = t0 - TC
            k.mm(pm[:, 0:nt], perm[:], x32, True, True, ['rc_perm', xn], ['d_pm'])
            k.tt(kr2[:, 0:nt], pm[:, 0:nt], sinb[:, l0:l0 + nt], ALU.mult, ['d_pm', 'd_sinb'], ['d_kr2'])
            k.tt(x32, x32, cosb[:, l0:l0 + nt], ALU.mult, [xn, 'd_cosb'], [xn])
            k.tt(dst, x32, kr2[:, 0:nt], ALU.add, [xn, 'd_kr2'], [dstn])

        for hh in range(8):
            for qi in range(4):
                ch = (0 if qi < 2 else 16) + hh * 2 + (qi % 2)
                gcol = gT[:, 0:1] if qi < 2 else gT[:, 1:2]
                for (t0, nt) in BLK9:
                    x = kx[cnt['x'] % 2]; xn = 'd_kx%d' % (cnt['x'] % 2); cnt['x'] += 1
                    k.dma(x[:, 0:nt], PT[ch * 128:(ch + 1) * 128, t0:t0 + nt], [], [xn])
                    rms1(x[:, 0:nt], gcol, 128, x[:, 0:nt], 128, xn, xn)
                    rope_fm(x[:, 0:nt], xn, t0, nt, QK[qi][:, t0:t0 + nt], 'd_QK%d' % qi)
            k.dma(Vsb[:], Vh[:, hh * 256:(hh + 1) * 256].rearrange("(t p) v -> p t v", p=128), [], ['d_V'])

            def fin(q0, nq, mi, hh=hh):
                k.P.op('dve', lambda e: e.reciprocal(rinv[:, 0:nq], rP[:, 0:nq]), ['a_rP'], ['d_rinv'])
                for dvc in range(2):
                    if mi == 0:
                        k.tt(acc[dvc][:, 0:nq], oP[dvc][:, 0:nq], rinv[:, 0:nq], ALU.mult, ['a_oP%d' % dvc, 'd_rinv'], ['d_acc%d' % dvc])
                    else:
                        k.tt(tmp[:, 0:nq], oP[dvc][:, 0:nq], rinv[:, 0:nq], ALU.mult, ['a_oP%d' % dvc, 'd_rinv'], ['d_tmp'])
                        k.stt(acc[dvc][:, 0:nq], tmp[:, 0:nq], neglam[:, 0:1], acc[dvc][:, 0:nq], ALU.mult, ALU.add,
                              ['d_tmp', 'd_neglam', 'd_acc%d' % dvc], ['d_acc%d' % dvc])
                if mi == 1:
                    for dvc in range(2):
                        k.act(sq[dvc][:, 0:nq], acc[dvc][:, 0:nq], AF.Square, ['d_acc%d' % dvc], ['d_sq%d' % dvc])
                        k.mm(pss[:, 0:nq], k.onesb[:], sq[dvc][:, 0:nq], dvc == 0, dvc == 1, ['onesb', 'd_sq%d' % dvc], ['d_pss'], sig=True)
                    k.rstd_from_ssq(rs[:, 0:nq], pss[:, 0:nq], 256, ['d_pss'], ['d_rs'])
                    for dvc in range(2):
                        o = ob[cnt['o'] % 2]; on = 'd_ob%d' % (cnt['o'] % 2); cnt['o'] += 1
                        k.stt(o[:, 0:nq], acc[dvc][:, 0:nq], gT[:, 2 + dvc:3 + dvc], rs[:, 0:nq], ALU.mult, ALU.mult,
                              ['d_acc%d' % dvc, 'd_gT', 'd_rs'], [on])
                        k.dma(GT[hh * 256 + dvc * 128:hh * 256 + (dvc + 1) * 128, q0:q0 + nq], o[:, 0:nq], [on], [('GTd', hh, dvc, q0)])

            maps = [[(QK[2], QK[0], 128, 'd_QK2', 'd_QK0')], [(QK[3], QK[1], 128, 'd_QK3', 'd_QK1')]]
            emit_attn_core(k, maps, Vsb, 2, scale, 'd_V', sP, oP, rP, pT, fin)
    P.barrier()


def emit_rglru(k, PT, convw, convb, wgate, bgate, dlam, GT):
    P = k.P
    with ExitStack() as s:
        cin = k.sb(s, 'g_cin', [80, 128], F32)
        cT = k.sb(s, 'g_cT', [128, 80], F32)
        bin_ = k.sb(s, 'g_bin', [96, 128], F32)
        bT = k.sb(s, 'g_bT', [128, 96], F32)
        sp8 = k.sb(s, 'g_sp8', [128, 32], F32)
        wg = k.sb(s, 'g_wg', [128, 4, 128], BF16)
        x32 = k.sb(s, 'g_x32', [128, TT], F32)
        xc = k.sb(s, 'g_xc', [128, TT], F32)
        xcb = k.sb(s, 'g_xcb', [128, TT], BF16)
        rg = k.sb(s, 'g_rg', [128, TT], F32)
        ig = k.sb(s, 'g_ig', [128, TT], F32)
        t1 = k.sb(s, 'g_t1', [128, TT], F32)
        hf = k.sb(s, 'g_hf', [128, TT], F32)
        hb = k.sb(s, 'g_hb', [128, TT], F32)
        ob = k.sb(s, 'g_ob', [128, TT], BF16)
        pz = [k.ps(s, 'g_pz%d' % i, [128, 512], F32) for i in range(4)]
        pm = k.ps(s, 'g_pm', [128, 128], F32)
        k.dma(cin[0:64, :], convw.rearrange("j (b c) -> (j b) c", c=128), [], ['g_cin'])
        k.dma(cin[64:80, :], convb.rearrange("o (b c) -> (o b) c", c=128), [], ['g_cin2'])
        k.tr(pm[:, 0:80], cin[:], k.ident[0:80, 0:80], ['g_cin', 'g_cin2', 'ident'], ['g_pm'])
        k.cp(cT[:], pm[:, 0:80], ['g_pm'], ['g_cT'])
        k.dma(bin_[0:64, :], bgate, [], ['g_bin'])
        k.dma(bin_[64:96, :], dlam, [], ['g_bin2'])
        k.tr(pm[:, 0:96], bin_[:], k.ident[0:96, 0:96], ['g_bin', 'g_bin2', 'ident', 'g_cT'], ['g_pm'])
        k.cp(bT[:], pm[:, 0:96], ['g_pm'], ['g_bT'])
        k.act(sp8[:], bT[:, 64:96], AF.Exp, ['g_bT'], ['g_sp8'], scale=-1.0)
        k.act(sp8[:], sp8[:], AF.Ln, ['g_sp8'], ['g_sp8'], bias=1.0)
        k.ts(sp8[:], sp8[:], -8.0, None, ALU.mult, None, ['g_sp8'], ['g_sp8'])
        SEG = [(0, TC), (TC, TT)]
        for bk in range(16):
            k.dma(x32[:], PT[(48 + bk) * 128:(49 + bk) * 128, :], [], ['g_x32'])
            for d in range(2):
                for g in range(2):
                    k.dma(wg[:, d * 2 + g, :], wgate[d, g, bk], [], [('g_wg', d, g)], eng='pool')
            for (a, b) in SEG:
                k.ts(xc[:, a:b], x32[:, a:b], cT[:, 2 * 16 + bk:2 * 16 + bk + 1], cT[:, 64 + bk:64 + bk + 1], ALU.mult, ALU.add,
                     ['g_x32', 'g_cT'], ['g_xc'])
                k.stt(xc[:, a + 2:b], x32[:, a:b - 2], cT[:, 0 * 16 + bk:0 * 16 + bk + 1], xc[:, a + 2:b], ALU.mult, ALU.add,
                      ['g_x32', 'g_cT', 'g_xc'], ['g_xc'])
                k.stt(xc[:, a + 1:b], x32[:, a:b - 1], cT[:, 1 * 16 + bk:1 * 16 + bk + 1], xc[:, a + 1:b], ALU.mult, ALU.add,
                      ['g_x32', 'g_cT', 'g_xc'], ['g_xc'])
                k.stt(xc[:, a:b - 1], x32[:, a + 1:b], cT[:, 3 * 16 + bk:3 * 16 + bk + 1], xc[:, a:b - 1], ALU.mult, ALU.add,
                      ['g_x32', 'g_cT', 'g_xc'], ['g_xc'])
            k.cp(xcb[:], xc[:], ['g_xc'], ['g_xcb'], eng='act')
            for d in range(2):
                for g in range(2):
                    dst = rg if g == 0 else ig
                    dn = 'g_rg' if g == 0 else 'g_ig'
                    bcol = bT[:, (d * 2 + g) * 16 + bk:(d * 2 + g) * 16 + bk + 1]
                    for bi_, (t0, nt) in enumerate(BLK9):
                        pzz = pz[bi_ % 4]; pn = 'g_pz%d' % (bi_ % 4)
                        k.mm(pzz[:, 0:nt], wg[:, d * 2 + g, :], xcb[:, t0:t0 + nt], True, True, [('g_wg', d, g), 'g_xcb'], [pn])
                        k.act(dst[:, t0:t0 + nt], pzz[:, 0:nt], AF.Sigmoid, [pn, 'g_bT'], [dn], bias=bcol)
                k.act(rg[:], rg[:], AF.Exp, ['g_rg', 'g_sp8'], ['g_rg'], scale=sp8[:, d * 16 + bk:d * 16 + bk + 1])
                k.tt(t1[:], rg[:], rg[:], ALU.mult, ['g_rg'], ['g_t1'])
                k.ts(t1[:], t1[:], -1.0, 1.0, ALU.mult, ALU.add, ['g_t1'], ['g_t1'])
                k.act(t1[:], t1[:], AF.Sqrt, ['g_t1'], ['g_t1'])
                k.tt(ig[:], ig[:], xc[:], ALU.mult, ['g_ig', 'g_xc'], ['g_ig'])
                k.tt(ig[:], ig[:], t1[:], ALU.mult, ['g_ig', 'g_t1'], ['g_ig'])
                if d == 0:
                    k.P.op('dve', lambda e: e.tensor_tensor_scan(hf[:], rg[:], ig[:], 0.0, ALU.mult, ALU.add), ['g_rg', 'g_ig'], ['g_hf'])
                else:
                    k.P.op('dve', lambda e: e.tensor_tensor_scan(hb[:, 0:TC][:, ::-1], rg[:, 0:TC][:, ::-1], ig[:, 0:TC][:, ::-1], 0.0, ALU.mult, ALU.add),
                           ['g_rg', 'g_ig'], ['g_hb'])
                    k.P.op('dve', lambda e: e.tensor_tensor_scan(hb[:, TC:TT][:, ::-1], rg[:, TC:TT][:, ::-1], ig[:, TC:TT][:, ::-1], hb[:, 0:1], ALU.mult, ALU.add),
                           ['g_rg', 'g_ig', 'g_hb'], ['g_hb'])
            k.dma(x32[:], PT[(32 + bk) * 128:(33 + bk) * 128, :], ['g_xc'], ['g_x32'])
            k.act(x32[:], x32[:], AF.Gelu, ['g_x32'], ['g_x32'])
            k.tt(hf[:], hf[:], hb[:], ALU.add, ['g_hf', 'g_hb'], ['g_hf'])
            k.tt(ob[:], hf[:], x32[:], ALU.mult, ['g_hf', 'g_x32'], ['g_ob'])
            k.dma(GT[2048 + bk * 128:2048 + (bk + 1) * 128, :], ob[:], ['g_ob'], [('GTr', bk)])
    P.barrier()


_NC_CACHE = {}


def kernel(**inputs):
    inp = {k_: np.asarray(v) for k_, v in inputs.items()}
    B = inp['x'].shape[0]
    if 'nc' not in _NC_CACHE:
        _NC_CACHE['nc'] = build_program(n_layers=2)
    nc = _NC_CACHE['nc']
    in_maps = [prep_inputs(inp, b, n_layers=2) for b in range(B)]
    res = run_bass_kernel_spmd(nc, in_maps, core_ids=list(range(B)))
    out = np.stack([np.asarray(res.results[b]['y'], dtype=np.float32) for b in range(B)], axis=0)
    return out
```
